# Optimizing a Trainium2 kernel written in Bass

```python
import math
import jax, jax.numpy as jnp
from jax import lax
import numpy as np

D_MODEL = 1024
BATCH = 8
SEQ = 4096
DEPTH = 1

N_META = 16
GRID_W = 64
Q_BLOCK = 128
HEAD_DIM = 64
ROPE_THETA = 10000.0
EPS = 1e-6
A_HEADS = 8
A_KV_HEADS = 2
A_GROUP = A_HEADS // A_KV_HEADS
A_WIDTH = A_HEADS * HEAD_DIM
A_KV_WIDTH = A_KV_HEADS * HEAD_DIM
B_HEADS = 4
B_VDIM = 2 * HEAD_DIM
B_WIDTH = B_HEADS * B_VDIM
MIX_WIDTH = A_WIDTH + B_WIDTH
IN_WIDTH = A_WIDTH + 2 * A_KV_WIDTH + 3 * B_WIDTH
D_FF = -(-8 * D_MODEL // (3 * 256)) * 256

kernel_name = "hymba_axial_gqa_diffattn_swiglu"


def rms_norm(x, g):
    xf = x.astype(jnp.float32)
    y = xf * lax.rsqrt(jnp.mean(xf * xf, axis=-1, keepdims=True) + EPS)
    return (y * g.astype(jnp.float32)).astype(x.dtype)


def apply_rope(x, ang):
    xf = x.astype(jnp.float32).reshape(x.shape[:-1] + (x.shape[-1] // 2, 2))
    c = jnp.cos(ang)[None, :, None, :]
    s = jnp.sin(ang)[None, :, None, :]
    x1, x2 = xf[..., 0], xf[..., 1]
    out = jnp.stack([x1 * c - x2 * s, x1 * s + x2 * c], axis=-1)
    return out.reshape(x.shape).astype(x.dtype)


def rope_angles_1d(length):
    pos = jnp.arange(length, dtype=jnp.float32)
    inv = ROPE_THETA ** (-jnp.arange(0, HEAD_DIM, 2, dtype=jnp.float32) / HEAD_DIM)
    return pos[:, None] * inv[None, :]


def rope_angles_axial(n_tokens):
    rows = n_tokens // GRID_W
    r = jnp.repeat(jnp.arange(rows, dtype=jnp.float32), GRID_W)
    c = jnp.tile(jnp.arange(GRID_W, dtype=jnp.float32), rows)
    half = HEAD_DIM // 2
    inv = ROPE_THETA ** (-jnp.arange(0, half, 2, dtype=jnp.float32) / half)
    ang = jnp.concatenate([r[:, None] * inv[None, :], c[:, None] * inv[None, :]], axis=-1)
    meta = jnp.zeros((N_META, HEAD_DIM // 2), jnp.float32)
    return jnp.concatenate([meta, ang], axis=0)


def sweep_queries(fn, q):
    out_meta = fn(q[:, :N_META])
    real = q[:, N_META:]
    bsz, s = real.shape[0], real.shape[1]
    nb = s // Q_BLOCK
    blocks = jnp.moveaxis(real.reshape((bsz, nb, Q_BLOCK) + real.shape[2:]), 1, 0)
    out = lax.map(fn, blocks)
    out = jnp.moveaxis(out, 0, 1).reshape((bsz, s) + out.shape[3:])
    return jnp.concatenate([out_meta, out], axis=1)


def setup_inputs(seed: int = 0) -> dict:
    key = jax.random.key(seed)
    ks = jax.random.split(key, 20)
    f32 = jnp.float32

    def gain(k, shape):
        return 1.0 + 0.1 * jax.random.normal(k, shape, f32)

    return {
        "x": jax.random.normal(ks[0], (BATCH, SEQ, D_MODEL), f32),
        "meta_tokens": jax.random.normal(ks[1], (N_META, D_MODEL), f32),
        "attn_norm_g": gain(ks[2], (DEPTH, D_MODEL)),
        "w_in": jax.random.normal(ks[3], (DEPTH, D_MODEL, IN_WIDTH), f32) * D_MODEL ** -0.5,
        "a_q_norm_g": gain(ks[4], (DEPTH, HEAD_DIM)),
        "a_k_norm_g": gain(ks[5], (DEPTH, HEAD_DIM)),
        "b_q_norm_g": gain(ks[6], (DEPTH, HEAD_DIM)),
        "b_k_norm_g": gain(ks[7], (DEPTH, HEAD_DIM)),
        "b_lambda_q1": 0.1 * jax.random.normal(ks[8], (DEPTH, HEAD_DIM), f32),
        "b_lambda_k1": 0.1 * jax.random.normal(ks[9], (DEPTH, HEAD_DIM), f32),
        "b_lambda_q2": 0.1 * jax.random.normal(ks[10], (DEPTH, HEAD_DIM), f32),
        "b_lambda_k2": 0.1 * jax.random.normal(ks[11], (DEPTH, HEAD_DIM), f32),
        "b_subln_g": gain(ks[12], (DEPTH, B_VDIM)),
        "w_out": jax.random.normal(ks[13], (DEPTH, MIX_WIDTH, D_MODEL), f32) * MIX_WIDTH ** -0.5,
        "ffn_norm_g": gain(ks[14], (DEPTH, D_MODEL)),
        "w_gate": jax.random.normal(ks[15], (DEPTH, D_MODEL, D_FF), f32) * D_MODEL ** -0.5,
        "w_up": jax.random.normal(ks[16], (DEPTH, D_MODEL, D_FF), f32) * D_MODEL ** -0.5,
        "w_down": jax.random.normal(ks[17], (DEPTH, D_FF, D_MODEL), f32) * D_FF ** -0.5,
    }


def reference(x, meta_tokens, attn_norm_g, w_in, a_q_norm_g, a_k_norm_g, b_q_norm_g, b_k_norm_g,
              b_lambda_q1, b_lambda_k1, b_lambda_q2, b_lambda_k2, b_subln_g, w_out,
              ffn_norm_g, w_gate, w_up, w_down):
    bsz, s, _ = x.shape
    meta = jnp.broadcast_to(meta_tokens.astype(x.dtype)[None], (bsz, N_META, D_MODEL))
    x = jnp.concatenate([meta, x], axis=1)
    length = N_META + s

    ang_a = rope_angles_axial(s)
    ang_b = rope_angles_1d(length)
    scale = HEAD_DIM ** -0.5
    splits = np.cumsum([A_WIDTH, A_KV_WIDTH, A_KV_WIDTH, B_WIDTH, B_WIDTH]).tolist()

    for l in range(DEPTH):
        lam_init = 0.8 - 0.6 * math.exp(-0.3 * l)
        h = rms_norm(x, attn_norm_g[l])
        proj = jnp.einsum('bld,de->ble', h, w_in[l])
        qa, ka, va, qb, kb, vb = jnp.split(proj, splits, axis=-1)

        qa = apply_rope(rms_norm(qa.reshape(bsz, length, A_HEADS, HEAD_DIM), a_q_norm_g[l]), ang_a)
        ka = apply_rope(rms_norm(ka.reshape(bsz, length, A_KV_HEADS, HEAD_DIM), a_k_norm_g[l]), ang_a)
        va = va.reshape(bsz, length, A_KV_HEADS, HEAD_DIM)

        def gqa_block(q):
            nq = q.shape[1]
            qg = q.reshape(q.shape[0], nq, A_KV_HEADS, A_GROUP, HEAD_DIM)
            sc = jnp.einsum('bqkgd,blkd->bkgql', qg, ka).astype(jnp.float32) * scale
            p = jax.nn.softmax(sc, axis=-1).astype(va.dtype)
            o = jnp.einsum('bkgql,blkd->bqkgd', p, va)
            return o.reshape(q.shape[0], nq, A_WIDTH)

        out_a = sweep_queries(gqa_block, qa)

        qb = apply_rope(rms_norm(qb.reshape(bsz, length, 2 * B_HEADS, HEAD_DIM), b_q_norm_g[l]), ang_b)
        kb = apply_rope(rms_norm(kb.reshape(bsz, length, 2 * B_HEADS, HEAD_DIM), b_k_norm_g[l]), ang_b)
        qb = qb.reshape(bsz, length, B_HEADS, 2, HEAD_DIM)
        kb = kb.reshape(bsz, length, B_HEADS, 2, HEAD_DIM)
        vb = vb.reshape(bsz, length, B_HEADS, B_VDIM)
        lam = (jnp.exp(jnp.sum(b_lambda_q1[l].astype(jnp.float32) * b_lambda_k1[l].astype(jnp.float32)))
               - jnp.exp(jnp.sum(b_lambda_q2[l].astype(jnp.float32) * b_lambda_k2[l].astype(jnp.float32)))
               + lam_init)
        subln_g = b_subln_g[l]

        def diff_block(q):
            nq = q.shape[1]
            sc = jnp.einsum('bqhjd,blhjd->bhjql', q, kb).astype(jnp.float32) * scale
            p = jax.nn.softmax(sc, axis=-1)
            w = (p[:, :, 0] - lam * p[:, :, 1]).astype(vb.dtype)
            o = jnp.einsum('bhql,blhe->bqhe', w, vb)
            o = rms_norm(o, subln_g) * (1.0 - lam_init)
            return o.reshape(q.shape[0], nq, B_WIDTH)

        out_b = sweep_queries(diff_block, qb)

        mixed = jnp.concatenate([out_a, out_b], axis=-1)
        x = x + jnp.einsum('ble,ed->bld', mixed, w_out[l])

        h = rms_norm(x, ffn_norm_g[l])
        g = jnp.einsum('bld,df->blf', h, w_gate[l])
        u = jnp.einsum('bld,df->blf', h, w_up[l])
        x = x + jnp.einsum('blf,fd->bld', jax.nn.silu(g) * u, w_down[l])

    return x[:, N_META:]
```

```python
import contextlib
import os
import numpy as np
import concourse.bass as bass
import concourse.mybir as mybir
from concourse.bass_utils import run_bass_kernel_spmd

F32 = mybir.dt.float32
BF16 = mybir.dt.bfloat16
AF = mybir.ActivationFunctionType
ALU = mybir.AluOpType
AX = mybir.AxisListType

S = 4096
D = 1024
NT = 33
EPS = 1e-6
DFF = 2816
NF = 22
INW = 2304
VW = 896
LAM_INIT = 0.2
N_CORES = 8


class Buf:
    __slots__ = ("name", "w", "r", "excl")

    def __init__(self, name, excl=False):
        self.name = name
        self.w = None
        self.r = []
        self.excl = excl


class FW:
    def __init__(self, nc, es):
        self.nc = nc
        self.es = es
        self.eng = {'pe': nc.tensor, 'act': nc.scalar, 'dve': nc.vector, 'pool': nc.gpsimd, 'sp': nc.sync}
        self.sems = {}
        self.cnt = {}
        for k in ['pe', 'act', 'dve', 'pool']:
            self.sems[k] = es.enter_context(nc.semaphore('s_' + k))
            self.cnt[k] = 0
        self.seen = {k: {} for k in self.eng}
        self.pe_unmarked = False

    def dma_sem(self, name):
        key = 'dma_' + name
        self.sems[key] = self.es.enter_context(self.nc.semaphore('s_' + key))
        self.cnt[key] = 0
        return key

    def _wait(self, e, tok):
        if tok is None:
            return
        key, val = tok
        if key == 'pe' and e == 'pe':
            return
        if self.seen[e].get(key, 0) >= val:
            return
        self.seen[e][key] = val
        self.eng[e].wait_ge(self.sems[key], val)

    def _eng_of(self, tok):
        return tok[0]

    def deps(self, e, reads=(), writes=()):
        for b in reads:
            self._wait(e, b.w)
            if b.excl:
                for t in b.r:
                    if t[0] != e:
                        self._wait(e, t)
        strict = (e == 'pool')
        for b in writes:
            if b.w is not None and (strict or b.w[0] != e):
                self._wait(e, b.w)
            for t in b.r:
                if strict or t[0] != e:
                    self._wait(e, t)

    def done(self, tok, reads=(), writes=()):
        for b in reads:
            b.r.append(tok)
            if len(b.r) > 8:
                best = {}
                for k, v in b.r:
                    if best.get(k, 0) < v:
                        best[k] = v
                b.r = list(best.items())
        for b in writes:
            b.w = tok
            b.r = []

    def op(self, e, inst_fn, reads=(), writes=(), mark=True):
        self.deps(e, reads, writes)
        inst = inst_fn()
        if mark:
            self.cnt[e] += 1
            inst.then_inc(self.sems[e], 1)
            tok = (e, self.cnt[e])
            if e == 'pe':
                self.pe_unmarked = False
        else:
            assert e == 'pe'
            tok = (e, self.cnt[e] + 1)
            self.pe_unmarked = True
        self.done(tok, reads, writes)
        return tok

    def dma(self, q, semkey, out, in_, reads=(), writes=(), **kw):
        self.deps(q, reads, writes)
        inst = self.eng[q].dma_start(out=out, in_=in_, **kw)
        self.cnt[semkey] += 16
        inst.then_inc(self.sems[semkey], 16)
        tok = (semkey, self.cnt[semkey])
        self.done(tok, reads, writes)
        return tok

    def barrier(self):
        assert not self.pe_unmarked
        for e in self.eng:
            for key, c in self.cnt.items():
                if c > 0 and key != e:
                    self._wait(e, (key, c))


def build(debug=False, stop_after=3):
    nc = bass.Bass("TRN2", target_bir_lowering=False)

    def din(name, shape):
        return nc.dram_tensor(name, shape, F32, kind="ExternalInput").ap()

    x = din("x", [S, D])
    meta = din("meta_tokens", [16, D])
    attn_g = din("attn_norm_g", [1, D])
    w_in = din("w_in", [1, D, INW])
    gvecs = [din(n, [1, 64]) for n in ("a_q_norm_g", "a_k_norm_g", "b_q_norm_g", "b_k_norm_g")]
    lvecs = [din(n, [1, 64]) for n in ("b_lambda_q1", "b_lambda_k1", "b_lambda_q2", "b_lambda_k2")]
    subln = din("b_subln_g", [1, 128])
    w_out = din("w_out", [1, D, D])
    ffn_g = din("ffn_norm_g", [1, D])
    w_gate = din("w_gate", [1, D, DFF])
    w_up = din("w_up", [1, D, DFF])
    w_down = din("w_down", [1, DFF, D])
    rope = din("rope_tab", [S + 16, 128])
    ident_d = din("ident", [128, 128])
    out = nc.dram_tensor("out", [S, D], F32, kind="ExternalOutput").ap()
    skind = "ExternalOutput" if debug else "Internal"
    qT_d = nc.dram_tensor("qT_d", [8, 128, S], BF16, kind=skind).ap()
    mT_d = nc.dram_tensor("mT_d", [8, 128, S], BF16, kind=skind).ap()
    wout_b = nc.dram_tensor("wout_b", [D, D], BF16).ap()
    wg_b = nc.dram_tensor("wg_b", [D, DFF], BF16).ap()
    wu_b = nc.dram_tensor("wu_b", [D, DFF], BF16).ap()
    wd_b = nc.dram_tensor("wd_b", [DFF, D], BF16).ap()
    if debug:
        dbg_kt = nc.dram_tensor("dbg_kt", [128, 6, S + 16], BF16, kind="ExternalOutput").ap()
        dbg_v = nc.dram_tensor("dbg_v", [128, NT, VW], BF16, kind="ExternalOutput").ap()

    es = contextlib.ExitStack()
    with es:
        fw = FW(nc, es)
        V = nc.vector
        A = nc.scalar
        PE = nc.tensor

        _alloc_n = [0]
        _alloc_max = int(os.environ.get("ALLOC_N", "100000"))

        def sb(stack, name, shape, dt):
            _alloc_n[0] += 1
            if _alloc_n[0] > _alloc_max:
                return None
            return stack.enter_context(nc.sbuf_tensor(name, shape, dt))

        ident = sb(es, "ident_bf", [128, 128], BF16)
        ones_bf = sb(es, "ones_bf", [128, 128], BF16)
        ones_f = sb(es, "ones_f", [128, 128], F32)
        epsc = sb(es, "epsc", [128, 1], F32)
        g4 = sb(es, "g4", [128, 4, 64], F32)
        gsn4 = sb(es, "gsn4", [128, 4, 64], F32)
        l4 = sb(es, "l4", [128, 4, 64], F32)
        ltmp = sb(es, "ltmp", [128, 2, 64], F32)
        lsum = sb(es, "lsum", [128, 2], F32)
        lexp = sb(es, "lexp", [128, 2], F32)
        lamc = sb(es, "lamc", [128, 1], F32)
        neglam = sb(es, "neglam", [128, 1], F32)
        gsubc = sb(es, "gsubc", [128, 1], F32)
        g2c = sb(es, "g2c", [128, 8], F32)
        junk = sb(es, "junk", [128, 1024], BF16)
        dbl = [es.enter_context(nc.psum_tensor(f"dbank{i}", [128, 1024], F32)) for i in range(2)]
        sing = [es.enter_context(nc.psum_tensor(f"bank{i}", [128, 512], F32)) for i in range(4, 8)]
        banks = [dbl[0][:, 0:512], dbl[0][:, 512:1024], dbl[1][:, 0:512], dbl[1][:, 512:1024]] + [t[:, :] for t in sing]
        Bbank = [Buf(f"bank{i}", excl=True) for i in range(8)]

        s_const = fw.dma_sem("const")
        s_constp = fw.dma_sem("constp")
        fw.dma('pool', s_constp, ident[:], ident_d)
        for i in range(4):
            fw.dma('sp', s_const, g4[:, i, :], gvecs[i][0].partition_broadcast(128))
            fw.dma('sp', s_const, l4[:, i, :], lvecs[i][0].partition_broadcast(128))
        fw.dma('sp', s_const, gsubc[:], subln[0].rearrange("(p o) -> p o", o=1))
        fw.dma('sp', s_const, g2c[:], ffn_g[0].rearrange("(c p) -> p c", p=128), allow_slow_non_contiguous=True)
        V.memset(ones_bf[:], 1.0)
        V.memset(ones_f[:], 1.0)
        V.memset(epsc[:], EPS).then_inc(fw.sems['dve'], 1)
        fw.cnt['dve'] += 1
        fw.barrier()
        if stop_after < -1:
            return nc
        g4v = g4[:].rearrange("p t (i two) -> p t i two", two=2)
        gsn4v = gsn4[:].rearrange("p t (i two) -> p t i two", two=2)
        Bc = Buf("consts")
        fw.op('dve', lambda: V.tensor_scalar(out=gsn4v[:, :, :, 0], in0=g4v[:, :, :, 1], scalar1=-1.0, scalar2=None,
                                             op0=ALU.mult), writes=[Bc])
        fw.op('dve', lambda: V.tensor_copy(out=gsn4v[:, :, :, 1], in_=g4v[:, :, :, 0]), writes=[Bc])
        fw.op('dve', lambda: V.tensor_tensor(out=ltmp[:, 0, :], in0=l4[:, 0, :], in1=l4[:, 1, :], op=ALU.mult), writes=[Bc])
        fw.op('dve', lambda: V.tensor_tensor(out=ltmp[:, 1, :], in0=l4[:, 2, :], in1=l4[:, 3, :], op=ALU.mult), writes=[Bc])
        fw.op('dve', lambda: V.tensor_reduce(out=lsum[:], in_=ltmp[:], axis=AX.X, op=ALU.add), reads=[Bc], writes=[Bc])
        fw.op('act', lambda: A.activation(out=lexp[:], in_=lsum[:], func=AF.Exp), reads=[Bc], writes=[Bc])
        fw.op('dve', lambda: V.tensor_tensor(out=lamc[:], in0=lexp[:, 0:1], in1=lexp[:, 1:2], op=ALU.subtract),
              reads=[Bc], writes=[Bc])
        fw.op('dve', lambda: V.tensor_scalar(out=neglam[:], in0=lamc[:], scalar1=LAM_INIT, scalar2=-1.0,
                                             op0=ALU.add, op1=ALU.mult), reads=[Bc], writes=[Bc])
        fw.op('dve', lambda: V.tensor_scalar(out=gsubc[:], in0=gsubc[:], scalar1=1.0 - LAM_INIT, scalar2=None,
                                             op0=ALU.mult), reads=[Bc], writes=[Bc])
        fw.barrier()

        if stop_after < 0:
            s_d0 = fw.dma_sem("d0")
            fw.dma('sp', s_d0, out[0:128, 0:256], gsn4[:].rearrange("p t d -> p (t d)"))
            fw.dma('sp', s_d0, out[128:256, 0:1], neglam[:], allow_slow_non_contiguous=True)
            fw.dma('sp', s_d0, out[256:384, 0:8], g2c[:])
            fw.barrier()
            return nc
        s_cw = [fw.dma_sem(f"cw{i}") for i in range(4)]
        Bcw = [Buf(f"cw{i}") for i in range(4)]
        cast_jobs = []
        for wi, (src, dst, rows, mld) in enumerate([(w_out[0], wout_b, D, None), (w_gate[0], wg_b, D, 5632),
                                                   (w_up[0], wu_b, D, 5632), (w_down[0], wd_b, DFF, None)]):
            r0 = 0
            while r0 < rows:
                r1 = min(rows, r0 + 512)
                kw = {} if mld is None else {"max_dma_last_dim": mld}
                cast_jobs.append((wi, dst[r0:r1, :], src[r0:r1, :], kw))
                r0 = r1

        def issue_cast():
            if cast_jobs:
                wi, dst, src, kw = cast_jobs.pop(0)
                fw.dma('pool', s_cw[wi], dst, src, writes=[Bcw[wi]], **kw)

        kv_stack = contextlib.ExitStack()
        with kv_stack:
            KT = sb(kv_stack, "KT", [128, 6, S + 16], BF16)
            Vs = sb(kv_stack, "Vs", [128, NT, VW], BF16)
            if os.environ.get("DUMP_EARLY") == "2":
                s_d1 = fw.dma_sem("d1")
                fw.dma('sp', s_d1, out[0:128, 0:256], gsn4[:].rearrange("p t d -> p (t d)"))
                fw.barrier()
                return nc

            p1 = contextlib.ExitStack()
            if stop_after < 0.1:
                return nc
            with p1:
                win = sb(p1, "win", [128, 8, INW], BF16)
                gA = sb(p1, "gA", [128, D], F32)
                xs = [sb(p1, f"xs{i}", [128, D], F32) for i in range(2)]
                rt = [sb(p1, f"rt{i}", [128, 128], F32) for i in range(3)]
                ssx = [sb(p1, f"ssx{i}", [128, 4], F32) for i in range(3)]
                hb = [sb(p1, f"hb{i}", [128, D], BF16) for i in range(2)]
                hT = [sb(p1, f"hT{i}", [128, 8, 128], BF16) for i in range(2)]
                ssh = [sb(p1, f"ssh{i}", [128, 3, 26], F32) for i in range(2)]
                tabs = [sb(p1, f"tabs{i}", [128, 4, 2, 64], F32) for i in range(2)]
                tmpB = [sb(p1, f"tmpB{i}", [128, 512], F32) for i in range(1)]
                qkb = [sb(p1, f"qkb{i}", [128, 14 * 128], BF16) for i in range(2)]
                qst = [sb(p1, f"qst{i}", [128, 8, 512], BF16) for i in range(2)]
                sqs = sb(p1, "sqs", [128, 26 * 64], F32)
                Bsqs = Buf("sqs")
                Bxs = [Buf("xs") for _ in range(3)]
                Brt = [Buf("rt") for _ in range(3)]
                Bssx = [Buf("ssx") for _ in range(3)]
                Bhb = [Buf("hb") for _ in range(2)]
                BhT = [Buf("hT") for _ in range(2)]
                Bssh = [Buf("ssh") for _ in range(2)]
                Btabs = [Buf("tabs") for _ in range(2)]
                BtmpB = [Buf("tmpB") for _ in range(1)]
                Bqkb = [Buf("qkb") for _ in range(2)]
                Bqst = [Buf("qst") for _ in range(2)]
                Bwin = [Buf(f"win{g}") for g in range(5)]
                sx = [fw.dma_sem(f"x{i}") for i in range(3)]
                srt = [fw.dma_sem(f"rt{i}") for i in range(3)]
                swin = [fw.dma_sem(f"win{g}") for g in range(5)]
                sqst = [fw.dma_sem(f"qst{i}") for i in range(2)]
                s_ga = fw.dma_sem("ga")
                BgA = Buf("gA")

                GC = [(0, 512), (512, 256), (768, 512), (1280, 512), (1792, 512)]
                w_in_v = w_in[0].rearrange("(c p) e -> p c e", p=128)
                if os.environ.get("DUMP_EARLY") == "4":
                    s_d1 = fw.dma_sem("d1")
                    fw.dma('sp', s_d1, out[0:128, 0:256], gsn4[:].rearrange("p t d -> p (t d)"))
                    fw.barrier()
                    return nc
                fw.dma('sp', s_ga, gA[:], attn_g[0].partition_broadcast(128), writes=[BgA])
                if os.environ.get("DUMP_EARLY"):
                    s_d1 = fw.dma_sem("d1")
                    if os.environ.get("DUMP_EARLY") == "3":
                        fw.dma('sp', s_d1, out[0:128, 0:256], gsn4[:].rearrange("p t d -> p (t d)"))
                    else:
                        fw.dma('sp', s_d1, out[0:128, :], gA[:], reads=[BgA])
                    fw.barrier()
                    return nc
                for g, (c0, cw) in enumerate(GC):
                    if os.environ.get("SKIP_WIN"):
                        break
                    fw.dma('pool', swin[g], win[:, :, c0:c0 + cw], w_in_v[:, :, c0:c0 + cw], writes=[Bwin[g]])
                Bvs = Buf("Vs")
                if not os.environ.get("SKIP_VMEM"):
                    fw.op('pool', lambda: nc.gpsimd.memset(Vs[:], 0.0), writes=[Bvs])
                    fw.op('pool', lambda: nc.gpsimd.memset(Vs[:, :, 64:65], 1.0), writes=[Bvs])
                    fw.op('pool', lambda: nc.gpsimd.memset(Vs[:, :, 256:257], 1.0), writes=[Bvs])

                pT, BpT = banks[0], Bbank[0]
                pT_bf = pT[:].bitcast(BF16)
                pj = banks[1:6]
                Bpj = Bbank[1:6]
                tq = [banks[6][:].bitcast(BF16), banks[7][:].bitcast(BF16)]
                Btq = Bbank[6:8]

                TYPES = [
                    ("QA", 0, 0, 8, 0, 0, 0, 0),
                    ("KA", 1, 0, 2, 0, 1, 8, 512),
                    ("QB", 2, 0, 8, 1, 2, 10, 768),
                    ("KB", 3, 0, 8, 1, 3, 18, 1280),
                ]
                QBLK = [0, 1, 2, 3, 6, 7, 8, 9]
                KBLK = [4, 5, 10, 11, 12, 13]

                def npart(t):
                    return 128 if t < 32 else 16

                def stageA(t):
                    n = npart(t)
                    s3, s2 = t % 3, t % 2
                    src = x[t * 128:(t + 1) * 128, :] if t < 32 else meta
                    fw.dma('sp', sx[t % 2], xs[t % 2][0:n, :], src, writes=[Bxs[t % 2]])
                    fw.dma('sp', srt[s3], rt[s3][0:n, :], rope[t * 128:t * 128 + n, :], writes=[Brt[s3]])
                    fw.op('act', lambda: A.activation(out=junk[0:n, :], in_=xs[t % 2][0:n, :], func=AF.Square,
                                                      accum_out=ssx[t % 2][0:n, 0:1]),
                          reads=[Bxs[t % 2]], writes=[Bssx[t % 2]])
                    fw.op('act', lambda: A.activation(out=ssx[t % 2][0:n, 1:2], in_=ssx[t % 2][0:n, 0:1], func=AF.Ln,
                                                      scale=1.0 / D, bias=epsc[0:n, :]),
                          reads=[Bssx[t % 2]], writes=[Bssx[t % 2]])
                    fw.op('act', lambda: A.activation(out=ssx[t % 2][0:n, 2:3], in_=ssx[t % 2][0:n, 1:2], func=AF.Exp,
                                                      scale=-0.5),
                          reads=[Bssx[t % 2]], writes=[Bssx[t % 2]])
                    fw.op('pool', lambda: nc.gpsimd.tensor_scalar(out=xs[t % 2][0:n, :], in0=xs[t % 2][0:n, :],
                                                                  scalar1=ssx[t % 2][0:n, 2:3], scalar2=1.0,
                                                                  op0=ALU.mult, op1=ALU.mult),
                          reads=[Bssx[t % 2]], writes=[Bxs[t % 2]])
                    fw.op('pool', lambda: nc.gpsimd.tensor_tensor(out=hb[s2][0:n, :], in0=xs[t % 2][0:n, :],
                                                                  in1=gA[0:n, :], op=ALU.mult),
                          reads=[Bxs[t % 2], BgA], writes=[Bhb[s2]])

                def stageApe(t):
                    n = npart(t)
                    s2 = t % 2
                    for c in range(8):
                        fw.op('pe', lambda: PE.transpose(out=pT_bf[:, c * 128:c * 128 + n],
                                                         in_=hb[s2][0:n, c * 128:(c + 1) * 128],
                                                         identity=ident[0:n, 0:n]),
                              reads=[Bhb[s2]], writes=[BpT], mark=(c == 7))

                def stageA2(t):
                    n = npart(t)
                    s2 = t % 2
                    pv = pT_bf.rearrange("p (c k) -> p c k", k=128)
                    fw.op('act', lambda: A.activation(out=hT[s2][:, :, 0:n], in_=pv[:, :, 0:n], func=AF.Copy),
                          reads=[BpT], writes=[BhT[s2]])

                def stageB(t):
                    n = npart(t)
                    s3, s2 = t % 3, t % 2
                    for g, (c0, cw) in enumerate(GC):
                        for c in range(8):
                            fw.op('pe', lambda: PE.matmul(pj[g][0:n, 0:cw], lhsT=hT[s2][:, c, 0:n],
                                                          rhs=win[:, c, c0:c0 + cw], start=(c == 0), stop=(c == 7)),
                                  reads=[BhT[s2], Bwin[g]], writes=[Bpj[g]], mark=(c == 7))
                    STB = int(os.environ.get("STB", "99"))
                    if STB < 2:
                        return
                    for (_, bi, c0, H, _, _, i0, _) in TYPES:
                        fw.op('act', lambda: A.activation(out=sqs[0:n, i0 * 64:(i0 + H) * 64],
                                                          in_=pj[bi][0:n, c0:c0 + 64 * H], func=AF.Square),
                              reads=[Bpj[bi]], writes=[Bsqs])
                    fw.op('dve', lambda: V.tensor_reduce(out=ssh[s2][0:n, 0, :],
                                                         in_=sqs[0:n, :].rearrange("p (h d) -> p h d", d=64),
                                                         axis=AX.X, op=ALU.add),
                          reads=[Bsqs], writes=[Bssh[s2]])
                    fw.op('act', lambda: A.activation(out=ssh[s2][0:n, 1, :], in_=ssh[s2][0:n, 0, :], func=AF.Ln,
                                                      scale=1.0 / 64, bias=epsc[0:n, :]),
                          reads=[Bssh[s2]], writes=[Bssh[s2]])
                    fw.op('act', lambda: A.activation(out=ssh[s2][0:n, 2, :], in_=ssh[s2][0:n, 1, :], func=AF.Exp,
                                                      scale=-0.5),
                          reads=[Bssh[s2]], writes=[Bssh[s2]])
                    if STB < 3:
                        return
                    va = Vs[0:n, t, 0:384].rearrange("p (k w) -> p k w", w=192)
                    vsrc = pj[1][0:n, 128:256].rearrange("p (k w) -> p k w", w=64)
                    fw.op('act', lambda: A.activation(out=va[:, :, 0:64], in_=vsrc, func=AF.Copy),
                          reads=[Bpj[1]], writes=[Bvs])
                    fw.op('act', lambda: A.activation(out=va[:, :, 128:192], in_=vsrc, func=AF.Copy),
                          reads=[Bpj[1]], writes=[Bvs])
                    fw.op('act', lambda: A.activation(out=Vs[0:n, t, 384:896], in_=pj[4][0:n, :], func=AF.Copy),
                          reads=[Bpj[4]], writes=[Bvs])
                    if STB < 4:
                        return
                    for ty in range(4):
                        tb = 0 if ty < 2 else 1
                        cosv = rt[s3][0:n, tb * 64:tb * 64 + 32].unsqueeze(2).broadcast_to([n, 32, 2])
                        sinv = rt[s3][0:n, tb * 64 + 32:tb * 64 + 64].unsqueeze(2).broadcast_to([n, 32, 2])
                        t1 = tabs[s2][0:n, ty, 0, :].rearrange("p (i two) -> p i two", two=2)
                        t2 = tabs[s2][0:n, ty, 1, :].rearrange("p (i two) -> p i two", two=2)
                        fw.op('pool', lambda: nc.gpsimd.tensor_tensor(out=t1, in0=cosv, in1=g4v[0:n, ty], op=ALU.mult),
                              reads=[Brt[s3]], writes=[Btabs[s2]])
                        fw.op('pool', lambda: nc.gpsimd.tensor_tensor(out=t2, in0=sinv, in1=gsn4v[0:n, ty], op=ALU.mult),
                              reads=[Brt[s3]], writes=[Btabs[s2]])
                    if STB < 5:
                        return
                    for k, (name, bi, c0, H, tb, ty, i0, o0) in enumerate(TYPES):
                        if k > STB - 5:
                            break
                        W = 64 * H
                        q = pj[bi][0:n, c0:c0 + W]
                        q3 = q.rearrange("p (h d) -> p h d", d=64)
                        q4 = q.rearrange("p (h i two) -> p h i two", i=32, two=2)
                        T1 = tabs[s2][0:n, ty, 0, :]
                        T2 = tabs[s2][0:n, ty, 1, :].rearrange("p (i two) -> p i two", two=2)
                        tB = tmpB[0][0:n, 0:W]
                        tB4 = tB.rearrange("p (h i two) -> p h i two", i=32, two=2)
                        tB3 = tB.rearrange("p (h d) -> p h d", d=64)
                        STB2 = int(os.environ.get("STB2", "99"))
                        if STB2 < 1:
                            continue
                        fw.op('dve', lambda: V.tensor_tensor(out=tB4[:, :, :, 0], in0=q4[:, :, :, 1],
                                                             in1=T2[:, :, 0].unsqueeze(1).broadcast_to([n, H, 32]),
                                                             op=ALU.mult),
                              reads=[Bpj[bi], Btabs[s2]] + ([Bssh[s2]] if os.environ.get("WAITACT") else []), writes=[BtmpB[0]])
                        if STB2 < 2:
                            continue
                        OP2V = os.environ.get("OP2V", "")
                        if OP2V == "a":
                            fw.op('dve', lambda: V.tensor_tensor(out=tB4[:, :, :, 0], in0=q4[:, :, :, 1],
                                                                 in1=T2[:, :, 0].unsqueeze(1).broadcast_to([n, H, 32]),
                                                                 op=ALU.mult),
                                  reads=[Bpj[bi], Btabs[s2]], writes=[BtmpB[0]])
                        elif OP2V == "b":
                            fw.op('dve', lambda: V.tensor_tensor(out=tB4[:, :, :, 1], in0=q4[:, :, :, 1],
                                                                 in1=T2[:, :, 0].unsqueeze(1).broadcast_to([n, H, 32]),
                                                                 op=ALU.mult),
                                  reads=[Bpj[bi], Btabs[s2]], writes=[BtmpB[0]])
                        elif OP2V == "c":
                            fw.op('dve', lambda: V.tensor_tensor(out=tB4[:, :, :, 0], in0=q4[:, :, :, 0],
                                                                 in1=T2[:, :, 0].unsqueeze(1).broadcast_to([n, H, 32]),
                                                                 op=ALU.mult),
                                  reads=[Bpj[bi], Btabs[s2]], writes=[BtmpB[0]])
                        else:
                            fw.op('dve', lambda: V.tensor_tensor(out=tB4[:, :, :, 1], in0=q4[:, :, :, 0],
                                                             in1=T2[:, :, 1].unsqueeze(1).broadcast_to([n, H, 32]),
                                                             op=ALU.mult),
                              reads=[Bpj[bi], Btabs[s2]], writes=[BtmpB[0]])
                        if STB2 < 3:
                            continue
                        fw.op('dve', lambda: V.tensor_tensor(out=q3, in0=q3,
                                                             in1=T1.unsqueeze(1).broadcast_to([n, H, 64]),
                                                             op=ALU.mult),
                              reads=[Btabs[s2], Bpj[bi]], writes=[Bpj[bi]])
                        if STB2 < 4:
                            continue
                        fw.op('dve', lambda: V.tensor_tensor(out=q3, in0=q3, in1=tB3, op=ALU.add),
                              reads=[BtmpB[0], Bpj[bi]], writes=[Bpj[bi]])
                        if STB2 < 5:
                            continue
                        rs = ssh[s2][0:n, 2, i0:i0 + H]
                        if name != "KA":
                            o3 = qkb[s2][0:n, o0:o0 + W].rearrange("p (h d) -> p h d", d=64)
                            fw.op('dve', lambda: V.tensor_tensor(out=o3, in0=q3,
                                                                 in1=rs.unsqueeze(2).broadcast_to([n, H, 64]),
                                                                 op=ALU.mult),
                                  reads=[Bpj[bi], Bssh[s2]], writes=[Bqkb[s2]])
                        else:
                            o4 = qkb[s2][0:n, o0:o0 + 256].rearrange("p (h r d) -> p h r d", r=2, d=64)
                            fw.op('dve', lambda: V.tensor_tensor(
                                out=o4, in0=q3.unsqueeze(2).broadcast_to([n, 2, 2, 64]),
                                in1=rs.unsqueeze(2).unsqueeze(3).broadcast_to([n, 2, 2, 64]), op=ALU.mult),
                                reads=[Bpj[bi], Bssh[s2]], writes=[Bqkb[s2]])

                def stageC(t):
                    n = npart(t)
                    s2 = t % 2
                    real = t < 32
                    if real:
                        for j, blk in enumerate(QBLK):
                            fw.op('pe', lambda: PE.transpose(out=tq[0][:, j * 128:j * 128 + n],
                                                             in_=qkb[s2][0:n, blk * 128:(blk + 1) * 128],
                                                             identity=ident[0:n, 0:n]),
                                  reads=[Bqkb[s2]], writes=[Btq[0]], mark=(j == 7))
                    for j, blk in enumerate(KBLK):
                        fw.op('pe', lambda: PE.transpose(out=tq[1][:, j * 128:j * 128 + n],
                                                         in_=qkb[s2][0:n, blk * 128:(blk + 1) * 128],
                                                         identity=ident[0:n, 0:n]),
                              reads=[Bqkb[s2]], writes=[Btq[1]], mark=(j == 5))
                    if real:
                        qs = (t // 4) % 2
                        tt = t % 4
                        fw.op('act', lambda: A.activation(out=qst[qs][:, :, tt * 128:(tt + 1) * 128],
                                                          in_=tq[0].rearrange("p (c k) -> p c k", k=128),
                                                          func=AF.Copy),
                              reads=[Btq[0]], writes=[Bqst[qs]])
                        if tt == 3:
                            cq = t // 4
                            fw.dma('sp', sqst[qs], qT_d[:, :, cq * 512:(cq + 1) * 512].rearrange("c p t -> p c t"),
                                   qst[qs][:], reads=[Bqst[qs]])
                    kv = tq[1][:, 0:768].rearrange("p (c k) -> p c k", k=128)
                    fw.op('act', lambda: A.activation(out=KT[:, :, t * 128:t * 128 + n], in_=kv[:, :, 0:n], func=AF.Copy),
                          reads=[Btq[1]])

                NTR = int(os.environ.get("NT_RUN", str(NT)))
                for it in range(NT + 2):
                    if stop_after < 0.3:
                        break
                    if it >= NTR + 2:
                        break
                    if it == 0:
                        stageA(0)
                    if it + 1 < min(NT, NTR):
                        stageA(it + 1)
                    if 0 <= it - 1 < min(NT, NTR) and stop_after >= 0.6:
                        stageB(it - 1)
                    if it < min(NT, NTR):
                        stageApe(it)
                        stageA2(it)
                    if 0 <= it - 2 < min(NT, NTR) and stop_after >= 0.8:
                        stageC(it - 2)
                fw.barrier()
                if debug:
                    s_dbg = fw.dma_sem("dbg")
                    fw.dma('sp', s_dbg, dbg_kt, KT[:])
                    fw.dma('sp', s_dbg, dbg_v, Vs[:])
                    fw.dma('sp', s_dbg, out[0:128, :], gA[:])
                    fw.barrier()

            p2 = contextlib.ExitStack()
            if stop_after < 2:
                return nc
            with p2:
                qc = [sb(p2, f"qc{i}", [128, 8, 512], BF16) for i in range(2)]
                pt = [sb(p2, f"pt{i}", [128, 1024], BF16) for i in range(3)]
                mixT = [sb(p2, f"mixT{i}", [128, 8, 512], BF16) for i in range(2)]
                osb = [sb(p2, f"osb{i}", [128, 512], F32) for i in range(2)]
                accP = [sb(p2, f"accP{i}", [128, 512], F32) for i in range(2)]
                tpair = [sb(p2, f"tpair{i}", [128, 1024], BF16) for i in range(2)]
                Btpair = [Buf("tpair") for _ in range(2)]
                zsb = sb(p2, "zsb", [128, 512], F32)
                rzt = sb(p2, "rzt", [128, 512], F32)
                dsb = sb(p2, "dsb", [128, 512], F32)
                sqf = sb(p2, "sqf", [128, 512], F32)
                lnb = sb(p2, "lnb", [128, 512], F32)
                rsb = sb(p2, "rsb", [128, 512], F32)
                Bqc = [Buf("qc") for _ in range(2)]
                Bpt = [Buf("pt") for _ in range(3)]
                BmixT = [Buf("mixT") for _ in range(2)]
                Bosb = [Buf("osb") for _ in range(2)]
                BaccP = [Buf("accP") for _ in range(2)]
                Bzsb, Brzt, Bdsb, Bsqf, Blnb, Brsb = (Buf(n) for n in ("zsb", "rzt", "dsb", "sqf", "lnb", "rsb"))
                sqc = [fw.dma_sem(f"qc{i}") for i in range(2)]
                smix = [fw.dma_sem(f"mix{i}") for i in range(2)]
                SS = [dbl[0], dbl[1]]
                BSS = [Buf("SS0", excl=True), Buf("SS1", excl=True)]
                O0, O1, ZB, AUX = banks[4], banks[5], banks[6], banks[7]
                BO0, BO1, BZB, BAUX = Bbank[4], Bbank[5], Bbank[6], Bbank[7]

                steps = [(c, p, k) for c in range(8) for p in range(8) for k in range(NT)]
                nsteps = len(steps)
                pending = {}

                def load_q(c):
                    sl = c % 2
                    fw.dma('sp', sqc[sl], qc[sl][:], qT_d[:, :, c * 512:(c + 1) * 512].rearrange("c p t -> p c t"),
                           writes=[Bqc[sl]])

                def emit_qk(i):
                    c, p, k = steps[i]
                    sl = c % 2
                    kb = (p // 2) if p < 4 else (p - 2)
                    nk = 128 if k < 32 else 16
                    col = k * 128
                    sb_ = i % 2
                    fw.op('pe', lambda: PE.matmul(SS[sb_][0:nk, 0:512], lhsT=KT[0:64, kb, col:col + nk],
                                                  rhs=qc[sl][0:64, p, :], start=True, stop=True),
                          reads=[Bqc[sl]], writes=[BSS[sb_]], mark=False)
                    fw.op('pe', lambda: PE.matmul(SS[sb_][0:nk, 512:1024], lhsT=KT[64:128, kb, col:col + nk],
                                                  rhs=qc[sl][64:128, p, :], start=True, stop=True),
                          reads=[Bqc[sl]], writes=[BSS[sb_]], mark=True)

                def emit_exp(i):
                    c, p, k = steps[i]
                    nk = 128 if k < 32 else 16
                    sb_ = i % 2
                    ps = i % 3
                    fw.op('act', lambda: A.activation(out=pt[ps][0:nk, :], in_=SS[sb_][0:nk, :], func=AF.Exp,
                                                      scale=0.125),
                          reads=[BSS[sb_]], writes=[Bpt[ps]])

                bpair_idx = [0]

                def emit_pv(i):
                    c, p, k = steps[i]
                    nk = 128 if k < 32 else 16
                    ps = i % 3
                    pt0 = pt[ps][0:nk, 0:512]
                    pt1 = pt[ps][0:nk, 512:1024]
                    st, sp_ = (k == 0), (k == NT - 1)
                    if p < 4:
                        base = (p // 2) * 192
                        fw.op('pe', lambda: PE.matmul(O0[0:65, :], lhsT=Vs[0:nk, k, base:base + 65],
                                                      rhs=pt0, start=st, stop=sp_),
                              reads=[Bpt[ps]], writes=[BO0], mark=False)
                        fw.op('pe', lambda: PE.matmul(O1[:, :], lhsT=Vs[0:nk, k, base + 64:base + 192],
                                                      rhs=pt1, start=st, stop=sp_),
                              reads=[Bpt[ps]], writes=[BO1], mark=True)
                    else:
                        base = 384 + (p - 4) * 128
                        ab = bpair_idx[0] % 2
                        fw.op('pe', lambda: PE.matmul(O0[:, :], lhsT=Vs[0:nk, k, base:base + 128],
                                                      rhs=pt0, start=st, stop=sp_),
                              reads=[Bpt[ps]], writes=[BO0], mark=False)
                        fw.op('pe', lambda: PE.matmul(O1[:, :], lhsT=Vs[0:nk, k, base:base + 128],
                                                      rhs=pt1, start=st, stop=sp_),
                              reads=[Bpt[ps]], writes=[BO1], mark=True)
                        if k < 32 and k % 2 == 1:
                            pp = (i - 1) % 3
                            j = (k // 2) % 2
                            fw.op('dve', lambda: V.tensor_tensor(out=tpair[j][:, :], in0=pt[pp][:, :], in1=pt[ps][:, :],
                                                                 op=ALU.add),
                                  reads=[Bpt[pp], Bpt[ps]], writes=[Btpair[j]])
                            if k == 1:
                                fw.op('pool', lambda: nc.gpsimd.tensor_copy(out=accP[ab][:, :], in_=tpair[j][:, 0:512]),
                                      reads=[Btpair[j]], writes=[BaccP[ab]])
                                fw.op('dve', lambda: V.tensor_copy(out=ZB[:, :], in_=tpair[j][:, 512:1024]),
                                      reads=[Btpair[j]], writes=[BZB])
                            else:
                                fw.op('pool', lambda: nc.gpsimd.tensor_tensor(out=accP[ab][:, :], in0=accP[ab][:, :],
                                                                              in1=tpair[j][:, 0:512], op=ALU.add),
                                      reads=[Btpair[j]], writes=[BaccP[ab]])
                                fw.op('dve', lambda: V.tensor_tensor(out=ZB[:, :], in0=ZB[:, :],
                                                                     in1=tpair[j][:, 512:1024], op=ALU.add),
                                      reads=[Btpair[j]], writes=[BZB])
                        elif k == 32:
                            fw.op('pool', lambda: nc.gpsimd.tensor_tensor(out=accP[ab][0:nk, :], in0=accP[ab][0:nk, :],
                                                                          in1=pt0, op=ALU.add),
                                  reads=[Bpt[ps]], writes=[BaccP[ab]])
                            fw.op('dve', lambda: V.tensor_tensor(out=ZB[0:nk, :], in0=ZB[0:nk, :], in1=pt1,
                                                                 op=ALU.add),
                                  reads=[Bpt[ps]], writes=[BZB])
                        if sp_:
                            bpair_idx[0] += 1

                def finish_chunk(c):
                    sl = c % 2
                    fw.dma('sp', smix[sl], mT_d[:, :, c * 512:(c + 1) * 512].rearrange("c p t -> p c t"),
                           mixT[sl][:], reads=[BmixT[sl]])

                def post_A(c, p):
                    sl = c % 2
                    L = []
                    L.append((0, lambda: fw.op('dve', lambda: V.tensor_copy(out=osb[0][0:65, :], in_=O0[0:65, :]),
                                               reads=[BO0], writes=[Bosb[0]])))
                    L.append((0, lambda: fw.op('dve', lambda: V.tensor_copy(out=osb[1][:, :], in_=O1[:, :]),
                                               reads=[BO1], writes=[Bosb[1]])))
                    if p == 3:
                        L.append((2, lambda: fw.op('act', lambda: A.activation(out=lnb[64:65, :], in_=osb[0][64:65, :],
                                                                               func=AF.Ln),
                                                   reads=[Bosb[0]], writes=[Blnb])))
                        L.append((2, lambda: fw.op('act', lambda: A.activation(out=rzt[64:65, :], in_=lnb[64:65, :],
                                                                               func=AF.Exp, scale=-1.0),
                                                   reads=[Blnb], writes=[Brzt])))
                        L.append((3, lambda: fw.op('act', lambda: A.activation(out=lnb[0:1, :], in_=osb[1][0:1, :],
                                                                               func=AF.Ln),
                                                   reads=[Bosb[1]], writes=[Blnb])))
                        L.append((3, lambda: fw.op('act', lambda: A.activation(out=rzt[0:1, :], in_=lnb[0:1, :],
                                                                               func=AF.Exp, scale=-1.0),
                                                   reads=[Blnb], writes=[Brzt])))
                    else:
                        L.append((1, lambda: fw.op('dve', lambda: V.reciprocal(out=rzt[64:65, :],
                                                                               in_=osb[0][64:65, :]),
                                                   reads=[Bosb[0]], writes=[Brzt])))
                        L.append((3, lambda: fw.op('dve', lambda: V.reciprocal(out=rzt[0:1, :],
                                                                               in_=osb[1][0:1, :]),
                                                   reads=[Bosb[1]], writes=[Brzt])))
                    L.append((8, lambda: fw.op('pe', lambda: PE.matmul(AUX[:, :], lhsT=ones_f[64:65, :],
                                                                        rhs=rzt[64:65, :], start=True, stop=True),
                                                reads=[Brzt], writes=[BAUX])))
                    L.append((11, lambda: fw.op('dve', lambda: V.tensor_tensor(out=mixT[sl][0:64, p, :],
                                                                               in0=osb[0][0:64, :], in1=AUX[0:64, :],
                                                                               op=ALU.mult),
                                                reads=[Bosb[0], BAUX], writes=[BmixT[sl]])))
                    L.append((14, lambda: fw.op('pe', lambda: PE.matmul(AUX[:, :], lhsT=ones_f[0:1, :],
                                                                        rhs=rzt[0:1, :], start=True, stop=True),
                                                reads=[Brzt], writes=[BAUX])))
                    L.append((17, lambda: fw.op('dve', lambda: V.tensor_tensor(out=mixT[sl][64:128, p, :],
                                                                               in0=osb[1][64:128, :],
                                                                               in1=AUX[64:128, :], op=ALU.mult),
                                                reads=[Bosb[1], BAUX], writes=[BmixT[sl]])))
                    return L

                def post_B(c, p):
                    sl = c % 2
                    ab = (bpair_idx[0] - 1) % 2
                    L = []
                    L.append((0, lambda: fw.op('dve', lambda: V.tensor_copy(out=osb[0][:, :], in_=O0[:, :]),
                                               reads=[BO0], writes=[Bosb[0]])))
                    L.append((0, lambda: fw.op('dve', lambda: V.tensor_copy(out=osb[1][:, :], in_=O1[:, :]),
                                               reads=[BO1], writes=[Bosb[1]])))
                    L.append((0, lambda: fw.op('dve', lambda: V.tensor_copy(out=zsb[:, :], in_=ZB[:, :]),
                                               reads=[BZB], writes=[Bzsb])))
                    L.append((2, lambda: fw.op('pe', lambda: PE.matmul(AUX[:, :], lhsT=ones_f[:, :],
                                                                       rhs=accP[ab][:, :], start=True, stop=True),
                                               reads=[BaccP[ab]], writes=[BAUX])))
                    L.append((5, lambda: fw.op('act', lambda: A.activation(out=lnb[:, :], in_=AUX[:, :], func=AF.Ln),
                                               reads=[BAUX], writes=[Blnb])))
                    L.append((5, lambda: fw.op('act', lambda: A.activation(out=rzt[:, :], in_=lnb[:, :], func=AF.Exp,
                                                                           scale=-1.0),
                                               reads=[Blnb], writes=[Brzt])))
                    L.append((8, lambda: fw.op('dve', lambda: V.tensor_tensor(out=osb[0][:, :], in0=osb[0][:, :],
                                                                              in1=rzt[:, :], op=ALU.mult),
                                               reads=[Brzt], writes=[Bosb[0]])))
                    L.append((8, lambda: fw.op('pe', lambda: PE.matmul(AUX[:, :], lhsT=ones_f[:, :], rhs=zsb[:, :],
                                                                       start=True, stop=True),
                                               reads=[Bzsb], writes=[BAUX])))
                    L.append((11, lambda: fw.op('act', lambda: A.activation(out=lnb[:, :], in_=AUX[:, :], func=AF.Ln),
                                               reads=[BAUX], writes=[Blnb])))
                    L.append((11, lambda: fw.op('act', lambda: A.activation(out=rzt[:, :], in_=lnb[:, :], func=AF.Exp,
                                                                           scale=-1.0),
                                               reads=[Blnb], writes=[Brzt])))
                    L.append((14, lambda: fw.op('dve', lambda: V.tensor_tensor(out=osb[1][:, :], in0=osb[1][:, :],
                                                                              in1=rzt[:, :], op=ALU.mult),
                                               reads=[Brzt], writes=[Bosb[1]])))
                    L.append((14, lambda: fw.op('dve', lambda: V.scalar_tensor_tensor(
                        out=dsb[:, :], in0=osb[1][:, :], scalar=neglam[:, 0:1], in1=osb[0][:, :],
                        op0=ALU.mult, op1=ALU.add), reads=[Bosb[0], Bosb[1]], writes=[Bdsb])))
                    L.append((17, lambda: fw.op('act', lambda: A.activation(out=sqf[:, :], in_=dsb[:, :],
                                                                           func=AF.Square),
                                               reads=[Bdsb], writes=[Bsqf])))
                    L.append((20, lambda: fw.op('pe', lambda: PE.matmul(AUX[:, :], lhsT=ones_f[:, :], rhs=sqf[:, :],
                                                                        start=True, stop=True),
                                                reads=[Bsqf], writes=[BAUX])))
                    L.append((23, lambda: fw.op('act', lambda: A.activation(out=lnb[:, :], in_=AUX[:, :], func=AF.Ln,
                                                                            scale=1.0 / 128, bias=epsc[:, :]),
                                                reads=[BAUX], writes=[Blnb])))
                    L.append((23, lambda: fw.op('act', lambda: A.activation(out=rsb[:, :], in_=lnb[:, :],
                                                                            func=AF.Exp, scale=-0.5),
                                                reads=[Blnb], writes=[Brsb])))

                    def fin():
                        fw.op('dve', lambda: V.scalar_tensor_tensor(out=mixT[sl][:, p, :], in0=dsb[:, :],
                                                                    scalar=gsubc[:, 0:1], in1=rsb[:, :],
                                                                    op0=ALU.mult, op1=ALU.mult),
                              reads=[Bdsb, Brsb], writes=[BmixT[sl]])
                        if p == 7:
                            finish_chunk(c)
                    L.append((26, fin))
                    return L

                load_q(0)
                emit_qk(0)
                emit_qk(1)
                for i in range(nsteps):
                    c, p, k = steps[i]
                    if p == 0 and k == 0 and c + 1 < 8:
                        load_q(c + 1)
                    emit_exp(i)
                    if i + 2 < nsteps:
                        emit_qk(i + 2)
                    emit_pv(i)
                    if p < 4 and k == 4:
                        issue_cast()
                    if k == NT - 1:
                        for d, fn in (post_A(c, p) if p < 4 else post_B(c, p)):
                            pending.setdefault(i + d, []).append(fn)
                    for fn in pending.pop(i, []):
                        fn()
                for key in sorted(pending):
                    for fn in pending[key]:
                        fn()
                while cast_jobs:
                    issue_cast()
                fw.barrier()

        p3 = contextlib.ExitStack()
        if stop_after < 3:
            return nc
        with p3:
            wout = sb(p3, "wout", [128, 8, D], BF16)
            wg = sb(p3, "wg", [128, 8, DFF], BF16)
            wu = sb(p3, "wu", [128, 8, DFF], BF16)
            wd = sb(p3, "wd", [128, NF, D], BF16)
            x1 = [sb(p3, f"x1_{i}", [128, 2, D], F32) for i in range(2)]
            mc = [sb(p3, f"mc{i}", [128, 8, 256], BF16) for i in range(2)]
            h2 = [sb(p3, f"h2_{i}", [128, D], BF16) for i in range(2)]
            h2T = [sb(p3, f"h2T{i}", [128, 8, 256], BF16) for i in range(2)]
            aT = [sb(p3, f"aT{i}", [128, 256], BF16) for i in range(4)]
            sg = [sb(p3, f"sg{i}", [128, 256], F32) for i in range(2)]
            ss2 = [sb(p3, f"ss2_{i}", [128, 4], F32) for i in range(2)]
            Bwout, Bwg, Bwu, Bwd = Buf("wout"), Buf("wg"), Buf("wu"), Buf("wd")
            Bx1 = [Buf("x1") for _ in range(2)]
            Bmc = [Buf("mc") for _ in range(2)]
            Bh2 = [Buf("h2") for _ in range(2)]
            Bh2T = [Buf("h2T") for _ in range(2)]
            BaT = [Buf("aT") for _ in range(4)]
            Bsg = [Buf("sg") for _ in range(2)]
            Bss2 = [Buf("ss2") for _ in range(2)]
            s_wout, s_wg, s_wu, s_wd = (fw.dma_sem(n) for n in ("wout", "wg", "wu", "wd"))
            sx1 = [fw.dma_sem(f"x1_{i}") for i in range(2)]
            smc = [fw.dma_sem(f"mc{i}") for i in range(2)]
            sout = [fw.dma_sem(f"out{i}") for i in range(2)]

            wout_v = wout_b.rearrange("(c p) e -> p c e", p=128)
            wg_v = wg_b.rearrange("(c p) e -> p c e", p=128)
            wu_v = wu_b.rearrange("(c p) e -> p c e", p=128)
            wd_v = wd_b.rearrange("(c p) e -> p c e", p=128)
            fw.dma('sp', s_wout, wout[:, 0:4, :], wout_v[:, 0:4, :], reads=[Bcw[0]], writes=[Bwout])
            fw.dma('act', s_wout, wout[:, 4:8, :], wout_v[:, 4:8, :], reads=[Bcw[0]], writes=[Bwout])
            for c in range(0, 8, 2):
                fw.dma('sp', s_wg, wg[:, c:c + 2, :], wg_v[:, c:c + 2, :], reads=[Bcw[1]], writes=[Bwg])
                fw.dma('act', s_wu, wu[:, c:c + 2, :], wu_v[:, c:c + 2, :], reads=[Bcw[2]], writes=[Bwu])
            for f in range(0, NF, 4):
                f1 = min(NF, f + 4)
                q = 'sp' if (f // 4) % 2 == 0 else 'act'
                fw.dma(q, s_wd, wd[:, f:f1, :], wd_v[:, f:f1, :], reads=[Bcw[3]], writes=[Bwd])

            YA, BYA = banks[0:4], Bbank[0:4]
            GU, BGU = banks[4:6], Bbank[4:6]
            OPJ, BOPJ = banks[6], Bbank[6]
            TP, BTP = banks[7][:].bitcast(BF16), Bbank[7]
            NCH = S // 256

            def load3(c):
                sl = c % 2
                fw.dma('sp', smc[sl], mc[sl][:], mT_d[:, :, c * 256:(c + 1) * 256].rearrange("c p t -> p c t"),
                       writes=[Bmc[sl]])
                fw.dma('sp', sx1[sl], x1[sl][:], x[c * 256:(c + 1) * 256, :].rearrange("(tt p) d -> p tt d", p=128),
                       writes=[Bx1[sl]])

            def pre1(c):
                sl = c % 2
                for tt in range(2):
                    for hf in range(2):
                        for ec in range(8):
                            fw.op('pe', lambda: PE.matmul(OPJ[:, :], lhsT=mc[sl][:, ec, tt * 128:(tt + 1) * 128],
                                                          rhs=wout[:, ec, hf * 512:(hf + 1) * 512],
                                                          start=(ec == 0), stop=(ec == 7)),
                                  reads=[Bmc[sl], Bwout], writes=[BOPJ], mark=(ec == 7))
                        xv = x1[sl][:, tt, hf * 512:(hf + 1) * 512]
                        fw.op('dve', lambda: V.tensor_tensor(out=xv, in0=xv, in1=OPJ[:, :], op=ALU.add),
                              reads=[BOPJ], writes=[Bx1[sl]])
                    fw.op('act', lambda: A.activation(out=junk[:, :], in_=x1[sl][:, tt, :], func=AF.Square,
                                                      accum_out=ss2[tt][:, 0:1]),
                          reads=[Bx1[sl]], writes=[Bss2[tt]])
                    fw.op('act', lambda: A.activation(out=ss2[tt][:, 1:2], in_=ss2[tt][:, 0:1], func=AF.Ln,
                                                      scale=1.0 / D, bias=epsc[:, :]),
                          reads=[Bss2[tt]], writes=[Bss2[tt]])
                    fw.op('act', lambda: A.activation(out=ss2[tt][:, 2:3], in_=ss2[tt][:, 1:2], func=AF.Exp,
                                                      scale=-0.5),
                          reads=[Bss2[tt]], writes=[Bss2[tt]])
                    fw.op('dve', lambda: V.tensor_scalar(out=h2[tt][:, :], in0=x1[sl][:, tt, :],
                                                         scalar1=ss2[tt][:, 2:3], scalar2=None, op0=ALU.mult),
                          reads=[Bx1[sl], Bss2[tt]], writes=[Bh2[tt]])

            def pre2(c):
                sl = c % 2
                for tt in range(2):
                    for dc in range(8):
                        fw.op('pe', lambda: PE.transpose(out=TP[:, dc * 128:(dc + 1) * 128],
                                                         in_=h2[tt][:, dc * 128:(dc + 1) * 128],
                                                         identity=ident[:, :]),
                              reads=[Bh2[tt]], writes=[BTP], mark=(dc == 7))
                    fw.op('dve', lambda: V.tensor_tensor(out=h2T[sl][:, :, tt * 128:(tt + 1) * 128],
                                                         in0=TP.rearrange("p (c k) -> p c k", k=128),
                                                         in1=g2c[:, :].unsqueeze(2).broadcast_to([128, 8, 128]),
                                                         op=ALU.mult),
                          reads=[BTP], writes=[Bh2T[sl]])

            def emit_gu(c, f):
                sl = c % 2
                gb = f % 2
                for dc in range(8):
                    fw.op('pe', lambda: PE.matmul(GU[gb][:, 0:256], lhsT=wg[:, dc, f * 128:(f + 1) * 128],
                                                  rhs=h2T[sl][:, dc, :], start=(dc == 0), stop=(dc == 7)),
                          reads=[Bh2T[sl], Bwg], writes=[BGU[gb]], mark=False)
                for dc in range(8):
                    fw.op('pe', lambda: PE.matmul(GU[gb][:, 256:512], lhsT=wu[:, dc, f * 128:(f + 1) * 128],
                                                  rhs=h2T[sl][:, dc, :], start=(dc == 0), stop=(dc == 7),
                                                  skip_group_check=True),
                          reads=[Bh2T[sl], Bwu], writes=[BGU[gb]], mark=(dc == 7))

            def emit_act(c, f):
                gb = f % 2
                fw.op('act', lambda: A.activation(out=sg[gb][:, :], in_=GU[gb][:, 0:256], func=AF.Silu),
                      reads=[BGU[gb]], writes=[Bsg[gb]])
                fw.op('dve', lambda: V.tensor_tensor(out=aT[f % 4][:, :], in0=sg[gb][:, :], in1=GU[gb][:, 256:512],
                                                     op=ALU.mult),
                      reads=[Bsg[gb], BGU[gb]], writes=[BaT[f % 4]])

            def emit_down(c, f):
                for tt in range(2):
                    for hf in range(2):
                        bi = tt * 2 + hf
                        fw.op('pe', lambda: PE.matmul(YA[bi][:, :], lhsT=aT[f % 4][:, tt * 128:(tt + 1) * 128],
                                                      rhs=wd[:, f, hf * 512:(hf + 1) * 512],
                                                      start=(f == 0), stop=(f == NF - 1)),
                              reads=[BaT[f % 4], Bwd], writes=[BYA[bi]], mark=(bi == 3))

            def fin3(c):
                sl = c % 2
                for tt in range(2):
                    for hf in range(2):
                        bi = tt * 2 + hf
                        xv = x1[sl][:, tt, hf * 512:(hf + 1) * 512]
                        fw.op('dve', lambda: V.tensor_tensor(out=xv, in0=xv, in1=YA[bi][:, :], op=ALU.add),
                              reads=[BYA[bi]], writes=[Bx1[sl]])
                fw.dma('sp', sout[sl], out[c * 256:(c + 1) * 256, :].rearrange("(tt p) d -> p tt d", p=128),
                       x1[sl][:], reads=[Bx1[sl]])

            load3(0)
            pre1(0)
            pre2(0)
            for c in range(NCH):
                if c + 1 < NCH:
                    load3(c + 1)
                emit_gu(c, 0)
                for f in range(NF):
                    if f + 1 < NF:
                        emit_gu(c, f + 1)
                    emit_act(c, f)
                    emit_down(c, f)
                    if c + 1 < NCH and f == 4:
                        pre1(c + 1)
                    if c + 1 < NCH and f == 12:
                        pre2(c + 1)
                fin3(c)
            fw.barrier()
    return nc


def rope_table():
    half = 32
    inv_a = np.power(np.float32(10000.0), -np.arange(0, half, 2, dtype=np.float32) / np.float32(half)).astype(np.float32)
    s = np.arange(S)
    r = (s // 64).astype(np.float32)
    c = (s % 64).astype(np.float32)
    ang_a = np.concatenate([r[:, None] * inv_a[None, :], c[:, None] * inv_a[None, :]], axis=-1).astype(np.float32)
    ang_a = np.concatenate([ang_a, np.zeros((16, 32), np.float32)], axis=0)
    inv_b = np.power(np.float32(10000.0), -np.arange(0, 64, 2, dtype=np.float32) / np.float32(64)).astype(np.float32)
    pos = np.concatenate([np.arange(16, S + 16), np.arange(0, 16)]).astype(np.float32)
    ang_b = (pos[:, None] * inv_b[None, :]).astype(np.float32)
    tab = np.concatenate([np.cos(ang_a), np.sin(ang_a), np.cos(ang_b), np.sin(ang_b)], axis=-1)
    return np.ascontiguousarray(tab.astype(np.float32))


_NC_CACHE = {}


def make_in_maps(inputs, cores):
    f = lambda a: np.ascontiguousarray(np.asarray(a, dtype=np.float32))
    shared = {k: f(v) for k, v in inputs.items() if k != "x"}
    shared["rope_tab"] = rope_table()
    shared["ident"] = np.eye(128, dtype=np.float32)
    xx = np.asarray(inputs["x"], dtype=np.float32)
    maps = []
    for b in cores:
        m = dict(shared)
        m["x"] = np.ascontiguousarray(xx[b])
        maps.append(m)
    return maps


def kernel(**inputs):
    if "nc" not in _NC_CACHE:
        _NC_CACHE["nc"] = build(False)
    nc = _NC_CACHE["nc"]
    in_maps = make_in_maps(inputs, list(range(N_CORES)))
    res = run_bass_kernel_spmd(nc, in_maps, core_ids=list(range(N_CORES)))
    return np.stack([np.asarray(r["out"], dtype=np.float32) for r in res.results], axis=0)
```

```python
import contextlib
import os
import numpy as np
import concourse.bass as bass
import concourse.mybir as mybir
from concourse.bass_utils import run_bass_kernel_spmd

F32 = mybir.dt.float32
BF16 = mybir.dt.bfloat16
AF = mybir.ActivationFunctionType
ALU = mybir.AluOpType
AX = mybir.AxisListType

S = 4096
D = 1024
NT = 33
EPS = 1e-6
DFF = 2816
NF = 22
INW = 2304
VW = 896
LAM_INIT = 0.2
N_CORES = 8


class Buf:
    __slots__ = ("name", "w", "r", "excl")

    def __init__(self, name, excl=False):
        self.name = name
        self.w = None
        self.r = []
        self.excl = excl


class FW:
    def __init__(self, nc, es):
        self.nc = nc
        self.es = es
        self.eng = {'pe': nc.tensor, 'act': nc.scalar, 'dve': nc.vector, 'pool': nc.gpsimd, 'sp': nc.sync}
        self.sems = {}
        self.cnt = {}
        for k in ['pe', 'act', 'dve', 'pool']:
            self.sems[k] = es.enter_context(nc.semaphore('s_' + k))
            self.cnt[k] = 0
        self.seen = {k: {} for k in self.eng}
        self.pe_unmarked = False

    def dma_sem(self, name):
        key = 'dma_' + name
        self.sems[key] = self.es.enter_context(self.nc.semaphore('s_' + key))
        self.cnt[key] = 0
        return key

    def _wait(self, e, tok):
        if tok is None:
            return
        key, val = tok
        if key == 'pe' and e == 'pe':
            return
        if self.seen[e].get(key, 0) >= val:
            return
        self.seen[e][key] = val
        self.eng[e].wait_ge(self.sems[key], val)

    def _eng_of(self, tok):
        return tok[0]

    def deps(self, e, reads=(), writes=()):
        for b in reads:
            self._wait(e, b.w)
            if b.excl:
                for t in b.r:
                    if t[0] != e:
                        self._wait(e, t)
        strict = (e == 'pool')
        for b in writes:
            if b.w is not None and (strict or b.w[0] != e):
                self._wait(e, b.w)
            for t in b.r:
                if strict or t[0] != e:
                    self._wait(e, t)

    def done(self, tok, reads=(), writes=()):
        for b in reads:
            b.r.append(tok)
            if len(b.r) > 8:
                best = {}
                for k, v in b.r:
                    if best.get(k, 0) < v:
                        best[k] = v
                b.r = list(best.items())
        for b in writes:
            b.w = tok
            b.r = []

    def op(self, e, inst_fn, reads=(), writes=(), mark=True):
        self.deps(e, reads, writes)
        inst = inst_fn()
        if mark:
            self.cnt[e] += 1
            inst.then_inc(self.sems[e], 1)
            tok = (e, self.cnt[e])
            if e == 'pe':
                self.pe_unmarked = False
        else:
            assert e == 'pe'
            tok = (e, self.cnt[e] + 1)
            self.pe_unmarked = True
        self.done(tok, reads, writes)
        return tok

    def dma(self, q, semkey, out, in_, reads=(), writes=(), **kw):
        self.deps(q, reads, writes)
        inst = self.eng[q].dma_start(out=out, in_=in_, **kw)
        self.cnt[semkey] += 16
        inst.then_inc(self.sems[semkey], 16)
        tok = (semkey, self.cnt[semkey])
        self.done(tok, reads, writes)
        return tok

    def barrier(self):
        assert not self.pe_unmarked
        for e in self.eng:
            for key, c in self.cnt.items():
                if c > 0 and key != e:
                    self._wait(e, (key, c))


def build(debug=False, stop_after=3):
    nc = bass.Bass("TRN2", target_bir_lowering=False)

    def din(name, shape):
        return nc.dram_tensor(name, shape, F32, kind="ExternalInput").ap()

    x = din("x", [S, D])
    meta = din("meta_tokens", [16, D])
    attn_g = din("attn_norm_g", [1, D])
    w_in = din("w_in", [1, D, INW])
    gvecs = [din(n, [1, 64]) for n in ("a_q_norm_g", "a_k_norm_g", "b_q_norm_g", "b_k_norm_g")]
    lvecs = [din(n, [1, 64]) for n in ("b_lambda_q1", "b_lambda_k1", "b_lambda_q2", "b_lambda_k2")]
    subln = din("b_subln_g", [1, 128])
    w_out = din("w_out", [1, D, D])
    ffn_g = din("ffn_norm_g", [1, D])
    w_gate = din("w_gate", [1, D, DFF])
    w_up = din("w_up", [1, D, DFF])
    w_down = din("w_down", [1, DFF, D])
    rope = din("rope_tab", [S + 16, 128])
    ident_d = din("ident", [128, 128])
    out = nc.dram_tensor("out", [S, D], F32, kind="ExternalOutput").ap()
    skind = "ExternalOutput" if debug else "Internal"
    qT_d = nc.dram_tensor("qT_d", [8, 128, S], BF16, kind=skind).ap()
    mT_d = nc.dram_tensor("mT_d", [8, 128, S], BF16, kind=skind).ap()
    wout_b = nc.dram_tensor("wout_b", [D, D], BF16).ap()
    wg_b = nc.dram_tensor("wg_b", [D, DFF], BF16).ap()
    wu_b = nc.dram_tensor("wu_b", [D, DFF], BF16).ap()
    wd_b = nc.dram_tensor("wd_b", [DFF, D], BF16).ap()
    if debug:
        dbg_kt = nc.dram_tensor("dbg_kt", [128, 6, S + 16], BF16, kind="ExternalOutput").ap()
        dbg_v = nc.dram_tensor("dbg_v", [128, NT, VW], BF16, kind="ExternalOutput").ap()

    es = contextlib.ExitStack()
    with es:
        fw = FW(nc, es)
        V = nc.vector
        A = nc.scalar
        PE = nc.tensor

        _alloc_n = [0]
        _alloc_max = int(os.environ.get("ALLOC_N", "100000"))

        def sb(stack, name, shape, dt):
            _alloc_n[0] += 1
            if _alloc_n[0] > _alloc_max:
                return None
            return stack.enter_context(nc.sbuf_tensor(name, shape, dt))

        ident = sb(es, "ident_bf", [128, 128], BF16)
        ones_bf = sb(es, "ones_bf", [128, 128], BF16)
        ones_f = sb(es, "ones_f", [128, 128], F32)
        epsc = sb(es, "epsc", [128, 1], F32)
        g4 = sb(es, "g4", [128, 4, 64], F32)
        gsn4 = sb(es, "gsn4", [128, 4, 64], F32)
        l4 = sb(es, "l4", [128, 4, 64], F32)
        ltmp = sb(es, "ltmp", [128, 2, 64], F32)
        lsum = sb(es, "lsum", [128, 2], F32)
        lexp = sb(es, "lexp", [128, 2], F32)
        lamc = sb(es, "lamc", [128, 1], F32)
        neglam = sb(es, "neglam", [128, 1], F32)
        gsubc = sb(es, "gsubc", [128, 1], F32)
        g2c = sb(es, "g2c", [128, 8], F32)
        junk = sb(es, "junk", [128, 1024], BF16)
        dbl = [es.enter_context(nc.psum_tensor(f"dbank{i}", [128, 1024], F32)) for i in range(2)]
        sing = [es.enter_context(nc.psum_tensor(f"bank{i}", [128, 512], F32)) for i in range(4, 8)]
        banks = [dbl[0][:, 0:512], dbl[0][:, 512:1024], dbl[1][:, 0:512], dbl[1][:, 512:1024]] + [t[:, :] for t in sing]
        Bbank = [Buf(f"bank{i}", excl=True) for i in range(8)]

        s_const = fw.dma_sem("const")
        s_constp = fw.dma_sem("constp")
        fw.dma('pool', s_constp, ident[:], ident_d)
        for i in range(4):
            fw.dma('sp', s_const, g4[:, i, :], gvecs[i][0].partition_broadcast(128))
            fw.dma('sp', s_const, l4[:, i, :], lvecs[i][0].partition_broadcast(128))
        fw.dma('sp', s_const, gsubc[:], subln[0].rearrange("(p o) -> p o", o=1))
        fw.dma('sp', s_const, g2c[:], ffn_g[0].rearrange("(c p) -> p c", p=128), allow_slow_non_contiguous=True)
        V.memset(ones_bf[:], 1.0)
        V.memset(ones_f[:], 1.0)
        V.memset(epsc[:], EPS).then_inc(fw.sems['dve'], 1)
        fw.cnt['dve'] += 1
        fw.barrier()
        if stop_after < -1:
            return nc
        g4v = g4[:].rearrange("p t (i two) -> p t i two", two=2)
        gsn4v = gsn4[:].rearrange("p t (i two) -> p t i two", two=2)
        Bc = Buf("consts")
        fw.op('dve', lambda: V.tensor_scalar(out=gsn4v[:, :, :, 0], in0=g4v[:, :, :, 1], scalar1=-1.0, scalar2=None,
                                             op0=ALU.mult), writes=[Bc])
        fw.op('dve', lambda: V.tensor_copy(out=gsn4v[:, :, :, 1], in_=g4v[:, :, :, 0]), writes=[Bc])
        fw.op('dve', lambda: V.tensor_tensor(out=ltmp[:, 0, :], in0=l4[:, 0, :], in1=l4[:, 1, :], op=ALU.mult), writes=[Bc])
        fw.op('dve', lambda: V.tensor_tensor(out=ltmp[:, 1, :], in0=l4[:, 2, :], in1=l4[:, 3, :], op=ALU.mult), writes=[Bc])
        fw.op('dve', lambda: V.tensor_reduce(out=lsum[:], in_=ltmp[:], axis=AX.X, op=ALU.add), reads=[Bc], writes=[Bc])
        fw.op('act', lambda: A.activation(out=lexp[:], in_=lsum[:], func=AF.Exp), reads=[Bc], writes=[Bc])
        fw.op('dve', lambda: V.tensor_tensor(out=lamc[:], in0=lexp[:, 0:1], in1=lexp[:, 1:2], op=ALU.subtract),
              reads=[Bc], writes=[Bc])
        fw.op('dve', lambda: V.tensor_scalar(out=neglam[:], in0=lamc[:], scalar1=LAM_INIT, scalar2=-1.0,
                                             op0=ALU.add, op1=ALU.mult), reads=[Bc], writes=[Bc])
        fw.op('dve', lambda: V.tensor_scalar(out=gsubc[:], in0=gsubc[:], scalar1=1.0 - LAM_INIT, scalar2=None,
                                             op0=ALU.mult), reads=[Bc], writes=[Bc])
        fw.barrier()

        if stop_after < 0:
            s_d0 = fw.dma_sem("d0")
            fw.dma('sp', s_d0, out[0:128, 0:256], gsn4[:].rearrange("p t d -> p (t d)"))
            fw.dma('sp', s_d0, out[128:256, 0:1], neglam[:], allow_slow_non_contiguous=True)
            fw.dma('sp', s_d0, out[256:384, 0:8], g2c[:])
            fw.barrier()
            return nc
        s_cw = [fw.dma_sem(f"cw{i}") for i in range(4)]
        Bcw = [Buf(f"cw{i}") for i in range(4)]
        cast_jobs = []
        for wi, (src, dst, rows, mld) in enumerate([(w_out[0], wout_b, D, None), (w_gate[0], wg_b, D, 5632),
                                                   (w_up[0], wu_b, D, 5632), (w_down[0], wd_b, DFF, None)]):
            r0 = 0
            while r0 < rows:
                r1 = min(rows, r0 + 512)
                kw = {} if mld is None else {"max_dma_last_dim": mld}
                cast_jobs.append((wi, dst[r0:r1, :], src[r0:r1, :], kw))
                r0 = r1

        def issue_cast():
            if cast_jobs:
                wi, dst, src, kw = cast_jobs.pop(0)
                fw.dma('pool', s_cw[wi], dst, src, writes=[Bcw[wi]], **kw)

        kv_stack = contextlib.ExitStack()
        with kv_stack:
            KT = sb(kv_stack, "KT", [128, 6, S + 16], BF16)
            Vs = sb(kv_stack, "Vs", [128, NT, VW], BF16)
            if os.environ.get("DUMP_EARLY") == "2":
                s_d1 = fw.dma_sem("d1")
                fw.dma('sp', s_d1, out[0:128, 0:256], gsn4[:].rearrange("p t d -> p (t d)"))
                fw.barrier()
                return nc

            p1 = contextlib.ExitStack()
            if stop_after < 0.1:
                return nc
            with p1:
                win = sb(p1, "win", [128, 8, INW], BF16)
                gA = sb(p1, "gA", [128, D], F32)
                xs = [sb(p1, f"xs{i}", [128, D], F32) for i in range(2)]
                rt = [sb(p1, f"rt{i}", [128, 128], F32) for i in range(3)]
                ssx = [sb(p1, f"ssx{i}", [128, 4], F32) for i in range(3)]
                hb = [sb(p1, f"hb{i}", [128, D], BF16) for i in range(2)]
                hT = [sb(p1, f"hT{i}", [128, 8, 128], BF16) for i in range(2)]
                ssh = [sb(p1, f"ssh{i}", [128, 3, 26], F32) for i in range(2)]
                tabs = [sb(p1, f"tabs{i}", [128, 4, 2, 64], F32) for i in range(2)]
                tmpB = [sb(p1, f"tmpB{i}", [128, 512], F32) for i in range(1)]
                qkb = [sb(p1, f"qkb{i}", [128, 14 * 128], BF16) for i in range(2)]
                qst = [sb(p1, f"qst{i}", [128, 8, 512], BF16) for i in range(2)]
                sqs = sb(p1, "sqs", [128, 26 * 64], F32)
                Bsqs = Buf("sqs")
                Bxs = [Buf("xs") for _ in range(3)]
                Brt = [Buf("rt") for _ in range(3)]
                Bssx = [Buf("ssx") for _ in range(3)]
                Bhb = [Buf("hb") for _ in range(2)]
                BhT = [Buf("hT") for _ in range(2)]
                Bssh = [Buf("ssh") for _ in range(2)]
                Btabs = [[[Buf("tabs") for _ in range(2)] for _ in range(4)] for _ in range(2)]
                BtmpB = [Buf("tmpB") for _ in range(1)]
                Bqkb = [Buf("qkb") for _ in range(2)]
                Bqst = [Buf("qst") for _ in range(2)]
                Bwin = [Buf(f"win{g}") for g in range(5)]
                sx = [fw.dma_sem(f"x{i}") for i in range(3)]
                srt = [fw.dma_sem(f"rt{i}") for i in range(3)]
                swin = [fw.dma_sem(f"win{g}") for g in range(5)]
                sqst = [fw.dma_sem(f"qst{i}") for i in range(2)]
                s_ga = fw.dma_sem("ga")
                BgA = Buf("gA")

                GC = [(0, 512), (512, 256), (768, 512), (1280, 512), (1792, 512)]
                w_in_v = w_in[0].rearrange("(c p) e -> p c e", p=128)
                if os.environ.get("DUMP_EARLY") == "4":
                    s_d1 = fw.dma_sem("d1")
                    fw.dma('sp', s_d1, out[0:128, 0:256], gsn4[:].rearrange("p t d -> p (t d)"))
                    fw.barrier()
                    return nc
                fw.dma('sp', s_ga, gA[:], attn_g[0].partition_broadcast(128), writes=[BgA])
                if os.environ.get("DUMP_EARLY"):
                    s_d1 = fw.dma_sem("d1")
                    if os.environ.get("DUMP_EARLY") == "3":
                        fw.dma('sp', s_d1, out[0:128, 0:256], gsn4[:].rearrange("p t d -> p (t d)"))
                    else:
                        fw.dma('sp', s_d1, out[0:128, :], gA[:], reads=[BgA])
                    fw.barrier()
                    return nc
                for g, (c0, cw) in enumerate(GC):
                    if os.environ.get("SKIP_WIN"):
                        break
                    fw.dma('pool', swin[g], win[:, :, c0:c0 + cw], w_in_v[:, :, c0:c0 + cw], writes=[Bwin[g]])
                Bvs = Buf("Vs")
                if not os.environ.get("SKIP_VMEM"):
                    fw.op('pool', lambda: nc.gpsimd.memset(Vs[:], 0.0), writes=[Bvs])
                    fw.op('pool', lambda: nc.gpsimd.memset(Vs[:, :, 64:65], 1.0), writes=[Bvs])
                    fw.op('pool', lambda: nc.gpsimd.memset(Vs[:, :, 256:257], 1.0), writes=[Bvs])

                pT, BpT = banks[0], Bbank[0]
                pT_bf = pT[:].bitcast(BF16)
                pj = banks[1:6]
                Bpj = Bbank[1:6]
                tq = [banks[6][:].bitcast(BF16), banks[7][:].bitcast(BF16)]
                Btq = Bbank[6:8]

                TYPES = [
                    ("QA", 0, 0, 8, 0, 0, 0, 0),
                    ("KA", 1, 0, 2, 0, 1, 8, 512),
                    ("QB", 2, 0, 8, 1, 2, 10, 768),
                    ("KB", 3, 0, 8, 1, 3, 18, 1280),
                ]
                QBLK = [0, 1, 2, 3, 6, 7, 8, 9]
                KBLK = [4, 5, 10, 11, 12, 13]

                def npart(t):
                    return 128 if t < 32 else 16

                def stageA(t):
                    n = npart(t)
                    s3, s2 = t % 3, t % 2
                    src = x[t * 128:(t + 1) * 128, :] if t < 32 else meta
                    fw.dma('sp', sx[t % 2], xs[t % 2][0:n, :], src, writes=[Bxs[t % 2]])
                    fw.dma('sp', srt[s3], rt[s3][0:n, :], rope[t * 128:t * 128 + n, :], writes=[Brt[s3]])
                    fw.op('act', lambda: A.activation(out=junk[0:n, :], in_=xs[t % 2][0:n, :], func=AF.Square,
                                                      accum_out=ssx[t % 2][0:n, 0:1]),
                          reads=[Bxs[t % 2]], writes=[Bssx[t % 2]])
                    fw.op('act', lambda: A.activation(out=ssx[t % 2][0:n, 1:2], in_=ssx[t % 2][0:n, 0:1], func=AF.Ln,
                                                      scale=1.0 / D, bias=epsc[0:n, :]),
                          reads=[Bssx[t % 2]], writes=[Bssx[t % 2]])
                    fw.op('act', lambda: A.activation(out=ssx[t % 2][0:n, 2:3], in_=ssx[t % 2][0:n, 1:2], func=AF.Exp,
                                                      scale=-0.5),
                          reads=[Bssx[t % 2]], writes=[Bssx[t % 2]])
                    fw.op('pool', lambda: nc.gpsimd.tensor_scalar(out=xs[t % 2][0:n, :], in0=xs[t % 2][0:n, :],
                                                                  scalar1=ssx[t % 2][0:n, 2:3], scalar2=1.0,
                                                                  op0=ALU.mult, op1=ALU.mult),
                          reads=[Bssx[t % 2]], writes=[Bxs[t % 2]])
                    fw.op('pool', lambda: nc.gpsimd.tensor_tensor(out=hb[s2][0:n, :], in0=xs[t % 2][0:n, :],
                                                                  in1=gA[0:n, :], op=ALU.mult),
                          reads=[Bxs[t % 2], BgA], writes=[Bhb[s2]])

                def stageApe(t):
                    n = npart(t)
                    s2 = t % 2
                    for c in range(8):
                        fw.op('pe', lambda: PE.transpose(out=pT_bf[:, c * 128:c * 128 + n],
                                                         in_=hb[s2][0:n, c * 128:(c + 1) * 128],
                                                         identity=ident[0:n, 0:n]),
                              reads=[Bhb[s2]], writes=[BpT], mark=(c == 7))

                def stageA2(t):
                    n = npart(t)
                    s2 = t % 2
                    pv = pT_bf.rearrange("p (c k) -> p c k", k=128)
                    fw.op('act', lambda: A.activation(out=hT[s2][:, :, 0:n], in_=pv[:, :, 0:n], func=AF.Copy),
                          reads=[BpT], writes=[BhT[s2]])

                def stageB(t):
                    n = npart(t)
                    s3, s2 = t % 3, t % 2
                    for g, (c0, cw) in enumerate(GC):
                        for c in range(8):
                            fw.op('pe', lambda: PE.matmul(pj[g][0:n, 0:cw], lhsT=hT[s2][:, c, 0:n],
                                                          rhs=win[:, c, c0:c0 + cw], start=(c == 0), stop=(c == 7)),
                                  reads=[BhT[s2], Bwin[g]], writes=[Bpj[g]], mark=(c == 7))
                    STB = int(os.environ.get("STB", "99"))
                    if STB < 2:
                        return
                    for (_, bi, c0, H, _, _, i0, _) in TYPES:
                        fw.op('act', lambda: A.activation(out=sqs[0:n, i0 * 64:(i0 + H) * 64],
                                                          in_=pj[bi][0:n, c0:c0 + 64 * H], func=AF.Square),
                              reads=[Bpj[bi]], writes=[Bsqs])
                    fw.op('dve', lambda: V.tensor_reduce(out=ssh[s2][0:n, 0, :],
                                                         in_=sqs[0:n, :].rearrange("p (h d) -> p h d", d=64),
                                                         axis=AX.X, op=ALU.add),
                          reads=[Bsqs], writes=[Bssh[s2]])
                    fw.op('act', lambda: A.activation(out=ssh[s2][0:n, 1, :], in_=ssh[s2][0:n, 0, :], func=AF.Ln,
                                                      scale=1.0 / 64, bias=epsc[0:n, :]),
                          reads=[Bssh[s2]], writes=[Bssh[s2]])
                    fw.op('act', lambda: A.activation(out=ssh[s2][0:n, 2, :], in_=ssh[s2][0:n, 1, :], func=AF.Exp,
                                                      scale=-0.5),
                          reads=[Bssh[s2]], writes=[Bssh[s2]])
                    if STB < 3:
                        return
                    va = Vs[0:n, t, 0:384].rearrange("p (k w) -> p k w", w=192)
                    vsrc = pj[1][0:n, 128:256].rearrange("p (k w) -> p k w", w=64)
                    fw.op('act', lambda: A.activation(out=va[:, :, 0:64], in_=vsrc, func=AF.Copy),
                          reads=[Bpj[1]], writes=[Bvs])
                    fw.op('act', lambda: A.activation(out=va[:, :, 128:192], in_=vsrc, func=AF.Copy),
                          reads=[Bpj[1]], writes=[Bvs])
                    fw.op('act', lambda: A.activation(out=Vs[0:n, t, 384:896], in_=pj[4][0:n, :], func=AF.Copy),
                          reads=[Bpj[4]], writes=[Bvs])
                    if STB < 4:
                        return
                    for ty in range(4):
                        tb = 0 if ty < 2 else 1
                        cosv = rt[s3][0:n, tb * 64:tb * 64 + 32].unsqueeze(2).broadcast_to([n, 32, 2])
                        sinv = rt[s3][0:n, tb * 64 + 32:tb * 64 + 64].unsqueeze(2).broadcast_to([n, 32, 2])
                        t1 = tabs[s2][0:n, ty, 0, :].rearrange("p (i two) -> p i two", two=2)
                        t2 = tabs[s2][0:n, ty, 1, :].rearrange("p (i two) -> p i two", two=2)
                        fw.op('pool', lambda: nc.gpsimd.tensor_tensor(out=t1, in0=cosv, in1=g4v[0:n, ty], op=ALU.mult),
                              reads=[Brt[s3]], writes=[Btabs[s2][ty][0]])
                        fw.op('pool', lambda: nc.gpsimd.tensor_tensor(out=t2, in0=sinv, in1=gsn4v[0:n, ty], op=ALU.mult),
                              reads=[Brt[s3]], writes=[Btabs[s2][ty][1]])
                    if STB < 5:
                        return
                    for k, (name, bi, c0, H, tb, ty, i0, o0) in enumerate(TYPES):
                        if k > STB - 5:
                            break
                        W = 64 * H
                        q = pj[bi][0:n, c0:c0 + W]
                        q3 = q.rearrange("p (h d) -> p h d", d=64)
                        q4 = q.rearrange("p (h i two) -> p h i two", i=32, two=2)
                        T1 = tabs[s2][0:n, ty, 0, :]
                        T2 = tabs[s2][0:n, ty, 1, :].rearrange("p (i two) -> p i two", two=2)
                        tB = tmpB[0][0:n, 0:W]
                        tB4 = tB.rearrange("p (h i two) -> p h i two", i=32, two=2)
                        tB3 = tB.rearrange("p (h d) -> p h d", d=64)
                        STB2 = int(os.environ.get("STB2", "99"))
                        if STB2 < 1:
                            continue
                        fw.op('dve', lambda: V.tensor_tensor(out=tB4[:, :, :, 0], in0=q4[:, :, :, 1],
                                                             in1=T2[:, :, 0].unsqueeze(1).broadcast_to([n, H, 32]),
                                                             op=ALU.mult),
                              reads=[Bpj[bi], Btabs[s2][ty][1]], writes=[BtmpB[0]])
                        if STB2 < 2:
                            continue
                        OP2V = os.environ.get("OP2V", "")
                        if OP2V == "a":
                            fw.op('dve', lambda: V.tensor_tensor(out=tB4[:, :, :, 0], in0=q4[:, :, :, 1],
                                                                 in1=T2[:, :, 0].unsqueeze(1).broadcast_to([n, H, 32]),
                                                                 op=ALU.mult),
                                  reads=[Bpj[bi], Btabs[s2][ty][1]], writes=[BtmpB[0]])
                        elif OP2V == "b":
                            fw.op('dve', lambda: V.tensor_tensor(out=tB4[:, :, :, 1], in0=q4[:, :, :, 1],
                                                                 in1=T2[:, :, 0].unsqueeze(1).broadcast_to([n, H, 32]),
                                                                 op=ALU.mult),
                                  reads=[Bpj[bi], Btabs[s2][ty][1]], writes=[BtmpB[0]])
                        elif OP2V == "c":
                            fw.op('dve', lambda: V.tensor_tensor(out=tB4[:, :, :, 0], in0=q4[:, :, :, 0],
                                                                 in1=T2[:, :, 0].unsqueeze(1).broadcast_to([n, H, 32]),
                                                                 op=ALU.mult),
                                  reads=[Bpj[bi], Btabs[s2][ty][1]], writes=[BtmpB[0]])
                        else:
                            fw.op('dve', lambda: V.tensor_tensor(out=tB4[:, :, :, 1], in0=q4[:, :, :, 0],
                                                             in1=T2[:, :, 1].unsqueeze(1).broadcast_to([n, H, 32]),
                                                             op=ALU.mult),
                              reads=[Bpj[bi], Btabs[s2][ty][1]], writes=[BtmpB[0]])
                        if STB2 < 3:
                            continue
                        fw.op('dve', lambda: V.tensor_tensor(out=q3, in0=q3,
                                                             in1=T1.unsqueeze(1).broadcast_to([n, H, 64]),
                                                             op=ALU.mult),
                              reads=[Btabs[s2][ty][0], Bpj[bi]], writes=[Bpj[bi]])
                        if STB2 < 4:
                            continue
                        fw.op('dve', lambda: V.tensor_tensor(out=q3, in0=q3, in1=tB3, op=ALU.add),
                              reads=[BtmpB[0], Bpj[bi]], writes=[Bpj[bi]])
                        if STB2 < 5:
                            continue
                        rs = ssh[s2][0:n, 2, i0:i0 + H]
                        if name != "KA":
                            o3 = qkb[s2][0:n, o0:o0 + W].rearrange("p (h d) -> p h d", d=64)
                            fw.op('dve', lambda: V.tensor_tensor(out=o3, in0=q3,
                                                                 in1=rs.unsqueeze(2).broadcast_to([n, H, 64]),
                                                                 op=ALU.mult),
                                  reads=[Bpj[bi], Bssh[s2]], writes=[Bqkb[s2]])
                        else:
                            o4 = qkb[s2][0:n, o0:o0 + 256].rearrange("p (h r d) -> p h r d", r=2, d=64)
                            fw.op('dve', lambda: V.tensor_tensor(
                                out=o4, in0=q3.unsqueeze(2).broadcast_to([n, 2, 2, 64]),
                                in1=rs.unsqueeze(2).unsqueeze(3).broadcast_to([n, 2, 2, 64]), op=ALU.mult),
                                reads=[Bpj[bi], Bssh[s2]], writes=[Bqkb[s2]])

                def stageC(t):
                    n = npart(t)
                    s2 = t % 2
                    real = t < 32
                    if real:
                        for j, blk in enumerate(QBLK):
                            fw.op('pe', lambda: PE.transpose(out=tq[0][:, j * 128:j * 128 + n],
                                                             in_=qkb[s2][0:n, blk * 128:(blk + 1) * 128],
                                                             identity=ident[0:n, 0:n]),
                                  reads=[Bqkb[s2]], writes=[Btq[0]], mark=(j == 7))
                    for j, blk in enumerate(KBLK):
                        fw.op('pe', lambda: PE.transpose(out=tq[1][:, j * 128:j * 128 + n],
                                                         in_=qkb[s2][0:n, blk * 128:(blk + 1) * 128],
                                                         identity=ident[0:n, 0:n]),
                              reads=[Bqkb[s2]], writes=[Btq[1]], mark=(j == 5))
                    if real:
                        qs = (t // 4) % 2
                        tt = t % 4
                        fw.op('act', lambda: A.activation(out=qst[qs][:, :, tt * 128:(tt + 1) * 128],
                                                          in_=tq[0].rearrange("p (c k) -> p c k", k=128),
                                                          func=AF.Copy),
                              reads=[Btq[0]], writes=[Bqst[qs]])
                        if tt == 3:
                            cq = t // 4
                            fw.dma('sp', sqst[qs], qT_d[:, :, cq * 512:(cq + 1) * 512].rearrange("c p t -> p c t"),
                                   qst[qs][:], reads=[Bqst[qs]])
                    kv = tq[1][:, 0:768].rearrange("p (c k) -> p c k", k=128)
                    fw.op('act', lambda: A.activation(out=KT[:, :, t * 128:t * 128 + n], in_=kv[:, :, 0:n], func=AF.Copy),
                          reads=[Btq[1]])

                NTR = int(os.environ.get("NT_RUN", str(NT)))
                for it in range(NT + 2):
                    if stop_after < 0.3:
                        break
                    if it >= NTR + 2:
                        break
                    if it == 0:
                        stageA(0)
                    if it + 1 < min(NT, NTR):
                        stageA(it + 1)
                    if 0 <= it - 1 < min(NT, NTR) and stop_after >= 0.6:
                        stageB(it - 1)
                    if it < min(NT, NTR):
                        stageApe(it)
                        stageA2(it)
                    if 0 <= it - 2 < min(NT, NTR) and stop_after >= 0.8:
                        stageC(it - 2)
                fw.barrier()
                if debug:
                    s_dbg = fw.dma_sem("dbg")
                    fw.dma('sp', s_dbg, dbg_kt, KT[:])
                    fw.dma('sp', s_dbg, dbg_v, Vs[:])
                    fw.dma('sp', s_dbg, out[0:128, :], gA[:])
                    fw.barrier()

            p2 = contextlib.ExitStack()
            if stop_after < 2:
                return nc
            with p2:
                qc = [sb(p2, f"qc{i}", [128, 8, 512], BF16) for i in range(2)]
                pt = [sb(p2, f"pt{i}", [128, 1024], BF16) for i in range(3)]
                mixT = [sb(p2, f"mixT{i}", [128, 8, 512], BF16) for i in range(2)]
                osb = [sb(p2, f"osb{i}", [128, 512], F32) for i in range(2)]
                accP = [sb(p2, f"accP{i}", [128, 512], F32) for i in range(2)]
                tpair = [sb(p2, f"tpair{i}", [128, 1024], BF16) for i in range(2)]
                Btpair = [Buf("tpair") for _ in range(2)]
                zsb = sb(p2, "zsb", [128, 512], F32)
                rzt = sb(p2, "rzt", [128, 512], F32)
                dsb = sb(p2, "dsb", [128, 512], F32)
                sqf = sb(p2, "sqf", [128, 512], F32)
                lnb = sb(p2, "lnb", [128, 512], F32)
                rsb = sb(p2, "rsb", [128, 512], F32)
                Bqc = [Buf("qc") for _ in range(2)]
                Bpt = [Buf("pt") for _ in range(3)]
                BmixT = [Buf("mixT") for _ in range(2)]
                Bosb = [Buf("osb") for _ in range(2)]
                BaccP = [Buf("accP") for _ in range(2)]
                Bzsb, Brzt, Bdsb, Bsqf, Blnb, Brsb = (Buf(n) for n in ("zsb", "rzt", "dsb", "sqf", "lnb", "rsb"))
                sqc = [fw.dma_sem(f"qc{i}") for i in range(2)]
                smix = [fw.dma_sem(f"mix{i}") for i in range(2)]
                SS = [dbl[0], dbl[1]]
                BSS = [Buf("SS0", excl=True), Buf("SS1", excl=True)]
                O0, O1, ZB, AUX = banks[4], banks[5], banks[6], banks[7]
                BO0, BO1, BZB, BAUX = Bbank[4], Bbank[5], Bbank[6], Bbank[7]

                steps = [(c, p, k) for c in range(8) for p in range(8) for k in range(NT)]
                nsteps = len(steps)
                pending = {}

                def load_q(c):
                    sl = c % 2
                    fw.dma('sp', sqc[sl], qc[sl][:], qT_d[:, :, c * 512:(c + 1) * 512].rearrange("c p t -> p c t"),
                           writes=[Bqc[sl]])

                def emit_qk(i):
                    c, p, k = steps[i]
                    sl = c % 2
                    kb = (p // 2) if p < 4 else (p - 2)
                    nk = 128 if k < 32 else 16
                    col = k * 128
                    sb_ = i % 2
                    fw.op('pe', lambda: PE.matmul(SS[sb_][0:nk, 0:512], lhsT=KT[0:64, kb, col:col + nk],
                                                  rhs=qc[sl][0:64, p, :], start=True, stop=True),
                          reads=[Bqc[sl]], writes=[BSS[sb_]], mark=False)
                    fw.op('pe', lambda: PE.matmul(SS[sb_][0:nk, 512:1024], lhsT=KT[64:128, kb, col:col + nk],
                                                  rhs=qc[sl][64:128, p, :], start=True, stop=True),
                          reads=[Bqc[sl]], writes=[BSS[sb_]], mark=True)

                def emit_exp(i):
                    c, p, k = steps[i]
                    nk = 128 if k < 32 else 16
                    sb_ = i % 2
                    ps = i % 3
                    fw.op('act', lambda: A.activation(out=pt[ps][0:nk, :], in_=SS[sb_][0:nk, :], func=AF.Exp,
                                                      scale=0.125),
                          reads=[BSS[sb_]], writes=[Bpt[ps]])

                bpair_idx = [0]

                def emit_pv(i):
                    c, p, k = steps[i]
                    nk = 128 if k < 32 else 16
                    ps = i % 3
                    pt0 = pt[ps][0:nk, 0:512]
                    pt1 = pt[ps][0:nk, 512:1024]
                    st, sp_ = (k == 0), (k == NT - 1)
                    if p < 4:
                        base = (p // 2) * 192
                        fw.op('pe', lambda: PE.matmul(O0[0:65, :], lhsT=Vs[0:nk, k, base:base + 65],
                                                      rhs=pt0, start=st, stop=sp_),
                              reads=[Bpt[ps]], writes=[BO0], mark=False)
                        fw.op('pe', lambda: PE.matmul(O1[:, :], lhsT=Vs[0:nk, k, base + 64:base + 192],
                                                      rhs=pt1, start=st, stop=sp_),
                              reads=[Bpt[ps]], writes=[BO1], mark=True)
                    else:
                        base = 384 + (p - 4) * 128
                        ab = bpair_idx[0] % 2
                        fw.op('pe', lambda: PE.matmul(O0[:, :], lhsT=Vs[0:nk, k, base:base + 128],
                                                      rhs=pt0, start=st, stop=sp_),
                              reads=[Bpt[ps]], writes=[BO0], mark=False)
                        fw.op('pe', lambda: PE.matmul(O1[:, :], lhsT=Vs[0:nk, k, base:base + 128],
                                                      rhs=pt1, start=st, stop=sp_),
                              reads=[Bpt[ps]], writes=[BO1], mark=True)
                        if k < 32 and k % 2 == 1:
                            pp = (i - 1) % 3
                            j = (k // 2) % 2
                            fw.op('dve', lambda: V.tensor_tensor(out=tpair[j][:, :], in0=pt[pp][:, :], in1=pt[ps][:, :],
                                                                 op=ALU.add),
                                  reads=[Bpt[pp], Bpt[ps]], writes=[Btpair[j]])
                            if k == 1:
                                fw.op('pool', lambda: nc.gpsimd.tensor_copy(out=accP[ab][:, :], in_=tpair[j][:, 0:512]),
                                      reads=[Btpair[j]], writes=[BaccP[ab]])
                                fw.op('dve', lambda: V.tensor_copy(out=ZB[:, :], in_=tpair[j][:, 512:1024]),
                                      reads=[Btpair[j]], writes=[BZB])
                            else:
                                fw.op('pool', lambda: nc.gpsimd.tensor_tensor(out=accP[ab][:, :], in0=accP[ab][:, :],
                                                                              in1=tpair[j][:, 0:512], op=ALU.add),
                                      reads=[Btpair[j]], writes=[BaccP[ab]])
                                fw.op('dve', lambda: V.tensor_tensor(out=ZB[:, :], in0=ZB[:, :],
                                                                     in1=tpair[j][:, 512:1024], op=ALU.add),
                                      reads=[Btpair[j]], writes=[BZB])
                        elif k == 32:
                            fw.op('pool', lambda: nc.gpsimd.tensor_tensor(out=accP[ab][0:nk, :], in0=accP[ab][0:nk, :],
                                                                          in1=pt0, op=ALU.add),
                                  reads=[Bpt[ps]], writes=[BaccP[ab]])
                            fw.op('dve', lambda: V.tensor_tensor(out=ZB[0:nk, :], in0=ZB[0:nk, :], in1=pt1,
                                                                 op=ALU.add),
                                  reads=[Bpt[ps]], writes=[BZB])
                        if sp_:
                            bpair_idx[0] += 1

                def finish_chunk(c):
                    sl = c % 2
                    fw.dma('sp', smix[sl], mT_d[:, :, c * 512:(c + 1) * 512].rearrange("c p t -> p c t"),
                           mixT[sl][:], reads=[BmixT[sl]])

                def post_A(c, p):
                    sl = c % 2
                    L = []
                    L.append((0, lambda: fw.op('dve', lambda: V.tensor_copy(out=osb[0][0:65, :], in_=O0[0:65, :]),
                                               reads=[BO0], writes=[Bosb[0]])))
                    L.append((0, lambda: fw.op('dve', lambda: V.tensor_copy(out=osb[1][:, :], in_=O1[:, :]),
                                               reads=[BO1], writes=[Bosb[1]])))
                    if p == 3:
                        L.append((2, lambda: fw.op('act', lambda: A.activation(out=lnb[64:65, :], in_=osb[0][64:65, :],
                                                                               func=AF.Ln),
                                                   reads=[Bosb[0]], writes=[Blnb])))
                        L.append((2, lambda: fw.op('act', lambda: A.activation(out=rzt[64:65, :], in_=lnb[64:65, :],
                                                                               func=AF.Exp, scale=-1.0),
                                                   reads=[Blnb], writes=[Brzt])))
                        L.append((3, lambda: fw.op('act', lambda: A.activation(out=lnb[0:1, :], in_=osb[1][0:1, :],
                                                                               func=AF.Ln),
                                                   reads=[Bosb[1]], writes=[Blnb])))
                        L.append((3, lambda: fw.op('act', lambda: A.activation(out=rzt[0:1, :], in_=lnb[0:1, :],
                                                                               func=AF.Exp, scale=-1.0),
                                                   reads=[Blnb], writes=[Brzt])))
                    else:
                        L.append((1, lambda: fw.op('dve', lambda: V.reciprocal(out=rzt[64:65, :],
                                                                               in_=osb[0][64:65, :]),
                                                   reads=[Bosb[0]], writes=[Brzt])))
                        L.append((3, lambda: fw.op('dve', lambda: V.reciprocal(out=rzt[0:1, :],
                                                                               in_=osb[1][0:1, :]),
                                                   reads=[Bosb[1]], writes=[Brzt])))
                    L.append((8, lambda: fw.op('pe', lambda: PE.matmul(AUX[:, :], lhsT=ones_f[64:65, :],
                                                                        rhs=rzt[64:65, :], start=True, stop=True),
                                                reads=[Brzt], writes=[BAUX])))
                    L.append((11, lambda: fw.op('dve', lambda: V.tensor_tensor(out=mixT[sl][0:64, p, :],
                                                                               in0=osb[0][0:64, :], in1=AUX[0:64, :],
                                                                               op=ALU.mult),
                                                reads=[Bosb[0], BAUX], writes=[BmixT[sl]])))
                    L.append((14, lambda: fw.op('pe', lambda: PE.matmul(AUX[:, :], lhsT=ones_f[0:1, :],
                                                                        rhs=rzt[0:1, :], start=True, stop=True),
                                                reads=[Brzt], writes=[BAUX])))
                    L.append((17, lambda: fw.op('dve', lambda: V.tensor_tensor(out=mixT[sl][64:128, p, :],
                                                                               in0=osb[1][64:128, :],
                                                                               in1=AUX[64:128, :], op=ALU.mult),
                                                reads=[Bosb[1], BAUX], writes=[BmixT[sl]])))
                    return L

                def post_B(c, p):
                    sl = c % 2
                    ab = (bpair_idx[0] - 1) % 2
                    L = []
                    L.append((0, lambda: fw.op('dve', lambda: V.tensor_copy(out=osb[0][:, :], in_=O0[:, :]),
                                               reads=[BO0], writes=[Bosb[0]])))
                    L.append((0, lambda: fw.op('dve', lambda: V.tensor_copy(out=osb[1][:, :], in_=O1[:, :]),
                                               reads=[BO1], writes=[Bosb[1]])))
                    L.append((0, lambda: fw.op('dve', lambda: V.tensor_copy(out=zsb[:, :], in_=ZB[:, :]),
                                               reads=[BZB], writes=[Bzsb])))
                    L.append((2, lambda: fw.op('pe', lambda: PE.matmul(AUX[:, :], lhsT=ones_f[:, :],
                                                                       rhs=accP[ab][:, :], start=True, stop=True),
                                               reads=[BaccP[ab]], writes=[BAUX])))
                    L.append((5, lambda: fw.op('act', lambda: A.activation(out=lnb[:, :], in_=AUX[:, :], func=AF.Ln),
                                               reads=[BAUX], writes=[Blnb])))
                    L.append((5, lambda: fw.op('act', lambda: A.activation(out=rzt[:, :], in_=lnb[:, :], func=AF.Exp,
                                                                           scale=-1.0),
                                               reads=[Blnb], writes=[Brzt])))
                    L.append((8, lambda: fw.op('dve', lambda: V.tensor_tensor(out=osb[0][:, :], in0=osb[0][:, :],
                                                                              in1=rzt[:, :], op=ALU.mult),
                                               reads=[Brzt], writes=[Bosb[0]])))
                    L.append((8, lambda: fw.op('pe', lambda: PE.matmul(AUX[:, :], lhsT=ones_f[:, :], rhs=zsb[:, :],
                                                                       start=True, stop=True),
                                               reads=[Bzsb], writes=[BAUX])))
                    L.append((11, lambda: fw.op('act', lambda: A.activation(out=lnb[:, :], in_=AUX[:, :], func=AF.Ln),
                                               reads=[BAUX], writes=[Blnb])))
                    L.append((11, lambda: fw.op('act', lambda: A.activation(out=rzt[:, :], in_=lnb[:, :], func=AF.Exp,
                                                                           scale=-1.0),
                                               reads=[Blnb], writes=[Brzt])))
                    L.append((14, lambda: fw.op('dve', lambda: V.tensor_tensor(out=osb[1][:, :], in0=osb[1][:, :],
                                                                              in1=rzt[:, :], op=ALU.mult),
                                               reads=[Brzt], writes=[Bosb[1]])))
                    L.append((14, lambda: fw.op('dve', lambda: V.scalar_tensor_tensor(
                        out=dsb[:, :], in0=osb[1][:, :], scalar=neglam[:, 0:1], in1=osb[0][:, :],
                        op0=ALU.mult, op1=ALU.add), reads=[Bosb[0], Bosb[1]], writes=[Bdsb])))
                    L.append((17, lambda: fw.op('act', lambda: A.activation(out=sqf[:, :], in_=dsb[:, :],
                                                                           func=AF.Square),
                                               reads=[Bdsb], writes=[Bsqf])))
                    L.append((20, lambda: fw.op('pe', lambda: PE.matmul(AUX[:, :], lhsT=ones_f[:, :], rhs=sqf[:, :],
                                                                        start=True, stop=True),
                                                reads=[Bsqf], writes=[BAUX])))
                    L.append((23, lambda: fw.op('act', lambda: A.activation(out=lnb[:, :], in_=AUX[:, :], func=AF.Ln,
                                                                            scale=1.0 / 128, bias=epsc[:, :]),
                                                reads=[BAUX], writes=[Blnb])))
                    L.append((23, lambda: fw.op('act', lambda: A.activation(out=rsb[:, :], in_=lnb[:, :],
                                                                            func=AF.Exp, scale=-0.5),
                                                reads=[Blnb], writes=[Brsb])))

                    def fin():
                        fw.op('dve', lambda: V.scalar_tensor_tensor(out=mixT[sl][:, p, :], in0=dsb[:, :],
                                                                    scalar=gsubc[:, 0:1], in1=rsb[:, :],
                                                                    op0=ALU.mult, op1=ALU.mult),
                              reads=[Bdsb, Brsb], writes=[BmixT[sl]])
                        if p == 7:
                            finish_chunk(c)
                    L.append((26, fin))
                    return L

                load_q(0)
                emit_qk(0)
                emit_qk(1)
                for i in range(nsteps):
                    c, p, k = steps[i]
                    if p == 0 and k == 0 and c + 1 < 8:
                        load_q(c + 1)
                    emit_exp(i)
                    if i + 2 < nsteps:
                        emit_qk(i + 2)
                    emit_pv(i)
                    if p < 4 and k == 4:
                        issue_cast()
                    if k == NT - 1:
                        for d, fn in (post_A(c, p) if p < 4 else post_B(c, p)):
                            pending.setdefault(i + d, []).append(fn)
                    for fn in pending.pop(i, []):
                        fn()
                for key in sorted(pending):
                    for fn in pending[key]:
                        fn()
                while cast_jobs:
                    issue_cast()
                fw.barrier()

        p3 = contextlib.ExitStack()
        if stop_after < 3:
            return nc
        with p3:
            wout = sb(p3, "wout", [128, 8, D], BF16)
            wg = sb(p3, "wg", [128, 8, DFF], BF16)
            wu = sb(p3, "wu", [128, 8, DFF], BF16)
            wd = sb(p3, "wd", [128, NF, D], BF16)
            x1 = [sb(p3, f"x1_{i}", [128, 2, D], F32) for i in range(2)]
            mc = [sb(p3, f"mc{i}", [128, 8, 256], BF16) for i in range(2)]
            h2 = [sb(p3, f"h2_{i}", [128, D], BF16) for i in range(2)]
            h2T = [sb(p3, f"h2T{i}", [128, 8, 256], BF16) for i in range(2)]
            aT = [sb(p3, f"aT{i}", [128, 256], BF16) for i in range(4)]
            sg = [sb(p3, f"sg{i}", [128, 256], F32) for i in range(2)]
            ss2 = [sb(p3, f"ss2_{i}", [128, 4], F32) for i in range(2)]
            Bwout, Bwg, Bwu, Bwd = Buf("wout"), Buf("wg"), Buf("wu"), Buf("wd")
            Bx1 = [Buf("x1") for _ in range(2)]
            Bmc = [Buf("mc") for _ in range(2)]
            Bh2 = [Buf("h2") for _ in range(2)]
            Bh2T = [Buf("h2T") for _ in range(2)]
            BaT = [Buf("aT") for _ in range(4)]
            Bsg = [Buf("sg") for _ in range(2)]
            Bss2 = [Buf("ss2") for _ in range(2)]
            s_wout, s_wg, s_wu, s_wd = (fw.dma_sem(n) for n in ("wout", "wg", "wu", "wd"))
            sx1 = [fw.dma_sem(f"x1_{i}") for i in range(2)]
            smc = [fw.dma_sem(f"mc{i}") for i in range(2)]
            sout = [fw.dma_sem(f"out{i}") for i in range(2)]

            wout_v = wout_b.rearrange("(c p) e -> p c e", p=128)
            wg_v = wg_b.rearrange("(c p) e -> p c e", p=128)
            wu_v = wu_b.rearrange("(c p) e -> p c e", p=128)
            wd_v = wd_b.rearrange("(c p) e -> p c e", p=128)
            fw.dma('sp', s_wout, wout[:, 0:4, :], wout_v[:, 0:4, :], reads=[Bcw[0]], writes=[Bwout])
            fw.dma('act', s_wout, wout[:, 4:8, :], wout_v[:, 4:8, :], reads=[Bcw[0]], writes=[Bwout])
            for c in range(0, 8, 2):
                fw.dma('sp', s_wg, wg[:, c:c + 2, :], wg_v[:, c:c + 2, :], reads=[Bcw[1]], writes=[Bwg])
                fw.dma('act', s_wu, wu[:, c:c + 2, :], wu_v[:, c:c + 2, :], reads=[Bcw[2]], writes=[Bwu])
            for f in range(0, NF, 4):
                f1 = min(NF, f + 4)
                q = 'sp' if (f // 4) % 2 == 0 else 'act'
                fw.dma(q, s_wd, wd[:, f:f1, :], wd_v[:, f:f1, :], reads=[Bcw[3]], writes=[Bwd])

            YA, BYA = banks[0:4], Bbank[0:4]
            GU, BGU = banks[4:6], Bbank[4:6]
            OPJ, BOPJ = banks[6], Bbank[6]
            TP, BTP = banks[7][:].bitcast(BF16), Bbank[7]
            NCH = S // 256

            def load3(c):
                sl = c % 2
                fw.dma('sp', smc[sl], mc[sl][:], mT_d[:, :, c * 256:(c + 1) * 256].rearrange("c p t -> p c t"),
                       writes=[Bmc[sl]])
                fw.dma('sp', sx1[sl], x1[sl][:], x[c * 256:(c + 1) * 256, :].rearrange("(tt p) d -> p tt d", p=128),
                       writes=[Bx1[sl]])

            def pre1(c):
                sl = c % 2
                for tt in range(2):
                    for hf in range(2):
                        for ec in range(8):
                            fw.op('pe', lambda: PE.matmul(OPJ[:, :], lhsT=mc[sl][:, ec, tt * 128:(tt + 1) * 128],
                                                          rhs=wout[:, ec, hf * 512:(hf + 1) * 512],
                                                          start=(ec == 0), stop=(ec == 7)),
                                  reads=[Bmc[sl], Bwout], writes=[BOPJ], mark=(ec == 7))
                        xv = x1[sl][:, tt, hf * 512:(hf + 1) * 512]
                        fw.op('dve', lambda: V.tensor_tensor(out=xv, in0=xv, in1=OPJ[:, :], op=ALU.add),
                              reads=[BOPJ], writes=[Bx1[sl]])
                    fw.op('act', lambda: A.activation(out=junk[:, :], in_=x1[sl][:, tt, :], func=AF.Square,
                                                      accum_out=ss2[tt][:, 0:1]),
                          reads=[Bx1[sl]], writes=[Bss2[tt]])
                    fw.op('act', lambda: A.activation(out=ss2[tt][:, 1:2], in_=ss2[tt][:, 0:1], func=AF.Ln,
                                                      scale=1.0 / D, bias=epsc[:, :]),
                          reads=[Bss2[tt]], writes=[Bss2[tt]])
                    fw.op('act', lambda: A.activation(out=ss2[tt][:, 2:3], in_=ss2[tt][:, 1:2], func=AF.Exp,
                                                      scale=-0.5),
                          reads=[Bss2[tt]], writes=[Bss2[tt]])
                    fw.op('dve', lambda: V.tensor_scalar(out=h2[tt][:, :], in0=x1[sl][:, tt, :],
                                                         scalar1=ss2[tt][:, 2:3], scalar2=None, op0=ALU.mult),
                          reads=[Bx1[sl], Bss2[tt]], writes=[Bh2[tt]])

            def pre2(c):
                sl = c % 2
                for tt in range(2):
                    for dc in range(8):
                        fw.op('pe', lambda: PE.transpose(out=TP[:, dc * 128:(dc + 1) * 128],
                                                         in_=h2[tt][:, dc * 128:(dc + 1) * 128],
                                                         identity=ident[:, :]),
                              reads=[Bh2[tt]], writes=[BTP], mark=(dc == 7))
                    fw.op('dve', lambda: V.tensor_tensor(out=h2T[sl][:, :, tt * 128:(tt + 1) * 128],
                                                         in0=TP.rearrange("p (c k) -> p c k", k=128),
                                                         in1=g2c[:, :].unsqueeze(2).broadcast_to([128, 8, 128]),
                                                         op=ALU.mult),
                          reads=[BTP], writes=[Bh2T[sl]])

            def emit_gu(c, f):
                sl = c % 2
                gb = f % 2
                for dc in range(8):
                    fw.op('pe', lambda: PE.matmul(GU[gb][:, 0:256], lhsT=wg[:, dc, f * 128:(f + 1) * 128],
                                                  rhs=h2T[sl][:, dc, :], start=(dc == 0), stop=(dc == 7)),
                          reads=[Bh2T[sl], Bwg], writes=[BGU[gb]], mark=False)
                for dc in range(8):
                    fw.op('pe', lambda: PE.matmul(GU[gb][:, 256:512], lhsT=wu[:, dc, f * 128:(f + 1) * 128],
                                                  rhs=h2T[sl][:, dc, :], start=(dc == 0), stop=(dc == 7),
                                                  skip_group_check=True),
                          reads=[Bh2T[sl], Bwu], writes=[BGU[gb]], mark=(dc == 7))

            def emit_act(c, f):
                gb = f % 2
                fw.op('act', lambda: A.activation(out=sg[gb][:, :], in_=GU[gb][:, 0:256], func=AF.Silu),
                      reads=[BGU[gb]], writes=[Bsg[gb]])
                fw.op('dve', lambda: V.tensor_tensor(out=aT[f % 4][:, :], in0=sg[gb][:, :], in1=GU[gb][:, 256:512],
                                                     op=ALU.mult),
                      reads=[Bsg[gb], BGU[gb]], writes=[BaT[f % 4]])

            def emit_down(c, f):
                for tt in range(2):
                    for hf in range(2):
                        bi = tt * 2 + hf
                        fw.op('pe', lambda: PE.matmul(YA[bi][:, :], lhsT=aT[f % 4][:, tt * 128:(tt + 1) * 128],
                                                      rhs=wd[:, f, hf * 512:(hf + 1) * 512],
                                                      start=(f == 0), stop=(f == NF - 1)),
                              reads=[BaT[f % 4], Bwd], writes=[BYA[bi]], mark=(bi == 3))

            def fin3(c):
                sl = c % 2
                for tt in range(2):
                    for hf in range(2):
                        bi = tt * 2 + hf
                        xv = x1[sl][:, tt, hf * 512:(hf + 1) * 512]
                        fw.op('dve', lambda: V.tensor_tensor(out=xv, in0=xv, in1=YA[bi][:, :], op=ALU.add),
                              reads=[BYA[bi]], writes=[Bx1[sl]])
                fw.dma('sp', sout[sl], out[c * 256:(c + 1) * 256, :].rearrange("(tt p) d -> p tt d", p=128),
                       x1[sl][:], reads=[Bx1[sl]])

            load3(0)
            pre1(0)
            pre2(0)
            for c in range(NCH):
                if c + 1 < NCH:
                    load3(c + 1)
                emit_gu(c, 0)
                for f in range(NF):
                    if f + 1 < NF:
                        emit_gu(c, f + 1)
                    emit_act(c, f)
                    emit_down(c, f)
                    if c + 1 < NCH and f == 4:
                        pre1(c + 1)
                    if c + 1 < NCH and f == 12:
                        pre2(c + 1)
                fin3(c)
            fw.barrier()
    return nc


def rope_table():
    half = 32
    inv_a = np.power(np.float32(10000.0), -np.arange(0, half, 2, dtype=np.float32) / np.float32(half)).astype(np.float32)
    s = np.arange(S)
    r = (s // 64).astype(np.float32)
    c = (s % 64).astype(np.float32)
    ang_a = np.concatenate([r[:, None] * inv_a[None, :], c[:, None] * inv_a[None, :]], axis=-1).astype(np.float32)
    ang_a = np.concatenate([ang_a, np.zeros((16, 32), np.float32)], axis=0)
    inv_b = np.power(np.float32(10000.0), -np.arange(0, 64, 2, dtype=np.float32) / np.float32(64)).astype(np.float32)
    pos = np.concatenate([np.arange(16, S + 16), np.arange(0, 16)]).astype(np.float32)
    ang_b = (pos[:, None] * inv_b[None, :]).astype(np.float32)
    tab = np.concatenate([np.cos(ang_a), np.sin(ang_a), np.cos(ang_b), np.sin(ang_b)], axis=-1)
    return np.ascontiguousarray(tab.astype(np.float32))


_NC_CACHE = {}


def make_in_maps(inputs, cores):
    f = lambda a: np.ascontiguousarray(np.asarray(a, dtype=np.float32))
    shared = {k: f(v) for k, v in inputs.items() if k != "x"}
    shared["rope_tab"] = rope_table()
    shared["ident"] = np.eye(128, dtype=np.float32)
    xx = np.asarray(inputs["x"], dtype=np.float32)
    maps = []
    for b in cores:
        m = dict(shared)
        m["x"] = np.ascontiguousarray(xx[b])
        maps.append(m)
    return maps


def kernel(**inputs):
    if "nc" not in _NC_CACHE:
        _NC_CACHE["nc"] = build(False)
    nc = _NC_CACHE["nc"]
    in_maps = make_in_maps(inputs, list(range(N_CORES)))
    res = run_bass_kernel_spmd(nc, in_maps, core_ids=list(range(N_CORES)))
    return np.stack([np.asarray(r["out"], dtype=np.float32) for r in res.results], axis=0)
```

```python
import contextlib
import os
import numpy as np
import concourse.bass as bass
import concourse.mybir as mybir
from concourse.bass_utils import run_bass_kernel_spmd

F32 = mybir.dt.float32
BF16 = mybir.dt.bfloat16
AF = mybir.ActivationFunctionType
ALU = mybir.AluOpType
AX = mybir.AxisListType

S = 4096
D = 1024
NT = 33
EPS = 1e-6
DFF = 2816
NF = 22
INW = 2304
VW = 896
LAM_INIT = 0.2
N_CORES = 8


class Buf:
    __slots__ = ("name", "w", "r", "excl")

    def __init__(self, name, excl=False):
        self.name = name
        self.w = None
        self.r = []
        self.excl = excl


class FW:
    def __init__(self, nc, es):
        self.nc = nc
        self.es = es
        self.eng = {'pe': nc.tensor, 'act': nc.scalar, 'dve': nc.vector, 'pool': nc.gpsimd, 'sp': nc.sync}
        self.sems = {}
        self.cnt = {}
        for k in ['pe', 'act', 'dve', 'pool']:
            self.sems[k] = es.enter_context(nc.semaphore('s_' + k))
            self.cnt[k] = 0
        self.seen = {k: {} for k in self.eng}
        self.pe_unmarked = False

    def dma_sem(self, name):
        key = 'dma_' + name
        self.sems[key] = self.es.enter_context(self.nc.semaphore('s_' + key))
        self.cnt[key] = 0
        return key

    def _wait(self, e, tok):
        if tok is None:
            return
        key, val = tok
        if key == 'pe' and e == 'pe':
            return
        if self.seen[e].get(key, 0) >= val:
            return
        self.seen[e][key] = val
        self.eng[e].wait_ge(self.sems[key], val)

    def _eng_of(self, tok):
        return tok[0]

    def deps(self, e, reads=(), writes=()):
        for b in reads:
            self._wait(e, b.w)
            if b.excl:
                for t in b.r:
                    if t[0] != e:
                        self._wait(e, t)
        strict = (e == 'pool')
        for b in writes:
            if b.w is not None and (strict or b.w[0] != e):
                self._wait(e, b.w)
            for t in b.r:
                if strict or t[0] != e:
                    self._wait(e, t)

    def done(self, tok, reads=(), writes=()):
        for b in reads:
            b.r.append(tok)
            if len(b.r) > 8:
                best = {}
                for k, v in b.r:
                    if best.get(k, 0) < v:
                        best[k] = v
                b.r = list(best.items())
        for b in writes:
            b.w = tok
            b.r = []

    def op(self, e, inst_fn, reads=(), writes=(), mark=True):
        self.deps(e, reads, writes)
        inst = inst_fn()
        if mark:
            self.cnt[e] += 1
            inst.then_inc(self.sems[e], 1)
            tok = (e, self.cnt[e])
            if e == 'pe':
                self.pe_unmarked = False
        else:
            assert e == 'pe'
            tok = (e, self.cnt[e] + 1)
            self.pe_unmarked = True
        self.done(tok, reads, writes)
        return tok

    def dma(self, q, semkey, out, in_, reads=(), writes=(), **kw):
        self.deps(q, reads, writes)
        inst = self.eng[q].dma_start(out=out, in_=in_, **kw)
        self.cnt[semkey] += 16
        inst.then_inc(self.sems[semkey], 16)
        tok = (semkey, self.cnt[semkey])
        self.done(tok, reads, writes)
        return tok

    def barrier(self):
        assert not self.pe_unmarked
        for e in self.eng:
            for key, c in self.cnt.items():
                if c > 0 and key != e:
                    self._wait(e, (key, c))


def build(debug=False, stop_after=3):
    nc = bass.Bass("TRN2", target_bir_lowering=False)

    def din(name, shape):
        return nc.dram_tensor(name, shape, F32, kind="ExternalInput").ap()

    x = din("x", [S, D])
    meta = din("meta_tokens", [16, D])
    attn_g = din("attn_norm_g", [1, D])
    w_in = din("w_in", [1, D, INW])
    gvecs = [din(n, [1, 64]) for n in ("a_q_norm_g", "a_k_norm_g", "b_q_norm_g", "b_k_norm_g")]
    lvecs = [din(n, [1, 64]) for n in ("b_lambda_q1", "b_lambda_k1", "b_lambda_q2", "b_lambda_k2")]
    subln = din("b_subln_g", [1, 128])
    w_out = din("w_out", [1, D, D])
    ffn_g = din("ffn_norm_g", [1, D])
    w_gate = din("w_gate", [1, D, DFF])
    w_up = din("w_up", [1, D, DFF])
    w_down = din("w_down", [1, DFF, D])
    rope = din("rope_tab", [S + 16, 128])
    ident_d = din("ident", [128, 128])
    out = nc.dram_tensor("out", [S, D], F32, kind="ExternalOutput").ap()
    skind = "ExternalOutput" if debug else "Internal"
    qT_d = nc.dram_tensor("qT_d", [8, 128, S], BF16, kind=skind).ap()
    mT_d = nc.dram_tensor("mT_d", [8, 128, S], BF16, kind=skind).ap()
    wout_b = nc.dram_tensor("wout_b", [D, D], BF16).ap()
    wg_b = nc.dram_tensor("wg_b", [D, DFF], BF16).ap()
    wu_b = nc.dram_tensor("wu_b", [D, DFF], BF16).ap()
    wd_b = nc.dram_tensor("wd_b", [DFF, D], BF16).ap()
    if debug:
        dbg_kt = nc.dram_tensor("dbg_kt", [128, 6, S + 16], BF16, kind="ExternalOutput").ap()
        dbg_v = nc.dram_tensor("dbg_v", [128, NT, VW], BF16, kind="ExternalOutput").ap()

    es = contextlib.ExitStack()
    with es:
        fw = FW(nc, es)
        V = nc.vector
        A = nc.scalar
        PE = nc.tensor

        _alloc_n = [0]
        _alloc_max = int(os.environ.get("ALLOC_N", "100000"))

        def sb(stack, name, shape, dt):
            _alloc_n[0] += 1
            if _alloc_n[0] > _alloc_max:
                return None
            return stack.enter_context(nc.sbuf_tensor(name, shape, dt))

        ident = sb(es, "ident_bf", [128, 128], BF16)
        ones_bf = sb(es, "ones_bf", [128, 128], BF16)
        ones_f = sb(es, "ones_f", [128, 128], F32)
        epsc = sb(es, "epsc", [128, 1], F32)
        g4 = sb(es, "g4", [128, 4, 64], F32)
        gsn4 = sb(es, "gsn4", [128, 4, 64], F32)
        l4 = sb(es, "l4", [128, 4, 64], F32)
        ltmp = sb(es, "ltmp", [128, 2, 64], F32)
        lsum = sb(es, "lsum", [128, 2], F32)
        lexp = sb(es, "lexp", [128, 2], F32)
        lamc = sb(es, "lamc", [128, 1], F32)
        neglam = sb(es, "neglam", [128, 1], F32)
        gsubc = sb(es, "gsubc", [128, 1], F32)
        g2c = sb(es, "g2c", [128, 8], F32)
        junk = sb(es, "junk", [128, 1024], BF16)
        dbl = [es.enter_context(nc.psum_tensor(f"dbank{i}", [128, 1024], F32)) for i in range(2)]
        sing = [es.enter_context(nc.psum_tensor(f"bank{i}", [128, 512], F32)) for i in range(4, 8)]
        banks = [dbl[0][:, 0:512], dbl[0][:, 512:1024], dbl[1][:, 0:512], dbl[1][:, 512:1024]] + [t[:, :] for t in sing]
        Bbank = [Buf(f"bank{i}", excl=True) for i in range(8)]

        s_const = fw.dma_sem("const")
        s_constp = fw.dma_sem("constp")
        fw.dma('pool', s_constp, ident[:], ident_d)
        for i in range(4):
            fw.dma('sp', s_const, g4[:, i, :], gvecs[i][0].partition_broadcast(128))
            fw.dma('sp', s_const, l4[:, i, :], lvecs[i][0].partition_broadcast(128))
        fw.dma('sp', s_const, gsubc[:], subln[0].rearrange("(p o) -> p o", o=1))
        fw.dma('sp', s_const, g2c[:], ffn_g[0].rearrange("(c p) -> p c", p=128), allow_slow_non_contiguous=True)
        V.memset(ones_bf[:], 1.0)
        V.memset(ones_f[:], 1.0)
        V.memset(epsc[:], EPS).then_inc(fw.sems['dve'], 1)
        fw.cnt['dve'] += 1
        fw.barrier()
        if stop_after < -1:
            return nc
        g4v = g4[:].rearrange("p t (i two) -> p t i two", two=2)
        gsn4v = gsn4[:].rearrange("p t (i two) -> p t i two", two=2)
        Bc = Buf("consts")
        fw.op('dve', lambda: V.tensor_scalar(out=gsn4v[:, :, :, 0], in0=g4v[:, :, :, 1], scalar1=-1.0, scalar2=None,
                                             op0=ALU.mult), writes=[Bc])
        fw.op('dve', lambda: V.tensor_copy(out=gsn4v[:, :, :, 1], in_=g4v[:, :, :, 0]), writes=[Bc])
        fw.op('dve', lambda: V.tensor_tensor(out=ltmp[:, 0, :], in0=l4[:, 0, :], in1=l4[:, 1, :], op=ALU.mult), writes=[Bc])
        fw.op('dve', lambda: V.tensor_tensor(out=ltmp[:, 1, :], in0=l4[:, 2, :], in1=l4[:, 3, :], op=ALU.mult), writes=[Bc])
        fw.op('dve', lambda: V.tensor_reduce(out=lsum[:], in_=ltmp[:], axis=AX.X, op=ALU.add), reads=[Bc], writes=[Bc])
        fw.op('act', lambda: A.activation(out=lexp[:], in_=lsum[:], func=AF.Exp), reads=[Bc], writes=[Bc])
        fw.op('dve', lambda: V.tensor_tensor(out=lamc[:], in0=lexp[:, 0:1], in1=lexp[:, 1:2], op=ALU.subtract),
              reads=[Bc], writes=[Bc])
        fw.op('dve', lambda: V.tensor_scalar(out=neglam[:], in0=lamc[:], scalar1=LAM_INIT, scalar2=-1.0,
                                             op0=ALU.add, op1=ALU.mult), reads=[Bc], writes=[Bc])
        fw.op('dve', lambda: V.tensor_scalar(out=gsubc[:], in0=gsubc[:], scalar1=1.0 - LAM_INIT, scalar2=None,
                                             op0=ALU.mult), reads=[Bc], writes=[Bc])
        fw.barrier()

        if stop_after < 0:
            s_d0 = fw.dma_sem("d0")
            fw.dma('sp', s_d0, out[0:128, 0:256], gsn4[:].rearrange("p t d -> p (t d)"))
            fw.dma('sp', s_d0, out[128:256, 0:1], neglam[:], allow_slow_non_contiguous=True)
            fw.dma('sp', s_d0, out[256:384, 0:8], g2c[:])
            fw.barrier()
            return nc
        s_cw = [fw.dma_sem(f"cw{i}") for i in range(4)]
        Bcw = [Buf(f"cw{i}") for i in range(4)]
        cast_jobs = []
        for wi, (src, dst, rows, mld) in enumerate([(w_out[0], wout_b, D, None), (w_gate[0], wg_b, D, 5632),
                                                   (w_up[0], wu_b, D, 5632), (w_down[0], wd_b, DFF, None)]):
            r0 = 0
            while r0 < rows:
                r1 = min(rows, r0 + 512)
                kw = {} if mld is None else {"max_dma_last_dim": mld}
                cast_jobs.append((wi, dst[r0:r1, :], src[r0:r1, :], kw))
                r0 = r1

        def issue_cast():
            if cast_jobs:
                wi, dst, src, kw = cast_jobs.pop(0)
                fw.dma('pool', s_cw[wi], dst, src, writes=[Bcw[wi]], **kw)

        kv_stack = contextlib.ExitStack()
        with kv_stack:
            KT = sb(kv_stack, "KT", [128, 6, S + 16], BF16)
            Vs = sb(kv_stack, "Vs", [128, NT, VW], BF16)
            if os.environ.get("DUMP_EARLY") == "2":
                s_d1 = fw.dma_sem("d1")
                fw.dma('sp', s_d1, out[0:128, 0:256], gsn4[:].rearrange("p t d -> p (t d)"))
                fw.barrier()
                return nc

            p1 = contextlib.ExitStack()
            if stop_after < 0.1:
                return nc
            with p1:
                win = sb(p1, "win", [128, 8, INW], BF16)
                gA = sb(p1, "gA", [128, D], F32)
                xs = [sb(p1, f"xs{i}", [128, D], F32) for i in range(2)]
                rt = [sb(p1, f"rt{i}", [128, 128], F32) for i in range(3)]
                ssx = [sb(p1, f"ssx{i}", [128, 4], F32) for i in range(3)]
                hb = [sb(p1, f"hb{i}", [128, D], BF16) for i in range(2)]
                hT = [sb(p1, f"hT{i}", [128, 8, 128], BF16) for i in range(2)]
                ssh = [sb(p1, f"ssh{i}", [128, 3, 26], F32) for i in range(2)]
                tabs = [sb(p1, f"tabs{i}", [128, 4, 2, 64], F32) for i in range(2)]
                tmpB = [sb(p1, f"tmpB{i}", [128, 512], F32) for i in range(1)]
                qkb = [sb(p1, f"qkb{i}", [128, 14 * 128], BF16) for i in range(2)]
                qst = [sb(p1, f"qst{i}", [128, 8, 512], BF16) for i in range(2)]
                sqs = sb(p1, "sqs", [128, 26 * 64], F32)
                Bsqs = Buf("sqs")
                Bxs = [Buf("xs") for _ in range(3)]
                Brt = [Buf("rt") for _ in range(3)]
                Bssx = [Buf("ssx") for _ in range(3)]
                Bhb = [Buf("hb") for _ in range(2)]
                BhT = [Buf("hT") for _ in range(2)]
                Bssh = [Buf("ssh") for _ in range(2)]
                Btabs = [[[Buf("tabs") for _ in range(2)] for _ in range(4)] for _ in range(2)]
                BtmpB = [Buf("tmpB") for _ in range(1)]
                Bqkb = [Buf("qkb") for _ in range(2)]
                Bqst = [Buf("qst") for _ in range(2)]
                Bwin = [Buf(f"win{g}") for g in range(5)]
                sx = [fw.dma_sem(f"x{i}") for i in range(3)]
                srt = [fw.dma_sem(f"rt{i}") for i in range(3)]
                swin = [fw.dma_sem(f"win{g}") for g in range(5)]
                sqst = [fw.dma_sem(f"qst{i}") for i in range(2)]
                s_ga = fw.dma_sem("ga")
                BgA = Buf("gA")

                GC = [(0, 512), (512, 256), (768, 512), (1280, 512), (1792, 512)]
                w_in_v = w_in[0].rearrange("(c p) e -> p c e", p=128)
                if os.environ.get("DUMP_EARLY") == "4":
                    s_d1 = fw.dma_sem("d1")
                    fw.dma('sp', s_d1, out[0:128, 0:256], gsn4[:].rearrange("p t d -> p (t d)"))
                    fw.barrier()
                    return nc
                fw.dma('sp', s_ga, gA[:], attn_g[0].partition_broadcast(128), writes=[BgA])
                if os.environ.get("DUMP_EARLY"):
                    s_d1 = fw.dma_sem("d1")
                    if os.environ.get("DUMP_EARLY") == "3":
                        fw.dma('sp', s_d1, out[0:128, 0:256], gsn4[:].rearrange("p t d -> p (t d)"))
                    else:
                        fw.dma('sp', s_d1, out[0:128, :], gA[:], reads=[BgA])
                    fw.barrier()
                    return nc
                for g, (c0, cw) in enumerate(GC):
                    if os.environ.get("SKIP_WIN"):
                        break
                    fw.dma('pool', swin[g], win[:, :, c0:c0 + cw], w_in_v[:, :, c0:c0 + cw], writes=[Bwin[g]])
                Bvs = Buf("Vs")
                if not os.environ.get("SKIP_VMEM"):
                    fw.op('pool', lambda: nc.gpsimd.memset(Vs[:], 0.0), writes=[Bvs])
                    fw.op('pool', lambda: nc.gpsimd.memset(Vs[:, :, 64:65], 1.0), writes=[Bvs])
                    fw.op('pool', lambda: nc.gpsimd.memset(Vs[:, :, 256:257], 1.0), writes=[Bvs])

                pT, BpT = banks[0], Bbank[0]
                pT_bf = pT[:].bitcast(BF16)
                pj = banks[1:6]
                Bpj = Bbank[1:6]
                tq = [banks[6][:].bitcast(BF16), banks[7][:].bitcast(BF16)]
                Btq = Bbank[6:8]

                TYPES = [
                    ("QA", 0, 0, 8, 0, 0, 0, 0),
                    ("KA", 1, 0, 2, 0, 1, 8, 512),
                    ("QB", 2, 0, 8, 1, 2, 10, 768),
                    ("KB", 3, 0, 8, 1, 3, 18, 1280),
                ]
                QBLK = [0, 1, 2, 3, 6, 7, 8, 9]
                KBLK = [4, 5, 10, 11, 12, 13]

                def npart(t):
                    return 128 if t < 32 else 16

                def stageA(t):
                    n = npart(t)
                    s3, s2 = t % 3, t % 2
                    src = x[t * 128:(t + 1) * 128, :] if t < 32 else meta
                    fw.dma('sp', sx[t % 2], xs[t % 2][0:n, :], src, writes=[Bxs[t % 2]])
                    fw.dma('sp', srt[s3], rt[s3][0:n, :], rope[t * 128:t * 128 + n, :], writes=[Brt[s3]])
                    fw.op('act', lambda: A.activation(out=junk[0:n, :], in_=xs[t % 2][0:n, :], func=AF.Square,
                                                      accum_out=ssx[t % 2][0:n, 0:1]),
                          reads=[Bxs[t % 2]], writes=[Bssx[t % 2]])
                    fw.op('act', lambda: A.activation(out=ssx[t % 2][0:n, 1:2], in_=ssx[t % 2][0:n, 0:1], func=AF.Ln,
                                                      scale=1.0 / D, bias=epsc[0:n, :]),
                          reads=[Bssx[t % 2]], writes=[Bssx[t % 2]])
                    fw.op('act', lambda: A.activation(out=ssx[t % 2][0:n, 2:3], in_=ssx[t % 2][0:n, 1:2], func=AF.Exp,
                                                      scale=-0.5),
                          reads=[Bssx[t % 2]], writes=[Bssx[t % 2]])
                    fw.op('pool', lambda: nc.gpsimd.tensor_scalar(out=xs[t % 2][0:n, :], in0=xs[t % 2][0:n, :],
                                                                  scalar1=ssx[t % 2][0:n, 2:3], scalar2=1.0,
                                                                  op0=ALU.mult, op1=ALU.mult),
                          reads=[Bssx[t % 2]], writes=[Bxs[t % 2]])
                    fw.op('pool', lambda: nc.gpsimd.tensor_tensor(out=hb[s2][0:n, :], in0=xs[t % 2][0:n, :],
                                                                  in1=gA[0:n, :], op=ALU.mult),
                          reads=[Bxs[t % 2], BgA], writes=[Bhb[s2]])

                def stageApe(t):
                    n = npart(t)
                    s2 = t % 2
                    for c in range(8):
                        fw.op('pe', lambda: PE.transpose(out=pT_bf[:, c * 128:c * 128 + n],
                                                         in_=hb[s2][0:n, c * 128:(c + 1) * 128],
                                                         identity=ident[0:n, 0:n]),
                              reads=[Bhb[s2]], writes=[BpT], mark=(c == 7))

                def stageA2(t):
                    n = npart(t)
                    s2 = t % 2
                    pv = pT_bf.rearrange("p (c k) -> p c k", k=128)
                    fw.op('act', lambda: A.activation(out=hT[s2][:, :, 0:n], in_=pv[:, :, 0:n], func=AF.Copy),
                          reads=[BpT], writes=[BhT[s2]])

                def stageB(t):
                    n = npart(t)
                    s3, s2 = t % 3, t % 2
                    for g, (c0, cw) in enumerate(GC):
                        for c in range(8):
                            fw.op('pe', lambda: PE.matmul(pj[g][0:n, 0:cw], lhsT=hT[s2][:, c, 0:n],
                                                          rhs=win[:, c, c0:c0 + cw], start=(c == 0), stop=(c == 7)),
                                  reads=[BhT[s2], Bwin[g]], writes=[Bpj[g]], mark=(c == 7))
                    STB = int(os.environ.get("STB", "99"))
                    if STB < 2:
                        return
                    for (_, bi, c0, H, _, _, i0, _) in TYPES:
                        fw.op('act', lambda: A.activation(out=sqs[0:n, i0 * 64:(i0 + H) * 64],
                                                          in_=pj[bi][0:n, c0:c0 + 64 * H], func=AF.Square),
                              reads=[Bpj[bi]], writes=[Bsqs])
                    fw.op('dve', lambda: V.tensor_reduce(out=ssh[s2][0:n, 0, :],
                                                         in_=sqs[0:n, :].rearrange("p (h d) -> p h d", d=64),
                                                         axis=AX.X, op=ALU.add),
                          reads=[Bsqs], writes=[Bssh[s2]])
                    fw.op('act', lambda: A.activation(out=ssh[s2][0:n, 1, :], in_=ssh[s2][0:n, 0, :], func=AF.Ln,
                                                      scale=1.0 / 64, bias=epsc[0:n, :]),
                          reads=[Bssh[s2]], writes=[Bssh[s2]])
                    fw.op('act', lambda: A.activation(out=ssh[s2][0:n, 2, :], in_=ssh[s2][0:n, 1, :], func=AF.Exp,
                                                      scale=-0.5),
                          reads=[Bssh[s2]], writes=[Bssh[s2]])
                    if STB < 3:
                        return
                    va = Vs[0:n, t, 0:384].rearrange("p (k w) -> p k w", w=192)
                    vsrc = pj[1][0:n, 128:256].rearrange("p (k w) -> p k w", w=64)
                    fw.op('act', lambda: A.activation(out=va[:, :, 0:64], in_=vsrc, func=AF.Copy),
                          reads=[Bpj[1]], writes=[Bvs])
                    fw.op('act', lambda: A.activation(out=va[:, :, 128:192], in_=vsrc, func=AF.Copy),
                          reads=[Bpj[1]], writes=[Bvs])
                    fw.op('act', lambda: A.activation(out=Vs[0:n, t, 384:896], in_=pj[4][0:n, :], func=AF.Copy),
                          reads=[Bpj[4]], writes=[Bvs])
                    if STB < 4:
                        return
                    for ty in range(4):
                        tb = 0 if ty < 2 else 1
                        cosv = rt[s3][0:n, tb * 64:tb * 64 + 32].unsqueeze(2).broadcast_to([n, 32, 2])
                        sinv = rt[s3][0:n, tb * 64 + 32:tb * 64 + 64].unsqueeze(2).broadcast_to([n, 32, 2])
                        t1 = tabs[s2][0:n, ty, 0, :].rearrange("p (i two) -> p i two", two=2)
                        t2 = tabs[s2][0:n, ty, 1, :].rearrange("p (i two) -> p i two", two=2)
                        fw.op('pool', lambda: nc.gpsimd.tensor_tensor(out=t1, in0=cosv, in1=g4v[0:n, ty], op=ALU.mult),
                              reads=[Brt[s3]], writes=[Btabs[s2][ty][0]])
                        fw.op('pool', lambda: nc.gpsimd.tensor_tensor(out=t2, in0=sinv, in1=gsn4v[0:n, ty], op=ALU.mult),
                              reads=[Brt[s3]], writes=[Btabs[s2][ty][1]])
                    if STB < 5:
                        return
                    for k, (name, bi, c0, H, tb, ty, i0, o0) in enumerate(TYPES):
                        if k > STB - 5:
                            break
                        W = 64 * H
                        q = pj[bi][0:n, c0:c0 + W]
                        q3 = q.rearrange("p (h d) -> p h d", d=64)
                        q4 = q.rearrange("p (h i two) -> p h i two", i=32, two=2)
                        T1 = tabs[s2][0:n, ty, 0, :]
                        T2 = tabs[s2][0:n, ty, 1, :].rearrange("p (i two) -> p i two", two=2)
                        tB = tmpB[0][0:n, 0:W]
                        tB4 = tB.rearrange("p (h i two) -> p h i two", i=32, two=2)
                        tB3 = tB.rearrange("p (h d) -> p h d", d=64)
                        STB2 = int(os.environ.get("STB2", "99"))
                        if STB2 < 1:
                            continue
                        fw.op('dve', lambda: V.tensor_tensor(out=tB4[:, :, :, 0], in0=q4[:, :, :, 1],
                                                             in1=T2[:, :, 0].unsqueeze(1).broadcast_to([n, H, 32]),
                                                             op=ALU.mult),
                              reads=[Bpj[bi], Btabs[s2][ty][1]], writes=[BtmpB[0]])
                        if STB2 < 2:
                            continue
                        OP2V = os.environ.get("OP2V", "")
                        if OP2V == "a":
                            fw.op('dve', lambda: V.tensor_tensor(out=tB4[:, :, :, 0], in0=q4[:, :, :, 1],
                                                                 in1=T2[:, :, 0].unsqueeze(1).broadcast_to([n, H, 32]),
                                                                 op=ALU.mult),
                                  reads=[Bpj[bi], Btabs[s2][ty][1]], writes=[BtmpB[0]])
                        elif OP2V == "b":
                            fw.op('dve', lambda: V.tensor_tensor(out=tB4[:, :, :, 1], in0=q4[:, :, :, 1],
                                                                 in1=T2[:, :, 0].unsqueeze(1).broadcast_to([n, H, 32]),
                                                                 op=ALU.mult),
                                  reads=[Bpj[bi], Btabs[s2][ty][1]], writes=[BtmpB[0]])
                        elif OP2V == "c":
                            fw.op('dve', lambda: V.tensor_tensor(out=tB4[:, :, :, 0], in0=q4[:, :, :, 0],
                                                                 in1=T2[:, :, 0].unsqueeze(1).broadcast_to([n, H, 32]),
                                                                 op=ALU.mult),
                                  reads=[Bpj[bi], Btabs[s2][ty][1]], writes=[BtmpB[0]])
                        else:
                            fw.op('dve', lambda: V.tensor_tensor(out=tB4[:, :, :, 1], in0=q4[:, :, :, 0],
                                                             in1=T2[:, :, 1].unsqueeze(1).broadcast_to([n, H, 32]),
                                                             op=ALU.mult),
                              reads=[Bpj[bi], Btabs[s2][ty][1]], writes=[BtmpB[0]])
                        if STB2 < 3:
                            continue
                        fw.op('dve', lambda: V.tensor_tensor(out=q3, in0=q3,
                                                             in1=T1.unsqueeze(1).broadcast_to([n, H, 64]),
                                                             op=ALU.mult),
                              reads=[Btabs[s2][ty][0], Bpj[bi]], writes=[Bpj[bi]])
                        if STB2 < 4:
                            continue
                        fw.op('dve', lambda: V.tensor_tensor(out=q3, in0=q3, in1=tB3, op=ALU.add),
                              reads=[BtmpB[0], Bpj[bi]], writes=[Bpj[bi]])
                        if STB2 < 5:
                            continue
                        rs = ssh[s2][0:n, 2, i0:i0 + H]
                        if name != "KA":
                            o3 = qkb[s2][0:n, o0:o0 + W].rearrange("p (h d) -> p h d", d=64)
                            fw.op('dve', lambda: V.tensor_tensor(out=o3, in0=q3,
                                                                 in1=rs.unsqueeze(2).broadcast_to([n, H, 64]),
                                                                 op=ALU.mult),
                                  reads=[Bpj[bi], Bssh[s2]], writes=[Bqkb[s2]])
                        else:
                            o4 = qkb[s2][0:n, o0:o0 + 256].rearrange("p (h r d) -> p h r d", r=2, d=64)
                            fw.op('dve', lambda: V.tensor_tensor(
                                out=o4, in0=q3.unsqueeze(2).broadcast_to([n, 2, 2, 64]),
                                in1=rs.unsqueeze(2).unsqueeze(3).broadcast_to([n, 2, 2, 64]), op=ALU.mult),
                                reads=[Bpj[bi], Bssh[s2]], writes=[Bqkb[s2]])

                def stageC(t):
                    n = npart(t)
                    s2 = t % 2
                    real = t < 32
                    if real:
                        for j, blk in enumerate(QBLK):
                            fw.op('pe', lambda: PE.transpose(out=tq[0][:, j * 128:j * 128 + n],
                                                             in_=qkb[s2][0:n, blk * 128:(blk + 1) * 128],
                                                             identity=ident[0:n, 0:n]),
                                  reads=[Bqkb[s2]], writes=[Btq[0]], mark=(j == 7))
                    for j, blk in enumerate(KBLK):
                        fw.op('pe', lambda: PE.transpose(out=tq[1][:, j * 128:j * 128 + n],
                                                         in_=qkb[s2][0:n, blk * 128:(blk + 1) * 128],
                                                         identity=ident[0:n, 0:n]),
                              reads=[Bqkb[s2]], writes=[Btq[1]], mark=(j == 5))
                    if real:
                        qs = (t // 4) % 2
                        tt = t % 4
                        fw.op('act', lambda: A.activation(out=qst[qs][:, :, tt * 128:(tt + 1) * 128],
                                                          in_=tq[0].rearrange("p (c k) -> p c k", k=128),
                                                          func=AF.Copy),
                              reads=[Btq[0]], writes=[Bqst[qs]])
                        if tt == 3:
                            cq = t // 4
                            fw.dma('sp', sqst[qs], qT_d[:, :, cq * 512:(cq + 1) * 512].rearrange("c p t -> p c t"),
                                   qst[qs][:], reads=[Bqst[qs]])
                    kv = tq[1][:, 0:768].rearrange("p (c k) -> p c k", k=128)
                    fw.op('act', lambda: A.activation(out=KT[:, :, t * 128:t * 128 + n], in_=kv[:, :, 0:n], func=AF.Copy),
                          reads=[Btq[1]])

                NTR = int(os.environ.get("NT_RUN", str(NT)))
                for it in range(NT + 2):
                    if stop_after < 0.3:
                        break
                    if it >= NTR + 2:
                        break
                    if it == 0:
                        stageA(0)
                    if it + 1 < min(NT, NTR):
                        stageA(it + 1)
                    if 0 <= it - 1 < min(NT, NTR) and stop_after >= 0.6:
                        stageB(it - 1)
                    if it < min(NT, NTR):
                        stageApe(it)
                        stageA2(it)
                    if 0 <= it - 2 < min(NT, NTR) and stop_after >= 0.8:
                        stageC(it - 2)
                fw.barrier()
                if debug:
                    s_dbg = fw.dma_sem("dbg")
                    fw.dma('sp', s_dbg, dbg_kt, KT[:])
                    fw.dma('sp', s_dbg, dbg_v, Vs[:])
                    fw.dma('sp', s_dbg, out[0:128, :], gA[:])
                    fw.barrier()

            p2 = contextlib.ExitStack()
            if stop_after < 2:
                return nc
            with p2:
                qc = [sb(p2, f"qc{i}", [128, 8, 512], BF16) for i in range(2)]
                PTR = 5
                pt = [sb(p2, f"pt{i}", [128, 1024], BF16) for i in range(PTR)]
                mixT = [sb(p2, f"mixT{i}", [128, 8, 512], BF16) for i in range(2)]
                osb = [sb(p2, f"osb{i}", [128, 512], F32) for i in range(2)]
                accP = [sb(p2, f"accP{i}", [128, 512], F32) for i in range(2)]
                tpair = [sb(p2, f"tpair{i}", [128, 1024], BF16) for i in range(2)]
                Btpair = [Buf("tpair") for _ in range(2)]
                zsb = sb(p2, "zsb", [128, 512], F32)
                rzt = sb(p2, "rzt", [128, 512], F32)
                dsb = sb(p2, "dsb", [128, 512], F32)
                sqf = sb(p2, "sqf", [128, 512], F32)
                lnb = sb(p2, "lnb", [128, 512], F32)
                rsb = sb(p2, "rsb", [128, 512], F32)
                Bqc = [Buf("qc") for _ in range(2)]
                Bpt = [Buf("pt") for _ in range(PTR)]
                BmixT = [Buf("mixT") for _ in range(2)]
                Bosb = [Buf("osb") for _ in range(2)]
                BaccP = [Buf("accP") for _ in range(2)]
                Bzsb, Brzt, Bdsb, Bsqf, Blnb, Brsb = (Buf(n) for n in ("zsb", "rzt", "dsb", "sqf", "lnb", "rsb"))
                sqc = [fw.dma_sem(f"qc{i}") for i in range(2)]
                smix = [fw.dma_sem(f"mix{i}") for i in range(2)]
                SS = [dbl[0], dbl[1]]
                BSS = [Buf("SS0", excl=True), Buf("SS1", excl=True)]
                O0, O1, ZB, AUX = banks[4], banks[5], banks[6], banks[7]
                BO0, BO1, BZB, BAUX = Bbank[4], Bbank[5], Bbank[6], Bbank[7]

                steps = [(c, p, k) for c in range(8) for p in range(8) for k in range(NT)]
                nsteps = len(steps)
                pending = {}

                def load_q(c):
                    sl = c % 2
                    fw.dma('sp', sqc[sl], qc[sl][:], qT_d[:, :, c * 512:(c + 1) * 512].rearrange("c p t -> p c t"),
                           writes=[Bqc[sl]])

                def emit_qk(i):
                    c, p, k = steps[i]
                    sl = c % 2
                    kb = (p // 2) if p < 4 else (p - 2)
                    nk = 128 if k < 32 else 16
                    col = k * 128
                    sb_ = i % 2
                    fw.op('pe', lambda: PE.matmul(SS[sb_][0:nk, 0:512], lhsT=KT[0:64, kb, col:col + nk],
                                                  rhs=qc[sl][0:64, p, :], start=True, stop=True),
                          reads=[Bqc[sl]], writes=[BSS[sb_]], mark=False)
                    fw.op('pe', lambda: PE.matmul(SS[sb_][0:nk, 512:1024], lhsT=KT[64:128, kb, col:col + nk],
                                                  rhs=qc[sl][64:128, p, :], start=True, stop=True),
                          reads=[Bqc[sl]], writes=[BSS[sb_]], mark=True)

                def emit_exp(i):
                    c, p, k = steps[i]
                    nk = 128 if k < 32 else 16
                    sb_ = i % 2
                    ps = i % PTR
                    fw.op('act', lambda: A.activation(out=pt[ps][0:nk, :], in_=SS[sb_][0:nk, :], func=AF.Exp,
                                                      scale=0.125),
                          reads=[BSS[sb_]], writes=[Bpt[ps]])

                bpair_idx = [0]

                def emit_pv(i):
                    c, p, k = steps[i]
                    nk = 128 if k < 32 else 16
                    ps = i % PTR
                    pt0 = pt[ps][0:nk, 0:512]
                    pt1 = pt[ps][0:nk, 512:1024]
                    st, sp_ = (k == 0), (k == NT - 1)
                    if p < 4:
                        base = (p // 2) * 192
                        fw.op('pe', lambda: PE.matmul(O0[0:65, :], lhsT=Vs[0:nk, k, base:base + 65],
                                                      rhs=pt0, start=st, stop=sp_),
                              reads=[Bpt[ps]], writes=[BO0], mark=False)
                        fw.op('pe', lambda: PE.matmul(O1[:, :], lhsT=Vs[0:nk, k, base + 64:base + 192],
                                                      rhs=pt1, start=st, stop=sp_),
                              reads=[Bpt[ps]], writes=[BO1], mark=True)
                    else:
                        base = 384 + (p - 4) * 128
                        ab = bpair_idx[0] % 2
                        fw.op('pe', lambda: PE.matmul(O0[:, :], lhsT=Vs[0:nk, k, base:base + 128],
                                                      rhs=pt0, start=st, stop=sp_),
                              reads=[Bpt[ps]], writes=[BO0], mark=False)
                        fw.op('pe', lambda: PE.matmul(O1[:, :], lhsT=Vs[0:nk, k, base:base + 128],
                                                      rhs=pt1, start=st, stop=sp_),
                              reads=[Bpt[ps]], writes=[BO1], mark=True)
                        if k < 32 and k % 2 == 1:
                            pp = (i - 1) % PTR
                            j = (k // 2) % 2
                            fw.op('dve', lambda: V.tensor_tensor(out=tpair[j][:, :], in0=pt[pp][:, :], in1=pt[ps][:, :],
                                                                 op=ALU.add),
                                  reads=[Bpt[pp], Bpt[ps]], writes=[Btpair[j]])
                            if k == 1:
                                fw.op('pool', lambda: nc.gpsimd.tensor_copy(out=accP[ab][:, :], in_=tpair[j][:, 0:512]),
                                      reads=[Btpair[j]], writes=[BaccP[ab]])
                                fw.op('dve', lambda: V.tensor_copy(out=ZB[:, :], in_=tpair[j][:, 512:1024]),
                                      reads=[Btpair[j]], writes=[BZB])
                            else:
                                fw.op('pool', lambda: nc.gpsimd.tensor_tensor(out=accP[ab][:, :], in0=accP[ab][:, :],
                                                                              in1=tpair[j][:, 0:512], op=ALU.add),
                                      reads=[Btpair[j]], writes=[BaccP[ab]])
                                fw.op('dve', lambda: V.tensor_tensor(out=ZB[:, :], in0=ZB[:, :],
                                                                     in1=tpair[j][:, 512:1024], op=ALU.add),
                                      reads=[Btpair[j]], writes=[BZB])
                        elif k == 32:
                            fw.op('pool', lambda: nc.gpsimd.tensor_tensor(out=accP[ab][0:nk, :], in0=accP[ab][0:nk, :],
                                                                          in1=pt0, op=ALU.add),
                                  reads=[Bpt[ps]], writes=[BaccP[ab]])
                            fw.op('dve', lambda: V.tensor_tensor(out=ZB[0:nk, :], in0=ZB[0:nk, :], in1=pt1,
                                                                 op=ALU.add),
                                  reads=[Bpt[ps]], writes=[BZB])
                        if sp_:
                            bpair_idx[0] += 1

                def finish_chunk(c):
                    sl = c % 2
                    fw.dma('sp', smix[sl], mT_d[:, :, c * 512:(c + 1) * 512].rearrange("c p t -> p c t"),
                           mixT[sl][:], reads=[BmixT[sl]])

                def post_A(c, p):
                    sl = c % 2
                    L = []
                    L.append((0, lambda: fw.op('dve', lambda: V.tensor_copy(out=osb[0][0:65, :], in_=O0[0:65, :]),
                                               reads=[BO0], writes=[Bosb[0]])))
                    L.append((0, lambda: fw.op('dve', lambda: V.tensor_copy(out=osb[1][:, :], in_=O1[:, :]),
                                               reads=[BO1], writes=[Bosb[1]])))
                    if p == 3:
                        L.append((2, lambda: fw.op('act', lambda: A.activation(out=lnb[64:65, :], in_=osb[0][64:65, :],
                                                                               func=AF.Ln),
                                                   reads=[Bosb[0]], writes=[Blnb])))
                        L.append((2, lambda: fw.op('act', lambda: A.activation(out=rzt[64:65, :], in_=lnb[64:65, :],
                                                                               func=AF.Exp, scale=-1.0),
                                                   reads=[Blnb], writes=[Brzt])))
                        L.append((3, lambda: fw.op('act', lambda: A.activation(out=lnb[0:1, :], in_=osb[1][0:1, :],
                                                                               func=AF.Ln),
                                                   reads=[Bosb[1]], writes=[Blnb])))
                        L.append((3, lambda: fw.op('act', lambda: A.activation(out=rzt[0:1, :], in_=lnb[0:1, :],
                                                                               func=AF.Exp, scale=-1.0),
                                                   reads=[Blnb], writes=[Brzt])))
                    else:
                        L.append((1, lambda: fw.op('dve', lambda: V.reciprocal(out=rzt[64:65, :],
                                                                               in_=osb[0][64:65, :]),
                                                   reads=[Bosb[0]], writes=[Brzt])))
                        L.append((3, lambda: fw.op('dve', lambda: V.reciprocal(out=rzt[0:1, :],
                                                                               in_=osb[1][0:1, :]),
                                                   reads=[Bosb[1]], writes=[Brzt])))
                    L.append((8, lambda: fw.op('pe', lambda: PE.matmul(AUX[:, :], lhsT=ones_f[64:65, :],
                                                                        rhs=rzt[64:65, :], start=True, stop=True),
                                                reads=[Brzt], writes=[BAUX])))
                    L.append((11, lambda: fw.op('dve', lambda: V.tensor_tensor(out=mixT[sl][0:64, p, :],
                                                                               in0=osb[0][0:64, :], in1=AUX[0:64, :],
                                                                               op=ALU.mult),
                                                reads=[Bosb[0], BAUX], writes=[BmixT[sl]])))
                    L.append((14, lambda: fw.op('pe', lambda: PE.matmul(AUX[:, :], lhsT=ones_f[0:1, :],
                                                                        rhs=rzt[0:1, :], start=True, stop=True),
                                                reads=[Brzt], writes=[BAUX])))
                    L.append((17, lambda: fw.op('dve', lambda: V.tensor_tensor(out=mixT[sl][64:128, p, :],
                                                                               in0=osb[1][64:128, :],
                                                                               in1=AUX[64:128, :], op=ALU.mult),
                                                reads=[Bosb[1], BAUX], writes=[BmixT[sl]])))
                    return L

                def post_B(c, p):
                    sl = c % 2
                    ab = (bpair_idx[0] - 1) % 2
                    L = []
                    L.append((0, lambda: fw.op('dve', lambda: V.tensor_copy(out=osb[0][:, :], in_=O0[:, :]),
                                               reads=[BO0], writes=[Bosb[0]])))
                    L.append((0, lambda: fw.op('dve', lambda: V.tensor_copy(out=osb[1][:, :], in_=O1[:, :]),
                                               reads=[BO1], writes=[Bosb[1]])))
                    L.append((0, lambda: fw.op('dve', lambda: V.tensor_copy(out=zsb[:, :], in_=ZB[:, :]),
                                               reads=[BZB], writes=[Bzsb])))
                    L.append((2, lambda: fw.op('pe', lambda: PE.matmul(AUX[:, :], lhsT=ones_f[:, :],
                                                                       rhs=accP[ab][:, :], start=True, stop=True),
                                               reads=[BaccP[ab]], writes=[BAUX])))
                    L.append((5, lambda: fw.op('act', lambda: A.activation(out=lnb[:, :], in_=AUX[:, :], func=AF.Ln),
                                               reads=[BAUX], writes=[Blnb])))
                    L.append((5, lambda: fw.op('act', lambda: A.activation(out=rzt[:, :], in_=lnb[:, :], func=AF.Exp,
                                                                           scale=-1.0),
                                               reads=[Blnb], writes=[Brzt])))
                    L.append((8, lambda: fw.op('dve', lambda: V.tensor_tensor(out=osb[0][:, :], in0=osb[0][:, :],
                                                                              in1=rzt[:, :], op=ALU.mult),
                                               reads=[Brzt], writes=[Bosb[0]])))
                    L.append((8, lambda: fw.op('pe', lambda: PE.matmul(AUX[:, :], lhsT=ones_f[:, :], rhs=zsb[:, :],
                                                                       start=True, stop=True),
                                               reads=[Bzsb], writes=[BAUX])))
                    L.append((11, lambda: fw.op('act', lambda: A.activation(out=lnb[:, :], in_=AUX[:, :], func=AF.Ln),
                                               reads=[BAUX], writes=[Blnb])))
                    L.append((11, lambda: fw.op('act', lambda: A.activation(out=rzt[:, :], in_=lnb[:, :], func=AF.Exp,
                                                                           scale=-1.0),
                                               reads=[Blnb], writes=[Brzt])))
                    L.append((14, lambda: fw.op('dve', lambda: V.tensor_tensor(out=osb[1][:, :], in0=osb[1][:, :],
                                                                              in1=rzt[:, :], op=ALU.mult),
                                               reads=[Brzt], writes=[Bosb[1]])))
                    L.append((14, lambda: fw.op('dve', lambda: V.scalar_tensor_tensor(
                        out=dsb[:, :], in0=osb[1][:, :], scalar=neglam[:, 0:1], in1=osb[0][:, :],
                        op0=ALU.mult, op1=ALU.add), reads=[Bosb[0], Bosb[1]], writes=[Bdsb])))
                    L.append((17, lambda: fw.op('pool', lambda: nc.gpsimd.tensor_tensor(out=sqf[:, :], in0=dsb[:, :],
                                                                                   in1=dsb[:, :], op=ALU.mult),
                                               reads=[Bdsb], writes=[Bsqf])))
                    L.append((20, lambda: fw.op('pe', lambda: PE.matmul(AUX[:, :], lhsT=ones_f[:, :], rhs=sqf[:, :],
                                                                        start=True, stop=True),
                                                reads=[Bsqf], writes=[BAUX])))
                    L.append((23, lambda: fw.op('act', lambda: A.activation(out=lnb[:, :], in_=AUX[:, :], func=AF.Ln,
                                                                            scale=1.0 / 128, bias=epsc[:, :]),
                                                reads=[BAUX], writes=[Blnb])))
                    L.append((23, lambda: fw.op('act', lambda: A.activation(out=rsb[:, :], in_=lnb[:, :],
                                                                            func=AF.Exp, scale=-0.5),
                                                reads=[Blnb], writes=[Brsb])))

                    def fin():
                        fw.op('dve', lambda: V.scalar_tensor_tensor(out=mixT[sl][:, p, :], in0=dsb[:, :],
                                                                    scalar=gsubc[:, 0:1], in1=rsb[:, :],
                                                                    op0=ALU.mult, op1=ALU.mult),
                              reads=[Bdsb, Brsb], writes=[BmixT[sl]])
                        if p == 7:
                            finish_chunk(c)
                    L.append((26, fin))
                    return L

                load_q(0)
                emit_qk(0)
                emit_qk(1)
                for i in range(nsteps):
                    c, p, k = steps[i]
                    if p == 0 and k == 0 and c + 1 < 8:
                        load_q(c + 1)
                    emit_exp(i)
                    if i + 2 < nsteps:
                        emit_qk(i + 2)
                    emit_pv(i)
                    if p < 4 and k == 4:
                        issue_cast()
                    if k == NT - 1:
                        for d, fn in (post_A(c, p) if p < 4 else post_B(c, p)):
                            pending.setdefault(i + d, []).append(fn)
                    for fn in pending.pop(i, []):
                        fn()
                for key in sorted(pending):
                    for fn in pending[key]:
                        fn()
                while cast_jobs:
                    issue_cast()
                fw.barrier()

        p3 = contextlib.ExitStack()
        if stop_after < 3:
            return nc
        with p3:
            wout = sb(p3, "wout", [128, 8, D], BF16)
            wg = sb(p3, "wg", [128, 8, DFF], BF16)
            wu = sb(p3, "wu", [128, 8, DFF], BF16)
            wd = sb(p3, "wd", [128, NF, D], BF16)
            x1 = [sb(p3, f"x1_{i}", [128, 2, D], F32) for i in range(2)]
            mc = [sb(p3, f"mc{i}", [128, 8, 256], BF16) for i in range(2)]
            h2 = [sb(p3, f"h2_{i}", [128, D], BF16) for i in range(2)]
            h2T = [sb(p3, f"h2T{i}", [128, 8, 256], BF16) for i in range(2)]
            aT = [sb(p3, f"aT{i}", [128, 256], BF16) for i in range(4)]
            sg = [sb(p3, f"sg{i}", [128, 256], F32) for i in range(2)]
            ss2 = [sb(p3, f"ss2_{i}", [128, 4], F32) for i in range(2)]
            Bwout, Bwg, Bwu, Bwd = Buf("wout"), Buf("wg"), Buf("wu"), Buf("wd")
            Bx1 = [Buf("x1") for _ in range(2)]
            Bmc = [Buf("mc") for _ in range(2)]
            Bh2 = [Buf("h2") for _ in range(2)]
            Bh2T = [Buf("h2T") for _ in range(2)]
            BaT = [Buf("aT") for _ in range(4)]
            Bsg = [Buf("sg") for _ in range(2)]
            Bss2 = [Buf("ss2") for _ in range(2)]
            s_wout, s_wg, s_wu, s_wd = (fw.dma_sem(n) for n in ("wout", "wg", "wu", "wd"))
            sx1 = [fw.dma_sem(f"x1_{i}") for i in range(2)]
            smc = [fw.dma_sem(f"mc{i}") for i in range(2)]
            sout = [fw.dma_sem(f"out{i}") for i in range(2)]

            wout_v = wout_b.rearrange("(c p) e -> p c e", p=128)
            wg_v = wg_b.rearrange("(c p) e -> p c e", p=128)
            wu_v = wu_b.rearrange("(c p) e -> p c e", p=128)
            wd_v = wd_b.rearrange("(c p) e -> p c e", p=128)
            NBLK = (NF + 3) // 4
            Bwgb = [Buf(f"wg{b}") for b in range(NBLK)]
            Bwub = [Buf(f"wu{b}") for b in range(NBLK)]
            Bwdb = [Buf(f"wd{b}") for b in range(NBLK)]
            s_wgb = [fw.dma_sem(f"wgb{b}") for b in range(NBLK)]
            s_wub = [fw.dma_sem(f"wub{b}") for b in range(NBLK)]
            s_wdb = [fw.dma_sem(f"wdb{b}") for b in range(NBLK)]

            def load_weights():
                fw.dma('sp', s_wout, wout[:, 0:4, :], wout_v[:, 0:4, :], reads=[Bcw[0]], writes=[Bwout])
                fw.dma('act', s_wout, wout[:, 4:8, :], wout_v[:, 4:8, :], reads=[Bcw[0]], writes=[Bwout])
                for b in range(NBLK):
                    f0, f1 = 4 * b, min(NF, 4 * b + 4)
                    fw.dma('sp', s_wgb[b], wg[:, :, f0 * 128:f1 * 128], wg_v[:, :, f0 * 128:f1 * 128],
                           reads=[Bcw[1]], writes=[Bwgb[b]])
                    fw.dma('act', s_wub[b], wu[:, :, f0 * 128:f1 * 128], wu_v[:, :, f0 * 128:f1 * 128],
                           reads=[Bcw[2]], writes=[Bwub[b]])
                    fw.dma('sp' if b % 2 == 0 else 'act', s_wdb[b], wd[:, f0:f1, :], wd_v[:, f0:f1, :],
                           reads=[Bcw[3]], writes=[Bwdb[b]])

            YA, BYA = banks[0:4], Bbank[0:4]
            GU, BGU = banks[4:6], Bbank[4:6]
            OPJ, BOPJ = banks[6], Bbank[6]
            TP, BTP = banks[7][:].bitcast(BF16), Bbank[7]
            NCH = S // 256

            def load3(c):
                sl = c % 2
                fw.dma('sp', smc[sl], mc[sl][:], mT_d[:, :, c * 256:(c + 1) * 256].rearrange("c p t -> p c t"),
                       writes=[Bmc[sl]])
                fw.dma('sp', sx1[sl], x1[sl][:], x[c * 256:(c + 1) * 256, :].rearrange("(tt p) d -> p tt d", p=128),
                       writes=[Bx1[sl]])

            def pre1(c):
                sl = c % 2
                for tt in range(2):
                    for hf in range(2):
                        for ec in range(8):
                            fw.op('pe', lambda: PE.matmul(OPJ[:, :], lhsT=mc[sl][:, ec, tt * 128:(tt + 1) * 128],
                                                          rhs=wout[:, ec, hf * 512:(hf + 1) * 512],
                                                          start=(ec == 0), stop=(ec == 7)),
                                  reads=[Bmc[sl], Bwout], writes=[BOPJ], mark=(ec == 7))
                        xv = x1[sl][:, tt, hf * 512:(hf + 1) * 512]
                        fw.op('dve', lambda: V.tensor_tensor(out=xv, in0=xv, in1=OPJ[:, :], op=ALU.add),
                              reads=[BOPJ], writes=[Bx1[sl]])
                    fw.op('act', lambda: A.activation(out=junk[:, :], in_=x1[sl][:, tt, :], func=AF.Square,
                                                      accum_out=ss2[tt][:, 0:1]),
                          reads=[Bx1[sl]], writes=[Bss2[tt]])
                    fw.op('act', lambda: A.activation(out=ss2[tt][:, 1:2], in_=ss2[tt][:, 0:1], func=AF.Ln,
                                                      scale=1.0 / D, bias=epsc[:, :]),
                          reads=[Bss2[tt]], writes=[Bss2[tt]])
                    fw.op('act', lambda: A.activation(out=ss2[tt][:, 2:3], in_=ss2[tt][:, 1:2], func=AF.Exp,
                                                      scale=-0.5),
                          reads=[Bss2[tt]], writes=[Bss2[tt]])
                    fw.op('dve', lambda: V.tensor_scalar(out=h2[tt][:, :], in0=x1[sl][:, tt, :],
                                                         scalar1=ss2[tt][:, 2:3], scalar2=None, op0=ALU.mult),
                          reads=[Bx1[sl], Bss2[tt]], writes=[Bh2[tt]])

            def pre2(c):
                sl = c % 2
                for tt in range(2):
                    for dc in range(8):
                        fw.op('pe', lambda: PE.transpose(out=TP[:, dc * 128:(dc + 1) * 128],
                                                         in_=h2[tt][:, dc * 128:(dc + 1) * 128],
                                                         identity=ident[:, :]),
                              reads=[Bh2[tt]], writes=[BTP], mark=(dc == 7))
                    fw.op('dve', lambda: V.tensor_tensor(out=h2T[sl][:, :, tt * 128:(tt + 1) * 128],
                                                         in0=TP.rearrange("p (c k) -> p c k", k=128),
                                                         in1=g2c[:, :].unsqueeze(2).broadcast_to([128, 8, 128]),
                                                         op=ALU.mult),
                          reads=[BTP], writes=[Bh2T[sl]])

            def emit_gu(c, f):
                sl = c % 2
                gb = f % 2
                for dc in range(8):
                    fw.op('pe', lambda: PE.matmul(GU[gb][:, 0:256], lhsT=wg[:, dc, f * 128:(f + 1) * 128],
                                                  rhs=h2T[sl][:, dc, :], start=(dc == 0), stop=(dc == 7)),
                          reads=[Bh2T[sl], Bwgb[f // 4]], writes=[BGU[gb]], mark=False)
                for dc in range(8):
                    fw.op('pe', lambda: PE.matmul(GU[gb][:, 256:512], lhsT=wu[:, dc, f * 128:(f + 1) * 128],
                                                  rhs=h2T[sl][:, dc, :], start=(dc == 0), stop=(dc == 7),
                                                  skip_group_check=True),
                          reads=[Bh2T[sl], Bwub[f // 4]], writes=[BGU[gb]], mark=(dc == 7))

            def emit_act(c, f):
                gb = f % 2
                fw.op('act', lambda: A.activation(out=sg[gb][:, :], in_=GU[gb][:, 0:256], func=AF.Silu),
                      reads=[BGU[gb]], writes=[Bsg[gb]])
                fw.op('dve', lambda: V.tensor_tensor(out=aT[f % 4][:, :], in0=sg[gb][:, :], in1=GU[gb][:, 256:512],
                                                     op=ALU.mult),
                      reads=[Bsg[gb], BGU[gb]], writes=[BaT[f % 4]])

            def emit_down(c, f):
                for tt in range(2):
                    for hf in range(2):
                        bi = tt * 2 + hf
                        fw.op('pe', lambda: PE.matmul(YA[bi][:, :], lhsT=aT[f % 4][:, tt * 128:(tt + 1) * 128],
                                                      rhs=wd[:, f, hf * 512:(hf + 1) * 512],
                                                      start=(f == 0), stop=(f == NF - 1)),
                              reads=[BaT[f % 4], Bwdb[f // 4]], writes=[BYA[bi]], mark=(bi == 3))

            def fin3(c):
                sl = c % 2
                for tt in range(2):
                    for hf in range(2):
                        bi = tt * 2 + hf
                        xv = x1[sl][:, tt, hf * 512:(hf + 1) * 512]
                        fw.op('dve', lambda: V.tensor_tensor(out=xv, in0=xv, in1=YA[bi][:, :], op=ALU.add),
                              reads=[BYA[bi]], writes=[Bx1[sl]])
                fw.dma('sp', sout[sl], out[c * 256:(c + 1) * 256, :].rearrange("(tt p) d -> p tt d", p=128),
                       x1[sl][:], reads=[Bx1[sl]])

            load3(0)
            load_weights()
            pre1(0)
            pre2(0)
            for c in range(NCH):
                if c + 1 < NCH:
                    load3(c + 1)
                emit_gu(c, 0)
                for f in range(NF):
                    if f + 1 < NF:
                        emit_gu(c, f + 1)
                    emit_act(c, f)
                    emit_down(c, f)
                    if c + 1 < NCH and f == 4:
                        pre1(c + 1)
                    if c + 1 < NCH and f == 12:
                        pre2(c + 1)
                fin3(c)
            fw.barrier()
    return nc


def rope_table():
    half = 32
    inv_a = np.power(np.float32(10000.0), -np.arange(0, half, 2, dtype=np.float32) / np.float32(half)).astype(np.float32)
    s = np.arange(S)
    r = (s // 64).astype(np.float32)
    c = (s % 64).astype(np.float32)
    ang_a = np.concatenate([r[:, None] * inv_a[None, :], c[:, None] * inv_a[None, :]], axis=-1).astype(np.float32)
    ang_a = np.concatenate([ang_a, np.zeros((16, 32), np.float32)], axis=0)
    inv_b = np.power(np.float32(10000.0), -np.arange(0, 64, 2, dtype=np.float32) / np.float32(64)).astype(np.float32)
    pos = np.concatenate([np.arange(16, S + 16), np.arange(0, 16)]).astype(np.float32)
    ang_b = (pos[:, None] * inv_b[None, :]).astype(np.float32)
    tab = np.concatenate([np.cos(ang_a), np.sin(ang_a), np.cos(ang_b), np.sin(ang_b)], axis=-1)
    return np.ascontiguousarray(tab.astype(np.float32))


_NC_CACHE = {}


def make_in_maps(inputs, cores):
    f = lambda a: np.ascontiguousarray(np.asarray(a, dtype=np.float32))
    shared = {k: f(v) for k, v in inputs.items() if k != "x"}
    shared["rope_tab"] = rope_table()
    shared["ident"] = np.eye(128, dtype=np.float32)
    xx = np.asarray(inputs["x"], dtype=np.float32)
    maps = []
    for b in cores:
        m = dict(shared)
        m["x"] = np.ascontiguousarray(xx[b])
        maps.append(m)
    return maps


def kernel(**inputs):
    if "nc" not in _NC_CACHE:
        _NC_CACHE["nc"] = build(False)
    nc = _NC_CACHE["nc"]
    in_maps = make_in_maps(inputs, list(range(N_CORES)))
    res = run_bass_kernel_spmd(nc, in_maps, core_ids=list(range(N_CORES)))
    return np.stack([np.asarray(r["out"], dtype=np.float32) for r in res.results], axis=0)
```

```python
import contextlib
import os
import numpy as np
import concourse.bass as bass
import concourse.mybir as mybir
from concourse.bass_utils import run_bass_kernel_spmd

F32 = mybir.dt.float32
BF16 = mybir.dt.bfloat16
AF = mybir.ActivationFunctionType
ALU = mybir.AluOpType
AX = mybir.AxisListType

S = 4096
D = 1024
NT = 33
EPS = 1e-6
DFF = 2816
NF = 22
INW = 2304
VW = 896
LAM_INIT = 0.2
N_CORES = 8


class Buf:
    __slots__ = ("name", "w", "r", "excl")

    def __init__(self, name, excl=False):
        self.name = name
        self.w = None
        self.r = []
        self.excl = excl


class FW:
    def __init__(self, nc, es):
        self.nc = nc
        self.es = es
        self.eng = {'pe': nc.tensor, 'act': nc.scalar, 'dve': nc.vector, 'pool': nc.gpsimd, 'sp': nc.sync}
        self.sems = {}
        self.cnt = {}
        for k in ['pe', 'act', 'dve', 'pool']:
            self.sems[k] = es.enter_context(nc.semaphore('s_' + k))
            self.cnt[k] = 0
        self.seen = {k: {} for k in self.eng}
        self.pe_unmarked = False

    def dma_sem(self, name):
        key = 'dma_' + name
        self.sems[key] = self.es.enter_context(self.nc.semaphore('s_' + key))
        self.cnt[key] = 0
        return key

    def _wait(self, e, tok):
        if tok is None:
            return
        key, val = tok
        if key == 'pe' and e == 'pe':
            return
        if self.seen[e].get(key, 0) >= val:
            return
        self.seen[e][key] = val
        self.eng[e].wait_ge(self.sems[key], val)

    def _eng_of(self, tok):
        return tok[0]

    def deps(self, e, reads=(), writes=()):
        for b in reads:
            self._wait(e, b.w)
            if b.excl:
                for t in b.r:
                    if t[0] != e:
                        self._wait(e, t)
        strict = (e == 'pool')
        for b in writes:
            if b.w is not None and (strict or b.w[0] != e):
                self._wait(e, b.w)
            for t in b.r:
                if strict or t[0] != e:
                    self._wait(e, t)

    def done(self, tok, reads=(), writes=()):
        for b in reads:
            b.r.append(tok)
            if len(b.r) > 8:
                best = {}
                for k, v in b.r:
                    if best.get(k, 0) < v:
                        best[k] = v
                b.r = list(best.items())
        for b in writes:
            b.w = tok
            b.r = []

    def op(self, e, inst_fn, reads=(), writes=(), mark=True):
        self.deps(e, reads, writes)
        inst = inst_fn()
        if mark:
            self.cnt[e] += 1
            inst.then_inc(self.sems[e], 1)
            tok = (e, self.cnt[e])
            if e == 'pe':
                self.pe_unmarked = False
        else:
            assert e == 'pe'
            tok = (e, self.cnt[e] + 1)
            self.pe_unmarked = True
        self.done(tok, reads, writes)
        return tok

    def dma(self, q, semkey, out, in_, reads=(), writes=(), **kw):
        self.deps(q, reads, writes)
        inst = self.eng[q].dma_start(out=out, in_=in_, **kw)
        self.cnt[semkey] += 16
        inst.then_inc(self.sems[semkey], 16)
        tok = (semkey, self.cnt[semkey])
        self.done(tok, reads, writes)
        return tok

    def barrier(self):
        assert not self.pe_unmarked
        for e in self.eng:
            for key, c in self.cnt.items():
                if c > 0 and key != e:
                    self._wait(e, (key, c))


def build(debug=False, stop_after=3):
    nc = bass.Bass("TRN2", target_bir_lowering=False)

    def din(name, shape):
        return nc.dram_tensor(name, shape, F32, kind="ExternalInput").ap()

    x = din("x", [S, D])
    meta = din("meta_tokens", [16, D])
    attn_g = din("attn_norm_g", [1, D])
    w_in = din("w_in", [1, D, INW])
    gvecs = [din(n, [1, 64]) for n in ("a_q_norm_g", "a_k_norm_g", "b_q_norm_g", "b_k_norm_g")]
    lvecs = [din(n, [1, 64]) for n in ("b_lambda_q1", "b_lambda_k1", "b_lambda_q2", "b_lambda_k2")]
    subln = din("b_subln_g", [1, 128])
    w_out = din("w_out", [1, D, D])
    ffn_g = din("ffn_norm_g", [1, D])
    w_gate = din("w_gate", [1, D, DFF])
    w_up = din("w_up", [1, D, DFF])
    w_down = din("w_down", [1, DFF, D])
    rope = din("rope_tab", [S + 16, 128])
    ident_d = din("ident", [128, 128])
    out = nc.dram_tensor("out", [S, D], F32, kind="ExternalOutput").ap()
    skind = "ExternalOutput" if debug else "Internal"
    qT_d = nc.dram_tensor("qT_d", [8, 128, S], BF16, kind=skind).ap()
    mT_d = nc.dram_tensor("mT_d", [8, 128, S], BF16, kind=skind).ap()
    wout_b = nc.dram_tensor("wout_b", [D, D], BF16).ap()
    wg_b = nc.dram_tensor("wg_b", [D, DFF], BF16).ap()
    wu_b = nc.dram_tensor("wu_b", [D, DFF], BF16).ap()
    wd_b = nc.dram_tensor("wd_b", [DFF, D], BF16).ap()
    if debug:
        dbg_kt = nc.dram_tensor("dbg_kt", [128, 6, S + 16], BF16, kind="ExternalOutput").ap()
        dbg_v = nc.dram_tensor("dbg_v", [128, NT, VW], BF16, kind="ExternalOutput").ap()

    es = contextlib.ExitStack()
    with es:
        fw = FW(nc, es)
        V = nc.vector
        A = nc.scalar
        PE = nc.tensor

        _alloc_n = [0]
        _alloc_max = int(os.environ.get("ALLOC_N", "100000"))

        def sb(stack, name, shape, dt):
            _alloc_n[0] += 1
            if _alloc_n[0] > _alloc_max:
                return None
            return stack.enter_context(nc.sbuf_tensor(name, shape, dt))

        ident = sb(es, "ident_bf", [128, 128], BF16)
        ones_bf = sb(es, "ones_bf", [128, 128], BF16)
        ones_f = sb(es, "ones_f", [128, 128], F32)
        epsc = sb(es, "epsc", [128, 1], F32)
        g4 = sb(es, "g4", [128, 4, 64], F32)
        gsn4 = sb(es, "gsn4", [128, 4, 64], F32)
        l4 = sb(es, "l4", [128, 4, 64], F32)
        ltmp = sb(es, "ltmp", [128, 2, 64], F32)
        lsum = sb(es, "lsum", [128, 2], F32)
        lexp = sb(es, "lexp", [128, 2], F32)
        lamc = sb(es, "lamc", [128, 1], F32)
        neglam = sb(es, "neglam", [128, 1], F32)
        gsubc = sb(es, "gsubc", [128, 1], F32)
        g2c = sb(es, "g2c", [128, 8], F32)
        junk = sb(es, "junk", [128, 1024], BF16)
        dbl = [es.enter_context(nc.psum_tensor(f"dbank{i}", [128, 1024], F32)) for i in range(2)]
        sing = [es.enter_context(nc.psum_tensor(f"bank{i}", [128, 512], F32)) for i in range(4, 8)]
        banks = [dbl[0][:, 0:512], dbl[0][:, 512:1024], dbl[1][:, 0:512], dbl[1][:, 512:1024]] + [t[:, :] for t in sing]
        Bbank = [Buf(f"bank{i}", excl=True) for i in range(8)]

        s_const = fw.dma_sem("const")
        s_constp = fw.dma_sem("constp")
        fw.dma('pool', s_constp, ident[:], ident_d)
        for i in range(4):
            fw.dma('sp', s_const, g4[:, i, :], gvecs[i][0].partition_broadcast(128))
            fw.dma('sp', s_const, l4[:, i, :], lvecs[i][0].partition_broadcast(128))
        fw.dma('sp', s_const, gsubc[:], subln[0].rearrange("(p o) -> p o", o=1))
        fw.dma('sp', s_const, g2c[:], ffn_g[0].rearrange("(c p) -> p c", p=128), allow_slow_non_contiguous=True)
        V.memset(ones_bf[:], 1.0)
        V.memset(ones_f[:], 1.0)
        V.memset(epsc[:], EPS).then_inc(fw.sems['dve'], 1)
        fw.cnt['dve'] += 1
        fw.barrier()
        if stop_after < -1:
            return nc
        g4v = g4[:].rearrange("p t (i two) -> p t i two", two=2)
        gsn4v = gsn4[:].rearrange("p t (i two) -> p t i two", two=2)
        Bc = Buf("consts")
        fw.op('dve', lambda: V.tensor_scalar(out=gsn4v[:, :, :, 0], in0=g4v[:, :, :, 1], scalar1=-1.0, scalar2=None,
                                             op0=ALU.mult), writes=[Bc])
        fw.op('dve', lambda: V.tensor_copy(out=gsn4v[:, :, :, 1], in_=g4v[:, :, :, 0]), writes=[Bc])
        fw.op('dve', lambda: V.tensor_tensor(out=ltmp[:, 0, :], in0=l4[:, 0, :], in1=l4[:, 1, :], op=ALU.mult), writes=[Bc])
        fw.op('dve', lambda: V.tensor_tensor(out=ltmp[:, 1, :], in0=l4[:, 2, :], in1=l4[:, 3, :], op=ALU.mult), writes=[Bc])
        fw.op('dve', lambda: V.tensor_reduce(out=lsum[:], in_=ltmp[:], axis=AX.X, op=ALU.add), reads=[Bc], writes=[Bc])
        fw.op('act', lambda: A.activation(out=lexp[:], in_=lsum[:], func=AF.Exp), reads=[Bc], writes=[Bc])
        fw.op('dve', lambda: V.tensor_tensor(out=lamc[:], in0=lexp[:, 0:1], in1=lexp[:, 1:2], op=ALU.subtract),
              reads=[Bc], writes=[Bc])
        fw.op('dve', lambda: V.tensor_scalar(out=neglam[:], in0=lamc[:], scalar1=LAM_INIT, scalar2=-1.0,
                                             op0=ALU.add, op1=ALU.mult), reads=[Bc], writes=[Bc])
        fw.op('dve', lambda: V.tensor_scalar(out=gsubc[:], in0=gsubc[:], scalar1=1.0 - LAM_INIT, scalar2=None,
                                             op0=ALU.mult), reads=[Bc], writes=[Bc])
        fw.barrier()

        if stop_after < 0:
            s_d0 = fw.dma_sem("d0")
            fw.dma('sp', s_d0, out[0:128, 0:256], gsn4[:].rearrange("p t d -> p (t d)"))
            fw.dma('sp', s_d0, out[128:256, 0:1], neglam[:], allow_slow_non_contiguous=True)
            fw.dma('sp', s_d0, out[256:384, 0:8], g2c[:])
            fw.barrier()
            return nc
        s_cw = [fw.dma_sem(f"cw{i}") for i in range(4)]
        Bcw = [Buf(f"cw{i}") for i in range(4)]
        cast_jobs = []
        for wi, (src, dst, rows, mld) in enumerate([(w_out[0], wout_b, D, None), (w_gate[0], wg_b, D, 5632),
                                                   (w_up[0], wu_b, D, 5632), (w_down[0], wd_b, DFF, None)]):
            r0 = 0
            while r0 < rows:
                r1 = min(rows, r0 + 512)
                kw = {} if mld is None else {"max_dma_last_dim": mld}
                cast_jobs.append((wi, dst[r0:r1, :], src[r0:r1, :], kw))
                r0 = r1

        def issue_cast():
            if cast_jobs:
                wi, dst, src, kw = cast_jobs.pop(0)
                fw.dma('pool', s_cw[wi], dst, src, writes=[Bcw[wi]], **kw)

        kv_stack = contextlib.ExitStack()
        with kv_stack:
            KT = sb(kv_stack, "KT", [128, 6, S + 16], BF16)
            Vs = sb(kv_stack, "Vs", [128, NT, VW], BF16)
            if os.environ.get("DUMP_EARLY") == "2":
                s_d1 = fw.dma_sem("d1")
                fw.dma('sp', s_d1, out[0:128, 0:256], gsn4[:].rearrange("p t d -> p (t d)"))
                fw.barrier()
                return nc

            p1 = contextlib.ExitStack()
            if stop_after < 0.1:
                return nc
            with p1:
                win = sb(p1, "win", [128, 8, INW], BF16)
                gA = sb(p1, "gA", [128, D], F32)
                xs = [sb(p1, f"xs{i}", [128, D], F32) for i in range(2)]
                rt = [sb(p1, f"rt{i}", [128, 128], F32) for i in range(3)]
                ssx = [sb(p1, f"ssx{i}", [128, 4], F32) for i in range(3)]
                hb = [sb(p1, f"hb{i}", [128, D], BF16) for i in range(2)]
                hT = [sb(p1, f"hT{i}", [128, 8, 128], BF16) for i in range(2)]
                ssh = [sb(p1, f"ssh{i}", [128, 3, 26], F32) for i in range(2)]
                tabs = [sb(p1, f"tabs{i}", [128, 4, 2, 64], F32) for i in range(2)]
                tmpB = [sb(p1, f"tmpB{i}", [128, 512], F32) for i in range(1)]
                qkb = [sb(p1, f"qkb{i}", [128, 14 * 128], BF16) for i in range(2)]
                qst = [sb(p1, f"qst{i}", [128, 8, 512], BF16) for i in range(2)]
                sqs = sb(p1, "sqs", [128, 26 * 64], F32)
                Bsqs = Buf("sqs")
                Bxs = [Buf("xs") for _ in range(3)]
                Brt = [Buf("rt") for _ in range(3)]
                Bssx = [Buf("ssx") for _ in range(3)]
                Bhb = [Buf("hb") for _ in range(2)]
                BhT = [Buf("hT") for _ in range(2)]
                Bssh = [Buf("ssh") for _ in range(2)]
                Btabs = [[[Buf("tabs") for _ in range(2)] for _ in range(4)] for _ in range(2)]
                BtmpB = [Buf("tmpB") for _ in range(1)]
                Bqkb = [Buf("qkb") for _ in range(2)]
                Bqst = [Buf("qst") for _ in range(2)]
                Bwin = [Buf(f"win{g}") for g in range(5)]
                sx = [fw.dma_sem(f"x{i}") for i in range(3)]
                srt = [fw.dma_sem(f"rt{i}") for i in range(3)]
                swin = [fw.dma_sem(f"win{g}") for g in range(5)]
                sqst = [fw.dma_sem(f"qst{i}") for i in range(2)]
                s_ga = fw.dma_sem("ga")
                BgA = Buf("gA")

                GC = [(0, 512), (512, 256), (768, 512), (1280, 512), (1792, 512)]
                w_in_v = w_in[0].rearrange("(c p) e -> p c e", p=128)
                if os.environ.get("DUMP_EARLY") == "4":
                    s_d1 = fw.dma_sem("d1")
                    fw.dma('sp', s_d1, out[0:128, 0:256], gsn4[:].rearrange("p t d -> p (t d)"))
                    fw.barrier()
                    return nc
                fw.dma('sp', s_ga, gA[:], attn_g[0].partition_broadcast(128), writes=[BgA])
                if os.environ.get("DUMP_EARLY"):
                    s_d1 = fw.dma_sem("d1")
                    if os.environ.get("DUMP_EARLY") == "3":
                        fw.dma('sp', s_d1, out[0:128, 0:256], gsn4[:].rearrange("p t d -> p (t d)"))
                    else:
                        fw.dma('sp', s_d1, out[0:128, :], gA[:], reads=[BgA])
                    fw.barrier()
                    return nc
                for g, (c0, cw) in enumerate(GC):
                    if os.environ.get("SKIP_WIN"):
                        break
                    fw.dma('pool', swin[g], win[:, :, c0:c0 + cw], w_in_v[:, :, c0:c0 + cw], writes=[Bwin[g]])
                Bvs = Buf("Vs")
                if not os.environ.get("SKIP_VMEM"):
                    fw.op('pool', lambda: nc.gpsimd.memset(Vs[:], 0.0), writes=[Bvs])
                    fw.op('pool', lambda: nc.gpsimd.memset(Vs[:, :, 64:65], 1.0), writes=[Bvs])
                    fw.op('pool', lambda: nc.gpsimd.memset(Vs[:, :, 256:257], 1.0), writes=[Bvs])

                pT, BpT = banks[0], Bbank[0]
                pT_bf = pT[:].bitcast(BF16)
                pj = banks[1:6]
                Bpj = Bbank[1:6]
                tq = [banks[6][:].bitcast(BF16), banks[7][:].bitcast(BF16)]
                Btq = Bbank[6:8]

                TYPES = [
                    ("QA", 0, 0, 8, 0, 0, 0, 0),
                    ("KA", 1, 0, 2, 0, 1, 8, 512),
                    ("QB", 2, 0, 8, 1, 2, 10, 768),
                    ("KB", 3, 0, 8, 1, 3, 18, 1280),
                ]
                QBLK = [0, 1, 2, 3, 6, 7, 8, 9]
                KBLK = [4, 5, 10, 11, 12, 13]

                def npart(t):
                    return 128 if t < 32 else 16

                def stageA(t):
                    n = npart(t)
                    s3, s2 = t % 3, t % 2
                    src = x[t * 128:(t + 1) * 128, :] if t < 32 else meta
                    fw.dma('sp', sx[t % 2], xs[t % 2][0:n, :], src, writes=[Bxs[t % 2]])
                    fw.dma('sp', srt[s3], rt[s3][0:n, :], rope[t * 128:t * 128 + n, :], writes=[Brt[s3]])
                    fw.op('act', lambda: A.activation(out=junk[0:n, :], in_=xs[t % 2][0:n, :], func=AF.Square,
                                                      accum_out=ssx[t % 2][0:n, 0:1]),
                          reads=[Bxs[t % 2]], writes=[Bssx[t % 2]])
                    fw.op('act', lambda: A.activation(out=ssx[t % 2][0:n, 1:2], in_=ssx[t % 2][0:n, 0:1], func=AF.Ln,
                                                      scale=1.0 / D, bias=epsc[0:n, :]),
                          reads=[Bssx[t % 2]], writes=[Bssx[t % 2]])
                    fw.op('act', lambda: A.activation(out=ssx[t % 2][0:n, 2:3], in_=ssx[t % 2][0:n, 1:2], func=AF.Exp,
                                                      scale=-0.5),
                          reads=[Bssx[t % 2]], writes=[Bssx[t % 2]])
                    fw.op('pool', lambda: nc.gpsimd.tensor_scalar(out=xs[t % 2][0:n, :], in0=xs[t % 2][0:n, :],
                                                                  scalar1=ssx[t % 2][0:n, 2:3], scalar2=1.0,
                                                                  op0=ALU.mult, op1=ALU.mult),
                          reads=[Bssx[t % 2]], writes=[Bxs[t % 2]])
                    fw.op('pool', lambda: nc.gpsimd.tensor_tensor(out=hb[s2][0:n, :], in0=xs[t % 2][0:n, :],
                                                                  in1=gA[0:n, :], op=ALU.mult),
                          reads=[Bxs[t % 2], BgA], writes=[Bhb[s2]])

                def stageApe(t):
                    n = npart(t)
                    s2 = t % 2
                    for c in range(8):
                        fw.op('pe', lambda: PE.transpose(out=pT_bf[:, c * 128:c * 128 + n],
                                                         in_=hb[s2][0:n, c * 128:(c + 1) * 128],
                                                         identity=ident[0:n, 0:n]),
                              reads=[Bhb[s2]], writes=[BpT], mark=(c == 7))

                def stageA2(t):
                    n = npart(t)
                    s2 = t % 2
                    pv = pT_bf.rearrange("p (c k) -> p c k", k=128)
                    fw.op('act', lambda: A.activation(out=hT[s2][:, :, 0:n], in_=pv[:, :, 0:n], func=AF.Copy),
                          reads=[BpT], writes=[BhT[s2]])

                def stageB(t):
                    n = npart(t)
                    s3, s2 = t % 3, t % 2
                    for g, (c0, cw) in enumerate(GC):
                        for c in range(8):
                            fw.op('pe', lambda: PE.matmul(pj[g][0:n, 0:cw], lhsT=hT[s2][:, c, 0:n],
                                                          rhs=win[:, c, c0:c0 + cw], start=(c == 0), stop=(c == 7)),
                                  reads=[BhT[s2], Bwin[g]], writes=[Bpj[g]], mark=(c == 7))
                    STB = int(os.environ.get("STB", "99"))
                    if STB < 2:
                        return
                    for (_, bi, c0, H, _, _, i0, _) in TYPES:
                        fw.op('act', lambda: A.activation(out=sqs[0:n, i0 * 64:(i0 + H) * 64],
                                                          in_=pj[bi][0:n, c0:c0 + 64 * H], func=AF.Square),
                              reads=[Bpj[bi]], writes=[Bsqs])
                    fw.op('dve', lambda: V.tensor_reduce(out=ssh[s2][0:n, 0, :],
                                                         in_=sqs[0:n, :].rearrange("p (h d) -> p h d", d=64),
                                                         axis=AX.X, op=ALU.add),
                          reads=[Bsqs], writes=[Bssh[s2]])
                    fw.op('act', lambda: A.activation(out=ssh[s2][0:n, 1, :], in_=ssh[s2][0:n, 0, :], func=AF.Ln,
                                                      scale=1.0 / 64, bias=epsc[0:n, :]),
                          reads=[Bssh[s2]], writes=[Bssh[s2]])
                    fw.op('act', lambda: A.activation(out=ssh[s2][0:n, 2, :], in_=ssh[s2][0:n, 1, :], func=AF.Exp,
                                                      scale=-0.5),
                          reads=[Bssh[s2]], writes=[Bssh[s2]])
                    if STB < 3:
                        return
                    va = Vs[0:n, t, 0:384].rearrange("p (k w) -> p k w", w=192)
                    vsrc = pj[1][0:n, 128:256].rearrange("p (k w) -> p k w", w=64)
                    fw.op('act', lambda: A.activation(out=va[:, :, 0:64], in_=vsrc, func=AF.Copy),
                          reads=[Bpj[1]], writes=[Bvs])
                    fw.op('act', lambda: A.activation(out=va[:, :, 128:192], in_=vsrc, func=AF.Copy),
                          reads=[Bpj[1]], writes=[Bvs])
                    fw.op('act', lambda: A.activation(out=Vs[0:n, t, 384:896], in_=pj[4][0:n, :], func=AF.Copy),
                          reads=[Bpj[4]], writes=[Bvs])
                    if STB < 4:
                        return
                    for ty in range(4):
                        tb = 0 if ty < 2 else 1
                        cosv = rt[s3][0:n, tb * 64:tb * 64 + 32].unsqueeze(2).broadcast_to([n, 32, 2])
                        sinv = rt[s3][0:n, tb * 64 + 32:tb * 64 + 64].unsqueeze(2).broadcast_to([n, 32, 2])
                        t1 = tabs[s2][0:n, ty, 0, :].rearrange("p (i two) -> p i two", two=2)
                        t2 = tabs[s2][0:n, ty, 1, :].rearrange("p (i two) -> p i two", two=2)
                        fw.op('pool', lambda: nc.gpsimd.tensor_tensor(out=t1, in0=cosv, in1=g4v[0:n, ty], op=ALU.mult),
                              reads=[Brt[s3]], writes=[Btabs[s2][ty][0]])
                        fw.op('pool', lambda: nc.gpsimd.tensor_tensor(out=t2, in0=sinv, in1=gsn4v[0:n, ty], op=ALU.mult),
                              reads=[Brt[s3]], writes=[Btabs[s2][ty][1]])
                    if STB < 5:
                        return
                    for k, (name, bi, c0, H, tb, ty, i0, o0) in enumerate(TYPES):
                        if k > STB - 5:
                            break
                        W = 64 * H
                        q = pj[bi][0:n, c0:c0 + W]
                        q3 = q.rearrange("p (h d) -> p h d", d=64)
                        q4 = q.rearrange("p (h i two) -> p h i two", i=32, two=2)
                        T1 = tabs[s2][0:n, ty, 0, :]
                        T2 = tabs[s2][0:n, ty, 1, :].rearrange("p (i two) -> p i two", two=2)
                        tB = tmpB[0][0:n, 0:W]
                        tB4 = tB.rearrange("p (h i two) -> p h i two", i=32, two=2)
                        tB3 = tB.rearrange("p (h d) -> p h d", d=64)
                        STB2 = int(os.environ.get("STB2", "99"))
                        if STB2 < 1:
                            continue
                        fw.op('dve', lambda: V.tensor_tensor(out=tB4[:, :, :, 0], in0=q4[:, :, :, 1],
                                                             in1=T2[:, :, 0].unsqueeze(1).broadcast_to([n, H, 32]),
                                                             op=ALU.mult),
                              reads=[Bpj[bi], Btabs[s2][ty][1]], writes=[BtmpB[0]])
                        if STB2 < 2:
                            continue
                        OP2V = os.environ.get("OP2V", "")
                        if OP2V == "a":
                            fw.op('dve', lambda: V.tensor_tensor(out=tB4[:, :, :, 0], in0=q4[:, :, :, 1],
                                                                 in1=T2[:, :, 0].unsqueeze(1).broadcast_to([n, H, 32]),
                                                                 op=ALU.mult),
                                  reads=[Bpj[bi], Btabs[s2][ty][1]], writes=[BtmpB[0]])
                        elif OP2V == "b":
                            fw.op('dve', lambda: V.tensor_tensor(out=tB4[:, :, :, 1], in0=q4[:, :, :, 1],
                                                                 in1=T2[:, :, 0].unsqueeze(1).broadcast_to([n, H, 32]),
                                                                 op=ALU.mult),
                                  reads=[Bpj[bi], Btabs[s2][ty][1]], writes=[BtmpB[0]])
                        elif OP2V == "c":
                            fw.op('dve', lambda: V.tensor_tensor(out=tB4[:, :, :, 0], in0=q4[:, :, :, 0],
                                                                 in1=T2[:, :, 0].unsqueeze(1).broadcast_to([n, H, 32]),
                                                                 op=ALU.mult),
                                  reads=[Bpj[bi], Btabs[s2][ty][1]], writes=[BtmpB[0]])
                        else:
                            fw.op('dve', lambda: V.tensor_tensor(out=tB4[:, :, :, 1], in0=q4[:, :, :, 0],
                                                             in1=T2[:, :, 1].unsqueeze(1).broadcast_to([n, H, 32]),
                                                             op=ALU.mult),
                              reads=[Bpj[bi], Btabs[s2][ty][1]], writes=[BtmpB[0]])
                        if STB2 < 3:
                            continue
                        fw.op('dve', lambda: V.tensor_tensor(out=q3, in0=q3,
                                                             in1=T1.unsqueeze(1).broadcast_to([n, H, 64]),
                                                             op=ALU.mult),
                              reads=[Btabs[s2][ty][0], Bpj[bi]], writes=[Bpj[bi]])
                        if STB2 < 4:
                            continue
                        fw.op('dve', lambda: V.tensor_tensor(out=q3, in0=q3, in1=tB3, op=ALU.add),
                              reads=[BtmpB[0], Bpj[bi]], writes=[Bpj[bi]])
                        if STB2 < 5:
                            continue
                        rs = ssh[s2][0:n, 2, i0:i0 + H]
                        if name != "KA":
                            o3 = qkb[s2][0:n, o0:o0 + W].rearrange("p (h d) -> p h d", d=64)
                            fw.op('dve', lambda: V.tensor_tensor(out=o3, in0=q3,
                                                                 in1=rs.unsqueeze(2).broadcast_to([n, H, 64]),
                                                                 op=ALU.mult),
                                  reads=[Bpj[bi], Bssh[s2]], writes=[Bqkb[s2]])
                        else:
                            o4 = qkb[s2][0:n, o0:o0 + 256].rearrange("p (h r d) -> p h r d", r=2, d=64)
                            fw.op('dve', lambda: V.tensor_tensor(
                                out=o4, in0=q3.unsqueeze(2).broadcast_to([n, 2, 2, 64]),
                                in1=rs.unsqueeze(2).unsqueeze(3).broadcast_to([n, 2, 2, 64]), op=ALU.mult),
                                reads=[Bpj[bi], Bssh[s2]], writes=[Bqkb[s2]])

                def stageC(t):
                    n = npart(t)
                    s2 = t % 2
                    real = t < 32
                    if real:
                        for j, blk in enumerate(QBLK):
                            fw.op('pe', lambda: PE.transpose(out=tq[0][:, j * 128:j * 128 + n],
                                                             in_=qkb[s2][0:n, blk * 128:(blk + 1) * 128],
                                                             identity=ident[0:n, 0:n]),
                                  reads=[Bqkb[s2]], writes=[Btq[0]], mark=(j == 7))
                    for j, blk in enumerate(KBLK):
                        fw.op('pe', lambda: PE.transpose(out=tq[1][:, j * 128:j * 128 + n],
                                                         in_=qkb[s2][0:n, blk * 128:(blk + 1) * 128],
                                                         identity=ident[0:n, 0:n]),
                              reads=[Bqkb[s2]], writes=[Btq[1]], mark=(j == 5))
                    if real:
                        qs = (t // 4) % 2
                        tt = t % 4
                        fw.op('act', lambda: A.activation(out=qst[qs][:, :, tt * 128:(tt + 1) * 128],
                                                          in_=tq[0].rearrange("p (c k) -> p c k", k=128),
                                                          func=AF.Copy),
                              reads=[Btq[0]], writes=[Bqst[qs]])
                        if tt == 3:
                            cq = t // 4
                            fw.dma('sp', sqst[qs], qT_d[:, :, cq * 512:(cq + 1) * 512].rearrange("c p t -> p c t"),
                                   qst[qs][:], reads=[Bqst[qs]])
                    kv = tq[1][:, 0:768].rearrange("p (c k) -> p c k", k=128)
                    fw.op('act', lambda: A.activation(out=KT[:, :, t * 128:t * 128 + n], in_=kv[:, :, 0:n], func=AF.Copy),
                          reads=[Btq[1]])

                NTR = int(os.environ.get("NT_RUN", str(NT)))
                for it in range(NT + 2):
                    if stop_after < 0.3:
                        break
                    if it >= NTR + 2:
                        break
                    if it == 0:
                        stageA(0)
                    if it + 1 < min(NT, NTR):
                        stageA(it + 1)
                    if 0 <= it - 1 < min(NT, NTR) and stop_after >= 0.6:
                        stageB(it - 1)
                    if it < min(NT, NTR):
                        stageApe(it)
                        stageA2(it)
                    if 0 <= it - 2 < min(NT, NTR) and stop_after >= 0.8:
                        stageC(it - 2)
                fw.barrier()
                if debug:
                    s_dbg = fw.dma_sem("dbg")
                    fw.dma('sp', s_dbg, dbg_kt, KT[:])
                    fw.dma('sp', s_dbg, dbg_v, Vs[:])
                    fw.dma('sp', s_dbg, out[0:128, :], gA[:])
                    fw.barrier()

            p2 = contextlib.ExitStack()
            if stop_after < 2:
                return nc
            with p2:
                qc = [sb(p2, f"qc{i}", [128, 8, 512], BF16) for i in range(2)]
                PTR = 8
                pt = [sb(p2, f"pt{i}", [128, 1024], BF16) for i in range(PTR)]
                mixT = [sb(p2, f"mixT{i}", [128, 8, 512], BF16) for i in range(2)]
                osb = [sb(p2, f"osb{i}", [128, 512], F32) for i in range(2)]
                accP = [sb(p2, f"accP{i}", [128, 512], F32) for i in range(2)]
                tpair = [sb(p2, f"tpair{i}", [128, 1024], BF16) for i in range(2)]
                Btpair = [Buf("tpair") for _ in range(2)]
                zsb = sb(p2, "zsb", [128, 512], F32)
                rzt = sb(p2, "rzt", [128, 512], F32)
                dsb = sb(p2, "dsb", [128, 512], F32)
                sqf = sb(p2, "sqf", [128, 512], F32)
                lnb = sb(p2, "lnb", [128, 512], F32)
                rsb = sb(p2, "rsb", [128, 512], F32)
                Bqc = [Buf("qc") for _ in range(2)]
                Bpt = [Buf("pt") for _ in range(PTR)]
                BmixT = [Buf("mixT") for _ in range(2)]
                Bosb = [Buf("osb") for _ in range(2)]
                BaccP = [Buf("accP") for _ in range(2)]
                Bzsb, Brzt, Bdsb, Bsqf, Blnb, Brsb = (Buf(n) for n in ("zsb", "rzt", "dsb", "sqf", "lnb", "rsb"))
                sqc = [fw.dma_sem(f"qc{i}") for i in range(2)]
                smix = [fw.dma_sem(f"mix{i}") for i in range(2)]
                SS = [dbl[0], dbl[1]]
                BSS = [Buf("SS0", excl=True), Buf("SS1", excl=True)]
                O0, O1, ZB, AUX = banks[4], banks[5], banks[6], banks[7]
                BO0, BO1, BZB, BAUX = Bbank[4], Bbank[5], Bbank[6], Bbank[7]

                steps = [(c, p, k) for c in range(8) for p in range(8) for k in range(NT)]
                nsteps = len(steps)
                pending = {}

                def load_q(c):
                    sl = c % 2
                    fw.dma('sp', sqc[sl], qc[sl][:], qT_d[:, :, c * 512:(c + 1) * 512].rearrange("c p t -> p c t"),
                           writes=[Bqc[sl]])

                def emit_qk(i):
                    c, p, k = steps[i]
                    sl = c % 2
                    kb = (p // 2) if p < 4 else (p - 2)
                    nk = 128 if k < 32 else 16
                    col = k * 128
                    sb_ = i % 2
                    fw.op('pe', lambda: PE.matmul(SS[sb_][0:nk, 0:512], lhsT=KT[0:64, kb, col:col + nk],
                                                  rhs=qc[sl][0:64, p, :], start=True, stop=True),
                          reads=[Bqc[sl]], writes=[BSS[sb_]], mark=False)
                    fw.op('pe', lambda: PE.matmul(SS[sb_][0:nk, 512:1024], lhsT=KT[64:128, kb, col:col + nk],
                                                  rhs=qc[sl][64:128, p, :], start=True, stop=True),
                          reads=[Bqc[sl]], writes=[BSS[sb_]], mark=True)

                def emit_exp(i):
                    c, p, k = steps[i]
                    nk = 128 if k < 32 else 16
                    sb_ = i % 2
                    ps = i % PTR
                    fw.op('act', lambda: A.activation(out=pt[ps][0:nk, :], in_=SS[sb_][0:nk, :], func=AF.Exp,
                                                      scale=0.125),
                          reads=[BSS[sb_]], writes=[Bpt[ps]])

                bpair_idx = [0]

                def emit_pv(i):
                    c, p, k = steps[i]
                    nk = 128 if k < 32 else 16
                    ps = i % PTR
                    pt0 = pt[ps][0:nk, 0:512]
                    pt1 = pt[ps][0:nk, 512:1024]
                    st, sp_ = (k == 0), (k == NT - 1)
                    if p < 4:
                        base = (p // 2) * 192
                        fw.op('pe', lambda: PE.matmul(O0[0:65, :], lhsT=Vs[0:nk, k, base:base + 65],
                                                      rhs=pt0, start=st, stop=sp_),
                              reads=[Bpt[ps]], writes=[BO0], mark=False)
                        fw.op('pe', lambda: PE.matmul(O1[:, :], lhsT=Vs[0:nk, k, base + 64:base + 192],
                                                      rhs=pt1, start=st, stop=sp_),
                              reads=[Bpt[ps]], writes=[BO1], mark=True)
                    else:
                        base = 384 + (p - 4) * 128
                        ab = bpair_idx[0] % 2
                        fw.op('pe', lambda: PE.matmul(O0[:, :], lhsT=Vs[0:nk, k, base:base + 128],
                                                      rhs=pt0, start=st, stop=sp_),
                              reads=[Bpt[ps]], writes=[BO0], mark=False)
                        fw.op('pe', lambda: PE.matmul(O1[:, :], lhsT=Vs[0:nk, k, base:base + 128],
                                                      rhs=pt1, start=st, stop=sp_),
                              reads=[Bpt[ps]], writes=[BO1], mark=True)
                        if k < 32 and k % 2 == 1:
                            pp = (i - 1) % PTR
                            j = (k // 2) % 2
                            fw.op('dve', lambda: V.tensor_tensor(out=tpair[j][:, :], in0=pt[pp][:, :], in1=pt[ps][:, :],
                                                                 op=ALU.add),
                                  reads=[Bpt[pp], Bpt[ps]], writes=[Btpair[j]])
                            if k == 1:
                                fw.op('pool', lambda: nc.gpsimd.tensor_copy(out=accP[ab][:, :], in_=tpair[j][:, 0:512]),
                                      reads=[Btpair[j]], writes=[BaccP[ab]])
                                fw.op('dve', lambda: V.tensor_copy(out=ZB[:, :], in_=tpair[j][:, 512:1024]),
                                      reads=[Btpair[j]], writes=[BZB])
                            else:
                                fw.op('pool', lambda: nc.gpsimd.tensor_tensor(out=accP[ab][:, :], in0=accP[ab][:, :],
                                                                              in1=tpair[j][:, 0:512], op=ALU.add),
                                      reads=[Btpair[j]], writes=[BaccP[ab]])
                                fw.op('dve', lambda: V.tensor_tensor(out=ZB[:, :], in0=ZB[:, :],
                                                                     in1=tpair[j][:, 512:1024], op=ALU.add),
                                      reads=[Btpair[j]], writes=[BZB])
                        elif k == 32:
                            fw.op('pool', lambda: nc.gpsimd.tensor_tensor(out=accP[ab][0:nk, :], in0=accP[ab][0:nk, :],
                                                                          in1=pt0, op=ALU.add),
                                  reads=[Bpt[ps]], writes=[BaccP[ab]])
                            fw.op('dve', lambda: V.tensor_tensor(out=ZB[0:nk, :], in0=ZB[0:nk, :], in1=pt1,
                                                                 op=ALU.add),
                                  reads=[Bpt[ps]], writes=[BZB])
                        if sp_:
                            bpair_idx[0] += 1

                def finish_chunk(c):
                    sl = c % 2
                    fw.dma('sp', smix[sl], mT_d[:, :, c * 512:(c + 1) * 512].rearrange("c p t -> p c t"),
                           mixT[sl][:], reads=[BmixT[sl]])

                def post_A(c, p):
                    sl = c % 2
                    L = []
                    L.append((0, lambda: fw.op('dve', lambda: V.tensor_copy(out=osb[0][0:65, :], in_=O0[0:65, :]),
                                               reads=[BO0], writes=[Bosb[0]])))
                    L.append((0, lambda: fw.op('dve', lambda: V.tensor_copy(out=osb[1][:, :], in_=O1[:, :]),
                                               reads=[BO1], writes=[Bosb[1]])))
                    if p == 3:
                        L.append((2, lambda: fw.op('act', lambda: A.activation(out=lnb[64:65, :], in_=osb[0][64:65, :],
                                                                               func=AF.Ln),
                                                   reads=[Bosb[0]], writes=[Blnb])))
                        L.append((2, lambda: fw.op('act', lambda: A.activation(out=rzt[64:65, :], in_=lnb[64:65, :],
                                                                               func=AF.Exp, scale=-1.0),
                                                   reads=[Blnb], writes=[Brzt])))
                        L.append((3, lambda: fw.op('act', lambda: A.activation(out=lnb[0:1, :], in_=osb[1][0:1, :],
                                                                               func=AF.Ln),
                                                   reads=[Bosb[1]], writes=[Blnb])))
                        L.append((3, lambda: fw.op('act', lambda: A.activation(out=rzt[0:1, :], in_=lnb[0:1, :],
                                                                               func=AF.Exp, scale=-1.0),
                                                   reads=[Blnb], writes=[Brzt])))
                    else:
                        L.append((1, lambda: fw.op('dve', lambda: V.reciprocal(out=rzt[64:65, :],
                                                                               in_=osb[0][64:65, :]),
                                                   reads=[Bosb[0]], writes=[Brzt])))
                        L.append((3, lambda: fw.op('dve', lambda: V.reciprocal(out=rzt[0:1, :],
                                                                               in_=osb[1][0:1, :]),
                                                   reads=[Bosb[1]], writes=[Brzt])))
                    L.append((8, lambda: fw.op('pe', lambda: PE.matmul(AUX[:, :], lhsT=ones_f[64:65, :],
                                                                        rhs=rzt[64:65, :], start=True, stop=True),
                                                reads=[Brzt], writes=[BAUX])))
                    L.append((11, lambda: fw.op('dve', lambda: V.tensor_tensor(out=mixT[sl][0:64, p, :],
                                                                               in0=osb[0][0:64, :], in1=AUX[0:64, :],
                                                                               op=ALU.mult),
                                                reads=[Bosb[0], BAUX], writes=[BmixT[sl]])))
                    L.append((14, lambda: fw.op('pe', lambda: PE.matmul(AUX[:, :], lhsT=ones_f[0:1, :],
                                                                        rhs=rzt[0:1, :], start=True, stop=True),
                                                reads=[Brzt], writes=[BAUX])))
                    L.append((17, lambda: fw.op('dve', lambda: V.tensor_tensor(out=mixT[sl][64:128, p, :],
                                                                               in0=osb[1][64:128, :],
                                                                               in1=AUX[64:128, :], op=ALU.mult),
                                                reads=[Bosb[1], BAUX], writes=[BmixT[sl]])))
                    return L

                def post_B(c, p):
                    sl = c % 2
                    ab = (bpair_idx[0] - 1) % 2
                    L = []
                    L.append((0, lambda: fw.op('dve', lambda: V.tensor_copy(out=osb[0][:, :], in_=O0[:, :]),
                                               reads=[BO0], writes=[Bosb[0]])))
                    L.append((0, lambda: fw.op('dve', lambda: V.tensor_copy(out=osb[1][:, :], in_=O1[:, :]),
                                               reads=[BO1], writes=[Bosb[1]])))
                    L.append((0, lambda: fw.op('dve', lambda: V.tensor_copy(out=zsb[:, :], in_=ZB[:, :]),
                                               reads=[BZB], writes=[Bzsb])))
                    L.append((2, lambda: fw.op('pe', lambda: PE.matmul(AUX[:, :], lhsT=ones_f[:, :],
                                                                       rhs=accP[ab][:, :], start=True, stop=True),
                                               reads=[BaccP[ab]], writes=[BAUX])))
                    L.append((5, lambda: fw.op('act', lambda: A.activation(out=lnb[:, :], in_=AUX[:, :], func=AF.Ln),
                                               reads=[BAUX], writes=[Blnb])))
                    L.append((5, lambda: fw.op('act', lambda: A.activation(out=rzt[:, :], in_=lnb[:, :], func=AF.Exp,
                                                                           scale=-1.0),
                                               reads=[Blnb], writes=[Brzt])))
                    L.append((8, lambda: fw.op('dve', lambda: V.tensor_tensor(out=osb[0][:, :], in0=osb[0][:, :],
                                                                              in1=rzt[:, :], op=ALU.mult),
                                               reads=[Brzt], writes=[Bosb[0]])))
                    L.append((8, lambda: fw.op('pe', lambda: PE.matmul(AUX[:, :], lhsT=ones_f[:, :], rhs=zsb[:, :],
                                                                       start=True, stop=True),
                                               reads=[Bzsb], writes=[BAUX])))
                    L.append((11, lambda: fw.op('act', lambda: A.activation(out=lnb[:, :], in_=AUX[:, :], func=AF.Ln),
                                               reads=[BAUX], writes=[Blnb])))
                    L.append((11, lambda: fw.op('act', lambda: A.activation(out=rzt[:, :], in_=lnb[:, :], func=AF.Exp,
                                                                           scale=-1.0),
                                               reads=[Blnb], writes=[Brzt])))
                    L.append((14, lambda: fw.op('dve', lambda: V.tensor_tensor(out=osb[1][:, :], in0=osb[1][:, :],
                                                                              in1=rzt[:, :], op=ALU.mult),
                                               reads=[Brzt], writes=[Bosb[1]])))
                    L.append((14, lambda: fw.op('dve', lambda: V.scalar_tensor_tensor(
                        out=dsb[:, :], in0=osb[1][:, :], scalar=neglam[:, 0:1], in1=osb[0][:, :],
                        op0=ALU.mult, op1=ALU.add), reads=[Bosb[0], Bosb[1]], writes=[Bdsb])))
                    L.append((17, lambda: fw.op('pool', lambda: nc.gpsimd.tensor_tensor(out=sqf[:, :], in0=dsb[:, :],
                                                                                   in1=dsb[:, :], op=ALU.mult),
                                               reads=[Bdsb], writes=[Bsqf])))
                    L.append((20, lambda: fw.op('pe', lambda: PE.matmul(AUX[:, :], lhsT=ones_f[:, :], rhs=sqf[:, :],
                                                                        start=True, stop=True),
                                                reads=[Bsqf], writes=[BAUX])))
                    L.append((23, lambda: fw.op('act', lambda: A.activation(out=lnb[:, :], in_=AUX[:, :], func=AF.Ln,
                                                                            scale=1.0 / 128, bias=epsc[:, :]),
                                                reads=[BAUX], writes=[Blnb])))
                    L.append((23, lambda: fw.op('act', lambda: A.activation(out=rsb[:, :], in_=lnb[:, :],
                                                                            func=AF.Exp, scale=-0.5),
                                                reads=[Blnb], writes=[Brsb])))

                    def fin():
                        fw.op('dve', lambda: V.scalar_tensor_tensor(out=mixT[sl][:, p, :], in0=dsb[:, :],
                                                                    scalar=gsubc[:, 0:1], in1=rsb[:, :],
                                                                    op0=ALU.mult, op1=ALU.mult),
                              reads=[Bdsb, Brsb], writes=[BmixT[sl]])
                        if p == 7:
                            finish_chunk(c)
                    L.append((26, fin))
                    return L

                load_q(0)
                emit_qk(0)
                emit_qk(1)
                for i in range(nsteps):
                    c, p, k = steps[i]
                    if p == 0 and k == 0 and c + 1 < 8:
                        load_q(c + 1)
                    emit_exp(i)
                    if i + 2 < nsteps:
                        emit_qk(i + 2)
                    emit_pv(i)
                    if p < 4 and k == 4:
                        issue_cast()
                    if k == NT - 1:
                        for d, fn in (post_A(c, p) if p < 4 else post_B(c, p)):
                            pending.setdefault(i + d, []).append(fn)
                    for fn in pending.pop(i, []):
                        fn()
                for key in sorted(pending):
                    for fn in pending[key]:
                        fn()
                while cast_jobs:
                    issue_cast()
                fw.barrier()

        p3 = contextlib.ExitStack()
        if stop_after < 3:
            return nc
        with p3:
            wout = sb(p3, "wout", [128, 8, D], BF16)
            wg = sb(p3, "wg", [128, 8, DFF], BF16)
            wu = sb(p3, "wu", [128, 8, DFF], BF16)
            wd = sb(p3, "wd", [128, NF, D], BF16)
            x1 = [sb(p3, f"x1_{i}", [128, 2, D], F32) for i in range(2)]
            mc = [sb(p3, f"mc{i}", [128, 8, 256], BF16) for i in range(2)]
            h2 = [sb(p3, f"h2_{i}", [128, D], BF16) for i in range(2)]
            h2T = [sb(p3, f"h2T{i}", [128, 8, 256], BF16) for i in range(2)]
            aT = [sb(p3, f"aT{i}", [128, 256], BF16) for i in range(4)]
            sg = [sb(p3, f"sg{i}", [128, 256], F32) for i in range(2)]
            ss2 = [sb(p3, f"ss2_{i}", [128, 4], F32) for i in range(2)]
            Bwout, Bwg, Bwu, Bwd = Buf("wout"), Buf("wg"), Buf("wu"), Buf("wd")
            Bx1 = [Buf("x1") for _ in range(2)]
            Bmc = [Buf("mc") for _ in range(2)]
            Bh2 = [Buf("h2") for _ in range(2)]
            Bh2T = [Buf("h2T") for _ in range(2)]
            BaT = [Buf("aT") for _ in range(4)]
            Bsg = [Buf("sg") for _ in range(2)]
            Bss2 = [Buf("ss2") for _ in range(2)]
            s_wout, s_wg, s_wu, s_wd = (fw.dma_sem(n) for n in ("wout", "wg", "wu", "wd"))
            sx1 = [fw.dma_sem(f"x1_{i}") for i in range(2)]
            smc = [fw.dma_sem(f"mc{i}") for i in range(2)]
            sout = [fw.dma_sem(f"out{i}") for i in range(2)]

            wout_v = wout_b.rearrange("(c p) e -> p c e", p=128)
            wg_v = wg_b.rearrange("(c p) e -> p c e", p=128)
            wu_v = wu_b.rearrange("(c p) e -> p c e", p=128)
            wd_v = wd_b.rearrange("(c p) e -> p c e", p=128)
            NBLK = (NF + 3) // 4
            Bwgb = [Buf(f"wg{b}") for b in range(NBLK)]
            Bwub = [Buf(f"wu{b}") for b in range(NBLK)]
            Bwdb = [Buf(f"wd{b}") for b in range(NBLK)]
            s_wgb = [fw.dma_sem(f"wgb{b}") for b in range(NBLK)]
            s_wub = [fw.dma_sem(f"wub{b}") for b in range(NBLK)]
            s_wdb = [fw.dma_sem(f"wdb{b}") for b in range(NBLK)]

            def load_weights():
                fw.dma('sp', s_wout, wout[:, 0:4, :], wout_v[:, 0:4, :], reads=[Bcw[0]], writes=[Bwout])
                fw.dma('act', s_wout, wout[:, 4:8, :], wout_v[:, 4:8, :], reads=[Bcw[0]], writes=[Bwout])
                for b in range(NBLK):
                    f0, f1 = 4 * b, min(NF, 4 * b + 4)
                    fw.dma('sp', s_wgb[b], wg[:, :, f0 * 128:f1 * 128], wg_v[:, :, f0 * 128:f1 * 128],
                           reads=[Bcw[1]], writes=[Bwgb[b]])
                    fw.dma('act', s_wub[b], wu[:, :, f0 * 128:f1 * 128], wu_v[:, :, f0 * 128:f1 * 128],
                           reads=[Bcw[2]], writes=[Bwub[b]])
                    fw.dma('sp' if b % 2 == 0 else 'act', s_wdb[b], wd[:, f0:f1, :], wd_v[:, f0:f1, :],
                           reads=[Bcw[3]], writes=[Bwdb[b]])

            YA, BYA = banks[0:4], Bbank[0:4]
            GU, BGU = banks[4:6], Bbank[4:6]
            OPJ, BOPJ = banks[6], Bbank[6]
            TP, BTP = banks[7][:].bitcast(BF16), Bbank[7]
            NCH = S // 256

            def load3(c):
                sl = c % 2
                fw.dma('sp', smc[sl], mc[sl][:], mT_d[:, :, c * 256:(c + 1) * 256].rearrange("c p t -> p c t"),
                       writes=[Bmc[sl]])
                fw.dma('sp', sx1[sl], x1[sl][:], x[c * 256:(c + 1) * 256, :].rearrange("(tt p) d -> p tt d", p=128),
                       writes=[Bx1[sl]])

            def pre1(c):
                sl = c % 2
                for tt in range(2):
                    for hf in range(2):
                        for ec in range(8):
                            fw.op('pe', lambda: PE.matmul(OPJ[:, :], lhsT=mc[sl][:, ec, tt * 128:(tt + 1) * 128],
                                                          rhs=wout[:, ec, hf * 512:(hf + 1) * 512],
                                                          start=(ec == 0), stop=(ec == 7)),
                                  reads=[Bmc[sl], Bwout], writes=[BOPJ], mark=(ec == 7))
                        xv = x1[sl][:, tt, hf * 512:(hf + 1) * 512]
                        fw.op('dve', lambda: V.tensor_tensor(out=xv, in0=xv, in1=OPJ[:, :], op=ALU.add),
                              reads=[BOPJ], writes=[Bx1[sl]])
                    fw.op('act', lambda: A.activation(out=junk[:, :], in_=x1[sl][:, tt, :], func=AF.Square,
                                                      accum_out=ss2[tt][:, 0:1]),
                          reads=[Bx1[sl]], writes=[Bss2[tt]])
                    fw.op('act', lambda: A.activation(out=ss2[tt][:, 1:2], in_=ss2[tt][:, 0:1], func=AF.Ln,
                                                      scale=1.0 / D, bias=epsc[:, :]),
                          reads=[Bss2[tt]], writes=[Bss2[tt]])
                    fw.op('act', lambda: A.activation(out=ss2[tt][:, 2:3], in_=ss2[tt][:, 1:2], func=AF.Exp,
                                                      scale=-0.5),
                          reads=[Bss2[tt]], writes=[Bss2[tt]])
                    fw.op('dve', lambda: V.tensor_scalar(out=h2[tt][:, :], in0=x1[sl][:, tt, :],
                                                         scalar1=ss2[tt][:, 2:3], scalar2=None, op0=ALU.mult),
                          reads=[Bx1[sl], Bss2[tt]], writes=[Bh2[tt]])

            def pre2(c):
                sl = c % 2
                for tt in range(2):
                    for dc in range(8):
                        fw.op('pe', lambda: PE.transpose(out=TP[:, dc * 128:(dc + 1) * 128],
                                                         in_=h2[tt][:, dc * 128:(dc + 1) * 128],
                                                         identity=ident[:, :]),
                              reads=[Bh2[tt]], writes=[BTP], mark=(dc == 7))
                    fw.op('dve', lambda: V.tensor_tensor(out=h2T[sl][:, :, tt * 128:(tt + 1) * 128],
                                                         in0=TP.rearrange("p (c k) -> p c k", k=128),
                                                         in1=g2c[:, :].unsqueeze(2).broadcast_to([128, 8, 128]),
                                                         op=ALU.mult),
                          reads=[BTP], writes=[Bh2T[sl]])

            def emit_gu(c, f):
                sl = c % 2
                gb = f % 2
                for dc in range(8):
                    fw.op('pe', lambda: PE.matmul(GU[gb][:, 0:256], lhsT=wg[:, dc, f * 128:(f + 1) * 128],
                                                  rhs=h2T[sl][:, dc, :], start=(dc == 0), stop=(dc == 7)),
                          reads=[Bh2T[sl], Bwgb[f // 4]], writes=[BGU[gb]], mark=False)
                for dc in range(8):
                    fw.op('pe', lambda: PE.matmul(GU[gb][:, 256:512], lhsT=wu[:, dc, f * 128:(f + 1) * 128],
                                                  rhs=h2T[sl][:, dc, :], start=(dc == 0), stop=(dc == 7),
                                                  skip_group_check=True),
                          reads=[Bh2T[sl], Bwub[f // 4]], writes=[BGU[gb]], mark=(dc == 7))

            def emit_act(c, f):
                gb = f % 2
                fw.op('act', lambda: A.activation(out=sg[gb][:, :], in_=GU[gb][:, 0:256], func=AF.Silu),
                      reads=[BGU[gb]], writes=[Bsg[gb]])
                fw.op('dve', lambda: V.tensor_tensor(out=aT[f % 4][:, :], in0=sg[gb][:, :], in1=GU[gb][:, 256:512],
                                                     op=ALU.mult),
                      reads=[Bsg[gb], BGU[gb]], writes=[BaT[f % 4]])

            def emit_down(c, f):
                for tt in range(2):
                    for hf in range(2):
                        bi = tt * 2 + hf
                        fw.op('pe', lambda: PE.matmul(YA[bi][:, :], lhsT=aT[f % 4][:, tt * 128:(tt + 1) * 128],
                                                      rhs=wd[:, f, hf * 512:(hf + 1) * 512],
                                                      start=(f == 0), stop=(f == NF - 1)),
                              reads=[BaT[f % 4], Bwdb[f // 4]], writes=[BYA[bi]], mark=(bi == 3))

            def fin3(c):
                sl = c % 2
                for tt in range(2):
                    for hf in range(2):
                        bi = tt * 2 + hf
                        xv = x1[sl][:, tt, hf * 512:(hf + 1) * 512]
                        fw.op('dve', lambda: V.tensor_tensor(out=xv, in0=xv, in1=YA[bi][:, :], op=ALU.add),
                              reads=[BYA[bi]], writes=[Bx1[sl]])
                fw.dma('sp', sout[sl], out[c * 256:(c + 1) * 256, :].rearrange("(tt p) d -> p tt d", p=128),
                       x1[sl][:], reads=[Bx1[sl]])

            load3(0)
            load_weights()
            pre1(0)
            pre2(0)
            for c in range(NCH):
                if c + 1 < NCH:
                    load3(c + 1)
                emit_gu(c, 0)
                for f in range(NF):
                    if f + 1 < NF:
                        emit_gu(c, f + 1)
                    emit_act(c, f)
                    emit_down(c, f)
                    if c + 1 < NCH and f == 4:
                        pre1(c + 1)
                    if c + 1 < NCH and f == 12:
                        pre2(c + 1)
                fin3(c)
            fw.barrier()
    return nc


def rope_table():
    half = 32
    inv_a = np.power(np.float32(10000.0), -np.arange(0, half, 2, dtype=np.float32) / np.float32(half)).astype(np.float32)
    s = np.arange(S)
    r = (s // 64).astype(np.float32)
    c = (s % 64).astype(np.float32)
    ang_a = np.concatenate([r[:, None] * inv_a[None, :], c[:, None] * inv_a[None, :]], axis=-1).astype(np.float32)
    ang_a = np.concatenate([ang_a, np.zeros((16, 32), np.float32)], axis=0)
    inv_b = np.power(np.float32(10000.0), -np.arange(0, 64, 2, dtype=np.float32) / np.float32(64)).astype(np.float32)
    pos = np.concatenate([np.arange(16, S + 16), np.arange(0, 16)]).astype(np.float32)
    ang_b = (pos[:, None] * inv_b[None, :]).astype(np.float32)
    tab = np.concatenate([np.cos(ang_a), np.sin(ang_a), np.cos(ang_b), np.sin(ang_b)], axis=-1)
    return np.ascontiguousarray(tab.astype(np.float32))


_NC_CACHE = {}


def make_in_maps(inputs, cores):
    f = lambda a: np.ascontiguousarray(np.asarray(a, dtype=np.float32))
    shared = {k: f(v) for k, v in inputs.items() if k != "x"}
    shared["rope_tab"] = rope_table()
    shared["ident"] = np.eye(128, dtype=np.float32)
    xx = np.asarray(inputs["x"], dtype=np.float32)
    maps = []
    for b in cores:
        m = dict(shared)
        m["x"] = np.ascontiguousarray(xx[b])
        maps.append(m)
    return maps


def kernel(**inputs):
    if "nc" not in _NC_CACHE:
        _NC_CACHE["nc"] = build(False)
    nc = _NC_CACHE["nc"]
    in_maps = make_in_maps(inputs, list(range(N_CORES)))
    res = run_bass_kernel_spmd(nc, in_maps, core_ids=list(range(N_CORES)))
    return np.stack([np.asarray(r["out"], dtype=np.float32) for r in res.results], axis=0)
```
